# Optimizing a Trainium2 kernel written in Bass

```python
import math
import jax, jax.numpy as jnp
from jax import lax
import numpy as np

D_MODEL = 2048
BATCH = 2
SEQ = 4096
DEPTH = 1

HEAD_DIM = 64
N_Q_HEADS = D_MODEL // 128
N_KV_HEADS = 4
Q_PER_KV = N_Q_HEADS // N_KV_HEADS
ATTN_W = N_Q_HEADS * HEAD_DIM
KV_W = N_KV_HEADS * HEAD_DIM
WINDOW = 128
BLOCK = 128
SSM_W = D_MODEL // 2
GROUP = 16
N_GROUPS = SSM_W // GROUP
STATE = 64
IN_SIZES = (ATTN_W, KV_W, KV_W, ATTN_W, SSM_W, SSM_W, D_MODEL, D_MODEL)
IN_W = sum(IN_SIZES)
NORM_EPS = 1e-6

kernel_name = "hybrid_swa_sink_s5_gated_merge"


def rms_norm(x, w):
    xf = x.astype(jnp.float32)
    y = xf * lax.rsqrt(jnp.mean(xf * xf, axis=-1, keepdims=True) + NORM_EPS)
    return (y * w.astype(jnp.float32)).astype(x.dtype)


def sliding_window_attention(q, k, v, sinks):
    b, l = q.shape[0], q.shape[1]
    nb = l // BLOCK
    qb = q.reshape(b, nb, BLOCK, N_KV_HEADS, Q_PER_KV, HEAD_DIM)
    kb = k.reshape(b, nb, BLOCK, N_KV_HEADS, HEAD_DIM)
    vb = v.reshape(b, nb, BLOCK, N_KV_HEADS, HEAD_DIM)
    k_prev = jnp.concatenate([jnp.zeros_like(kb[:, :1]), kb[:, :-1]], axis=1)
    v_prev = jnp.concatenate([jnp.zeros_like(vb[:, :1]), vb[:, :-1]], axis=1)
    kk = jnp.concatenate([k_prev, kb], axis=2)
    vv = jnp.concatenate([v_prev, vb], axis=2)
    scale = 1.0 / math.sqrt(HEAD_DIM)
    scores = jnp.einsum('bnqgrd,bnsgd->bngrqs', qb, kk).astype(jnp.float32) * scale
    q_loc = jnp.arange(BLOCK)[:, None] + BLOCK
    k_loc = jnp.arange(2 * BLOCK)[None, :]
    diff = q_loc - k_loc
    k_abs = (jnp.arange(nb)[:, None, None] - 1) * BLOCK + k_loc[None]
    valid = (diff >= 0)[None] & (diff < WINDOW)[None] & (k_abs >= 0)
    scores = jnp.where(valid[None, :, None, None], scores, -1e30)
    sink = jnp.broadcast_to(
        sinks.astype(jnp.float32).reshape(1, 1, N_KV_HEADS, Q_PER_KV, 1, 1),
        scores.shape[:-1] + (1,))
    probs = jax.nn.softmax(jnp.concatenate([scores, sink], axis=-1), axis=-1)[..., :-1]
    out = jnp.einsum('bngrqs,bnsgd->bnqgrd', probs.astype(v.dtype), vv)
    return out.reshape(b, l, ATTN_W)


def s5_ssm(u, A_re, A_im, log_dt, B_re, B_im, C_re, C_im, D_skip):
    dt = jnp.exp(log_dt)[:, None]
    mag = jnp.exp(dt * A_re)
    ab_re = mag * jnp.cos(dt * A_im)
    ab_im = mag * jnp.sin(dt * A_im)
    num_re = ab_re - 1.0
    num_im = ab_im
    den = A_re * A_re + A_im * A_im
    cf_re = (num_re * A_re + num_im * A_im) / den
    cf_im = (num_im * A_re - num_re * A_im) / den
    bu_re = jnp.einsum('blgh,gph->blgp', u, B_re)
    bu_im = jnp.einsum('blgh,gph->blgp', u, B_im)
    b_re = cf_re * bu_re - cf_im * bu_im
    b_im = cf_re * bu_im + cf_im * bu_re
    a_re = jnp.broadcast_to(ab_re, b_re.shape)
    a_im = jnp.broadcast_to(ab_im, b_im.shape)

    def combine(e1, e2):
        a1r, a1i, b1r, b1i = e1
        a2r, a2i, b2r, b2i = e2
        return (a2r * a1r - a2i * a1i,
                a2r * a1i + a2i * a1r,
                a2r * b1r - a2i * b1i + b2r,
                a2r * b1i + a2i * b1r + b2i)

    _, _, s_re, s_im = lax.associative_scan(combine, (a_re, a_im, b_re, b_im), axis=1)
    y = (jnp.einsum('blgp,ghp->blgh', s_re, C_re)
         - jnp.einsum('blgp,ghp->blgh', s_im, C_im)
         + D_skip * u)
    return y


def setup_inputs(seed: int = 0) -> dict:
    key = jax.random.key(seed)
    ks = jax.random.split(key, 20)
    f32 = jnp.float32
    n = jnp.arange(STATE, dtype=f32)
    x = jax.random.normal(ks[0], (BATCH, SEQ, D_MODEL), f32)
    norm_w = 1.0 + 0.02 * jax.random.normal(ks[1], (D_MODEL,), f32)
    w_in = jax.random.normal(ks[2], (D_MODEL, IN_W), f32) * D_MODEL ** -0.5
    q_norm_w = 1.0 + 0.02 * jax.random.normal(ks[3], (HEAD_DIM,), f32)
    k_norm_w = 1.0 + 0.02 * jax.random.normal(ks[4], (HEAD_DIM,), f32)
    sinks = jax.random.normal(ks[5], (N_Q_HEADS,), f32)
    w_attn_proj = jax.random.normal(ks[6], (ATTN_W, D_MODEL), f32) * ATTN_W ** -0.5
    A_re = -0.5 + 0.01 * jax.random.normal(ks[7], (N_GROUPS, STATE), f32)
    A_im = math.pi * n[None, :] + 0.01 * jax.random.normal(ks[8], (N_GROUPS, STATE), f32)
    log_dt = jax.random.uniform(ks[9], (N_GROUPS,), f32, math.log(1e-3), math.log(1e-1))
    b_scale = (2.0 * GROUP) ** -0.5
    B_re = jax.random.normal(ks[10], (N_GROUPS, STATE, GROUP), f32) * b_scale
    B_im = jax.random.normal(ks[11], (N_GROUPS, STATE, GROUP), f32) * b_scale
    c_scale = (2.0 * STATE) ** -0.5
    C_re = jax.random.normal(ks[12], (N_GROUPS, GROUP, STATE), f32) * c_scale
    C_im = jax.random.normal(ks[13], (N_GROUPS, GROUP, STATE), f32) * c_scale
    D_skip = jax.random.normal(ks[14], (N_GROUPS, GROUP), f32)
    w_glu = jax.random.normal(ks[15], (SSM_W, 2 * SSM_W), f32) * SSM_W ** -0.5
    b_glu = 0.01 * jax.random.normal(ks[16], (2 * SSM_W,), f32)
    w_ssm_proj = jax.random.normal(ks[17], (SSM_W, D_MODEL), f32) * SSM_W ** -0.5
    w_out = jax.random.normal(ks[18], (D_MODEL, D_MODEL), f32) * D_MODEL ** -0.5
    return {"x": x, "norm_w": norm_w, "w_in": w_in, "q_norm_w": q_norm_w,
            "k_norm_w": k_norm_w, "sinks": sinks, "w_attn_proj": w_attn_proj,
            "A_re": A_re, "A_im": A_im, "log_dt": log_dt, "B_re": B_re, "B_im": B_im,
            "C_re": C_re, "C_im": C_im, "D_skip": D_skip, "w_glu": w_glu,
            "b_glu": b_glu, "w_ssm_proj": w_ssm_proj, "w_out": w_out}


def reference(x, norm_w, w_in, q_norm_w, k_norm_w, sinks, w_attn_proj, A_re, A_im,
              log_dt, B_re, B_im, C_re, C_im, D_skip, w_glu, b_glu, w_ssm_proj, w_out):
    b, l, _ = x.shape
    split_pts = list(np.cumsum(IN_SIZES)[:-1])
    f32 = jnp.float32
    for _layer in range(DEPTH):
        h = rms_norm(x, norm_w)
        proj = h @ w_in
        q, k, v, a_gate, u, z, g_a, g_s = jnp.split(proj, split_pts, axis=-1)
        q = rms_norm(q.reshape(b, l, N_Q_HEADS, HEAD_DIM), q_norm_w)
        k = rms_norm(k.reshape(b, l, N_KV_HEADS, HEAD_DIM), k_norm_w)
        v = v.reshape(b, l, N_KV_HEADS, HEAD_DIM)
        attn = sliding_window_attention(q, k, v, sinks)
        y_a = (attn * jax.nn.silu(a_gate)) @ w_attn_proj
        u_g = u.reshape(b, l, N_GROUPS, GROUP).astype(f32)
        y_ssm = s5_ssm(u_g, A_re.astype(f32), A_im.astype(f32), log_dt.astype(f32),
                       B_re.astype(f32), B_im.astype(f32), C_re.astype(f32),
                       C_im.astype(f32), D_skip.astype(f32))
        y_ssm = jax.nn.gelu(y_ssm.reshape(b, l, SSM_W)).astype(x.dtype)
        glu_a, glu_b = jnp.split(y_ssm @ w_glu + b_glu, 2, axis=-1)
        y_s = (glu_a * jax.nn.sigmoid(glu_b) * jax.nn.silu(z)) @ w_ssm_proj
        merged = jax.nn.sigmoid(g_a) * y_a + jax.nn.sigmoid(g_s) * y_s
        x = x + merged @ w_out
    return x
```

```python
import contextlib
import math
import numpy as np
import concourse.bass as bass
import concourse.mybir as mybir
from concourse.bass_utils import run_bass_kernel_spmd

F32 = mybir.dt.float32
BF16 = mybir.dt.bfloat16
ALU = mybir.AluOpType
AF = mybir.ActivationFunctionType

N_DMA_SEMS = 24
N_POOL_SEMS = 6
NCORES = 8
D = 2048
NT = 1024
NH = 1152
KT = 16
IN_W = 8704
EPS = 1e-6
SCAN_ENG = "pool"
TREE_SCAN = True
NKMAX = 4
PUMP_C = 0
PUMP_T = 0
CC_INC = 1
USE_CC = True
CAST_ENGS = ("act",)


class Prog:
    ENGS = ("pe", "act", "dve", "pool", "sp")

    def __init__(self, nc):
        self.nc = nc
        self.ops = {e: [] for e in self.ENGS}
        self.cnt = {e: 0 for e in self.ENGS}
        self.seen = {e: {} for e in self.ENGS}
        self.last_w = {}
        self.readers = {}
        self.dma_cnt = [0] * (N_DMA_SEMS + 1 + N_POOL_SEMS)
        self.pool_rr = 0
        self.dma_rr = 0

    def _deps(self, eng, reads, writes):
        deps = []
        for t in reads:
            w = self.last_w.get(t)
            if w is not None:
                deps.append(w)
        for t in writes:
            w = self.last_w.get(t)
            if w is not None:
                deps.append(w)
            deps.extend(self.readers.get(t, ()))
        need = {}
        for (s, v) in deps:
            if s == "pe" and eng == "pe":
                continue
            if self.seen[eng].get(s, 0) >= v:
                continue
            if need.get(s, 0) < v:
                need[s] = v
        for s, v in need.items():
            self.seen[eng][s] = v
        return list(need.items())

    def _commit(self, ev, reads, writes):
        for t in reads:
            self.readers.setdefault(t, []).append(ev)
        for t in writes:
            self.last_w[t] = ev
            self.readers[t] = []

    def op(self, eng, fn, reads=(), writes=()):
        psr = [t for t in reads if isinstance(t, str) and t.startswith("ps")]
        if psr:
            reads = [t for t in reads if t not in psr]
            writes = list(writes) + psr
        waits = self._deps(eng, reads, writes)
        self.cnt[eng] += 1
        ev = (eng, self.cnt[eng])
        self.ops[eng].append(("op", waits, fn, None))
        self._commit(ev, reads, writes)
        return ev

    def dma(self, q, out, in_, reads=(), writes=(), **kw):
        if q == "pool":
            i = N_DMA_SEMS + 1 + self.pool_rr
            self.pool_rr = (self.pool_rr + 1) % N_POOL_SEMS
        else:
            i = self.dma_rr
            self.dma_rr = (self.dma_rr + 1) % N_DMA_SEMS
        sname = ("dma", i)
        waits = self._deps(q, reads, writes)
        prev = self.dma_cnt[i]
        if prev > 0 and self.seen[q].get(sname, 0) < prev:
            waits.append((sname, prev))
            self.seen[q][sname] = prev
        self.dma_cnt[i] += 16
        ev = (sname, self.dma_cnt[i])
        self.ops[q].append(("dma", waits, (out, in_, kw), i))
        self._commit(ev, reads, writes)
        return ev

    def custom(self, eng, fn, sem_i, reads=(), writes=()):
        sname = ("dma", sem_i)
        waits = self._deps(eng, reads, writes)
        self.dma_cnt[sem_i] += CC_INC
        ev = (sname, self.dma_cnt[sem_i])
        self.ops[eng].append(("custom", waits, fn, sem_i))
        self._commit(ev, reads, writes)
        return ev

    def alias(self, new_tokens, old_tokens):
        evs = []
        for t in old_tokens:
            w = self.last_w.get(t)
            if w is not None:
                evs.append(w)
            evs.extend(self.readers.get(t, ()))
        for t in new_tokens:
            self.last_w.pop(t, None)
            self.readers[t] = list(evs)

    def wait_all(self, eng, evs):
        waits = []
        for (s, v) in evs:
            if self.seen[eng].get(s, 0) < v:
                waits.append((s, v))
                self.seen[eng][s] = v
        self.ops[eng].append(("wait", waits, None, None))

    def emit(self, st):
        nc = self.nc
        sems = {}
        for e in self.ENGS:
            sems[e] = st.enter_context(nc.semaphore("s_" + e))
        for i in range(N_DMA_SEMS + 1 + N_POOL_SEMS):
            sems[("dma", i)] = st.enter_context(nc.semaphore("s_dma%d" % i))
        block = st.enter_context(nc.Block())

        def run(engname):
            def body(eng):
                for kind, waits, payload, di in self.ops[engname]:
                    for (s, v) in waits:
                        eng.wait_ge(sems[s], v)
                    if kind == "op":
                        payload(eng).then_inc(sems[engname], 1)
                    elif kind == "dma":
                        out, in_, kw = payload
                        eng.dma_start(out=out, in_=in_, **kw).then_inc(sems[("dma", di)], 16)
                    elif kind == "custom":
                        payload(eng).then_inc(sems[("dma", di)], CC_INC)
            return body

        block.tensor(run("pe"))
        block.scalar(run("act"))
        block.vector(run("dve"))
        block.gpsimd(run("pool"))
        block.sync(run("sp"))


def _q_head_order():
    order = []
    for c in range(8):
        if c < 4:
            order += [c, 4 + c]
        else:
            order += [8 + (c - 4), 12 + (c - 4)]
    return order


def _in_col_perm():
    qh = _q_head_order()
    attn_feat = np.concatenate([np.arange(h * 64, (h + 1) * 64) for h in qh])
    cols = [attn_feat, np.arange(1024, 1536), 1536 + attn_feat, np.arange(2560, IN_W)]
    return np.concatenate(cols), attn_feat


def _prep_shared(inp):
    f = np.float32
    perm, attn_feat = _in_col_perm()
    sh = {}
    sh["w_in"] = np.ascontiguousarray(inp["w_in"][:, perm])
    sh["w_ap"] = np.ascontiguousarray(inp["w_attn_proj"][attn_feat, :])
    sh["w_glu"] = np.ascontiguousarray(inp["w_glu"])
    sh["w_sp"] = np.ascontiguousarray(inp["w_ssm_proj"])
    sh["w_out"] = np.ascontiguousarray(inp["w_out"])
    sh["normw_row"] = np.ascontiguousarray(np.broadcast_to(inp["norm_w"][None, :], (128, D))).astype(f)
    sh["qkw"] = np.stack([np.tile(inp["q_norm_w"], 2), np.tile(inp["k_norm_w"], 2)], axis=1).astype(f)
    sh["sinks_rep"] = np.ascontiguousarray(np.broadcast_to(inp["sinks"][None, :], (128, 16))).astype(f)
    sh["bglu_fm"] = np.ascontiguousarray(inp["b_glu"].reshape(16, 128).T).astype(f)
    sh["ident"] = np.eye(128, dtype=f)
    s_idx = np.arange(128)[:, None]
    i_idx = np.arange(128)[None, :]
    NEG = -30000.0
    sh["maskP"] = np.where(s_idx > i_idx, 0.0, NEG).astype(f)
    sh["maskC"] = np.where(s_idx <= i_idx, 0.0, NEG).astype(f)

    def mp(a):
        return np.ascontiguousarray(a.reshape(32, 2, 64).transpose(1, 2, 0).reshape(128, 32)).astype(f)

    sh["Are"] = mp(inp["A_re"])
    sh["Aim"] = mp(inp["A_im"])
    sh["ldt"] = mp(np.broadcast_to(inp["log_dt"][:, None], (64, 64)))

    def bt(b):
        o = np.zeros((2, 64, 32, 2, 16), f)
        b4 = b.reshape(32, 2, 64, 16)
        for gi in range(2):
            o[gi, :, :, gi, :] = b4[:, gi].transpose(1, 0, 2)
        return np.ascontiguousarray(o.reshape(128, 32, 32))

    def ct(c):
        o = np.zeros((2, 64, 32, 2, 16), f)
        c4 = c.reshape(32, 2, 16, 64)
        for gi in range(2):
            o[gi, :, :, gi, :] = c4[:, gi].transpose(2, 0, 1)
        return np.ascontiguousarray(o.reshape(128, 32, 32))

    sh["BTre"] = bt(inp["B_re"])
    sh["BTim"] = bt(inp["B_im"])
    sh["CTre"] = ct(inp["C_re"])
    sh["CTim"] = ct(inp["C_im"])
    sh["Dfm"] = np.ascontiguousarray(inp["D_skip"].reshape(8, 128).T).astype(f)
    return sh


def _prep_core(inp, c):
    f = np.float32
    b, j = c // 4, c % 4
    x = inp["x"]
    xc = np.zeros((NH, D), f)
    t0 = j * NT
    if j > 0:
        xc[:] = x[b, t0 - 128:t0 + NT]
    else:
        xc[128:] = x[b, 0:NT]
    s_idx = np.arange(128)[:, None]
    i_idx = np.arange(128)[None, :]
    maskP0 = np.where(s_idx > i_idx, 0.0, -30000.0).astype(f) if j > 0 else np.full((128, 128), -30000.0, f)
    sel = np.zeros((3, 4), f)
    for d in range(1, 4):
        if j - d >= 0:
            sel[d - 1, j - d] = 1.0
    selr = np.ascontiguousarray(np.broadcast_to(sel.reshape(1, 12), (128, 12))).astype(f)
    return {"x": xc, "maskP0": maskP0, "sel": selr}


def build(debug=(), stop=None):
    wseq = _build(debug, stop, None)[1]
    return _build(debug, stop, wseq)[0]


def _build(debug, stop, wseq):
    wseq_rec = []
    nc = bass.Bass("TRN2", target_bir_lowering=False)

    def din(name, shape):
        return nc.dram_tensor(name, list(shape), F32, kind="ExternalInput").ap()

    x_d = din("x", [NH, D])
    w_in_d = din("w_in", [D, IN_W])
    w_ap_d = din("w_ap", [1024, D])
    w_glu_d = din("w_glu", [1024, D])
    w_sp_d = din("w_sp", [1024, D])
    w_out_d = din("w_out", [D, D])
    normw_d = din("normw_row", [128, D])
    qkw_d = din("qkw", [128, 2])
    sinks_d = din("sinks_rep", [128, 16])
    bglu_d = din("bglu_fm", [128, 16])
    ident_d = din("ident", [128, 128])
    maskP_d = din("maskP", [128, 128])
    maskC_d = din("maskC", [128, 128])
    maskP0_d = din("maskP0", [128, 128])
    Are_d = din("Are", [128, 32])
    Aim_d = din("Aim", [128, 32])
    ldt_d = din("ldt", [128, 32])
    BTre_d = din("BTre", [128, 32, 32])
    BTim_d = din("BTim", [128, 32, 32])
    CTre_d = din("CTre", [128, 32, 32])
    CTim_d = din("CTim", [128, 32, 32])
    Dfm_d = din("Dfm", [128, 8])
    sel_d = din("sel", [128, 12])
    out_d = nc.dram_tensor("out", [NT, D], F32, kind="ExternalOutput").ap()
    cc_in = nc.dram_tensor("cc_in", [128, 64], F32, kind="Internal").ap()
    cc_out = nc.dram_tensor("cc_out", [4 * 128, 64], F32, kind="Internal").ap()
    dbg_out = {}

    P = Prog(nc)
    st = contextlib.ExitStack()
    with st:
        def sb(name, shape, dt=F32):
            return st.enter_context(nc.sbuf_tensor(name, list(shape), dt))

        ps = [st.enter_context(nc.psum_tensor("ps%d" % i, [128, 512], F32)) for i in range(8)]
        PS = ["ps%d" % i for i in range(8)]

        def dbg(name, ap, shape, toks):
            if name not in debug:
                return
            d = nc.dram_tensor("dbg_" + name, list(shape), ap.dtype, kind="ExternalOutput").ap()
            dbg_out[name] = P.dma("sp", d, ap, reads=toks, writes=["dbg_" + name])

        hT = sb("hT", [128, KT, NH], BF16)
        uT = sb("uT", [128, 8, NT], BF16)
        Sloc = sb("Sloc", [128, 128, 64], F32)
        R1 = sb("R1", [128, 16384], BF16)
        R2 = sb("R2", [128, 12288], BF16)
        KBD = sb("KBD", [128, 8, 8, 128], BF16)
        wstf = [sb("wst%d" % i, [128, 2048], F32) for i in range(2)]
        wbf = [sb("wbf%d" % i, [128, KT, 128], BF16) for i in range(2)]
        tmpA = sb("tmpA", [128, 2048], F32)
        tmpB = sb("tmpB", [128, 512], F32)
        consts = sb("consts", [128, 128], F32)
        ident_bf = sb("ident_bf", [128, 128], BF16)
        ident_f = sb("ident_f", [128, 128], F32)
        masks = sb("masks", [128, 3, 128], BF16)
        sq_bf = sb("sq_bf", [128, 512], BF16)
        ss = sb("ssm_small", [128, 40, 32], F32)
        pw_re = sb("pw_re", [128, 9, 32], F32)
        pw_im = sb("pw_im", [128, 9, 32], F32)
        wk_re = sb("wk_re", [128, 8, 32], F32)
        wk_im = sb("wk_im", [128, 8, 32], F32)
        scan_t = sb("scan_t", [128, 1 if TREE_SCAN else 5, 64], F32)
        tree_t = sb("tree_t", [128, 3, NKMAX * 64], F32) if TREE_SCAN else None
        Dfm = sb("Dfm_sb", [128, 8], F32)
        sel_sb = sb("sel_sb", [128, 12], F32)
        Gall = sb("Gall", [128, 4, 64], F32)
        zeros_bf = sb("zeros_bf", [128, 128], BF16)
        ones_blk = sb("ones_blk", [128, 128], BF16)
        eskb = sb("eskb", [128, 16], BF16)
        dsel = sb("dsel", [1, 2, 128], BF16)
        wst = [w[:].rearrange("p (a b) -> p a b", a=KT) for w in wstf]
        wst4 = [w[:].rearrange("p (a b) -> p a b", a=4) for w in wstf]

        qkw = consts[:, 0:2]
        qw8 = consts[:, 2:3]
        eps_c = consts[:, 3:4]
        esk = consts[:, 16:32]
        bglu = consts[:, 32:48]
        rstd_blk = consts[:, 48:64]
        ssq_blk = consts[:, 64:80]

        def view(ar, off, shape, dt=BF16):
            n = int(np.prod(shape))
            if dt == BF16:
                a = ar[:, off:off + n]
            else:
                a = ar[:, off:off + 2 * n].bitcast(F32)
            if len(shape) == 1:
                return a
            names = " ".join("d%d" % i for i in range(len(shape)))
            kw = {"d%d" % i: shape[i] for i in range(len(shape))}
            return a.rearrange("p (%s) -> p %s" % (names, names), **kw)

        xst = view(R1, 0, [2048], F32)
        normw = view(R1, 4096, [2048], F32)
        xn_bf = view(R1, 8192, [2048])
        Vre = view(R1, 0, [8, 16, 32])
        Vim = view(R1, 4096, [8, 16, 32])
        GinT = view(R1, 8192, [4, 8, 2, 128])
        sgate = view(R1, 0, [8, NT])
        vaug = view(R1, 8192, [9, 4, 128])
        kT = view(R1, 12800, [2, NH])
        Pa = view(R1, 0, [16, NT])
        BT_f = view(R2, 0, [2, 16, 32], F32)
        CT_f = view(R2, 2048, [2, 32, 32], F32)
        CTre_bf = view(R2, 6144, [32, 32])
        nCTim_bf = view(R2, 7168, [32, 32])
        qT = view(R2, 0, [8, NT])
        pT = view(R2, 8192, [2, 2, 512])
        Sin_bf = view(R2, 0, [2, 32, 128])
        Hout = view(R2, 8192, [2, 4, 8, 2, 32])
        Sloc_bf = Sloc[:].rearrange("p c f -> p (c f)").bitcast(BF16)
        sz = view(Sloc_bf, 0, [8, NT])
        Sloc_f = Sloc[:].rearrange("p c f -> p (c f)")
        xsts = [Sloc_f[:, 0:2048], Sloc_f[:, 2048:4096]]
        normw = Sloc_f[:, 4096:6144]
        xns = [Sloc_bf[:, 12288:14336], Sloc_bf[:, 14336:16384]]
        yssm = view(Sloc_bf, 8192, [8, NT])
        vglu = KBD[:].rearrange("p a b c -> p (a b c)").rearrange("p (a b) -> p a b", a=8)
        hT_flat = hT[:].rearrange("p a b -> p (a b)")
        woutbf = view(hT_flat, 0, [2, KT, 512])
        uT_flat = uT[:].rearrange("p a b -> p (a b)")
        xres = view(uT_flat, 0, [2, 512], F32)
        ores = view(uT_flat, 2048, [2, 512], F32)

        out_evs = []

        def body():
            P.dma("sp", consts[:, 0:2], qkw_d, writes=["qkw"])
            P.dma("sp", bglu, bglu_d, writes=["bglu"])
            P.dma("sp", ident_f[:], ident_d, writes=["ident_f"])
            mf = tmpA[:, 0:384].rearrange("p (a b) -> p a b", a=3)
            P.dma("sp", mf[:, 0, :], maskP_d, writes=["tA0"])
            P.dma("sp", mf[:, 1, :], maskC_d, writes=["tA0"])
            P.dma("sp", mf[:, 2, :], maskP0_d, writes=["tA0"])
            P.dma("sp", tmpB[:, 0:16], sinks_d, writes=["tmpB"])
            P.dma("sp", Dfm[:], Dfm_d, writes=["Dfm"])
            P.dma("sp", sel_sb[:], sel_d, writes=["sel"])
            P.dma("sp", normw, normw_d, writes=["normw"])
            P.op("dve", lambda e: e.tensor_copy(out=ident_bf[:], in_=ident_f[:]), reads=["ident_f"], writes=["ident_bf"])
            P.op("dve", lambda e: e.memset(zeros_bf[:], 0.0), writes=["zeros_bf"])
            P.op("dve", lambda e: e.memset(eps_c, EPS), writes=["eps"])
            P.op("dve", lambda e: e.tensor_copy(out=masks[:], in_=mf), reads=["tA0"], writes=["masks"])
            P.op("act", lambda e: e.activation(out=esk, in_=tmpB[:, 0:16], func=AF.Exp), reads=["tmpB"], writes=["esk"])
            P.op("dve", lambda e: e.tensor_scalar(out=qw8, in0=consts[:, 0:1], scalar1=0.125, scalar2=None, op0=ALU.mult),
                 reads=["qkw"], writes=["qw8"])
            P.op("dve", lambda e: e.tensor_copy(out=eskb[:], in_=esk), reads=["esk"], writes=["eskb"])
            P.op("dve", lambda e: e.memset(dsel[0:1, 0, 0:64], 0.0), writes=["dsel"])
            P.op("dve", lambda e: e.memset(dsel[0:1, 0, 64:128], 1.0), writes=["dsel"])
            P.op("dve", lambda e: e.memset(dsel[0:1, 1, 0:64], 1.0), writes=["dsel"])
            P.op("dve", lambda e: e.memset(dsel[0:1, 1, 64:128], 0.0), writes=["dsel"])
            P.op("dve", lambda e: e.memset(ones_blk[:], 0.0), writes=["ones_blk"])
            P.op("dve", lambda e: e.memset(ones_blk[0:64, 0:64], 1.0), writes=["ones_blk"])
            P.op("dve", lambda e: e.memset(ones_blk[64:128, 64:128], 1.0), writes=["ones_blk"])

            bank_rr = {"i": 0}

            def next_banks(n):
                i = bank_rr["i"]
                bank_rr["i"] = (i + n) % 8
                return [(i + k) % 8 for k in range(n)]

            state = {"w": 0}

            def _issue_w(i, src):
                sl = i % 2
                ktn = src.shape[1]
                P.dma("sp", wst[sl][:, 0:ktn, :], src, writes=[("wst", sl)])
                P.op("act", lambda e: e.activation(out=wbf[sl][:, 0:ktn, :], in_=wst[sl][:, 0:ktn, :], func=AF.Copy),
                     reads=[("wst", sl)], writes=[("wbf", sl)])

            def load_w(src):
                i = state["w"]
                state["w"] += 1
                wseq_rec.append(src)
                if wseq is None:
                    _issue_w(i, src)
                else:
                    if i == 0:
                        _issue_w(0, wseq[0])
                    if i + 1 < len(wseq):
                        _issue_w(i + 1, wseq[i + 1])
                return wbf[i % 2], ("wbf", i % 2)

            w_in_v = w_in_d.rearrange("(kt p) c -> p kt c", p=128)

            def win_chunk(ci):
                return w_in_v[:, :, ci * 128:(ci + 1) * 128]

            CQ, CK, CV, CG, CU, CZ, CGA, CGS = 0, 8, 10, 12, 20, 28, 36, 52
            MAIN = [(128, 512), (640, 512)]

            def proj_fm(ci, banks, spans):
                wt, wtok = load_w(win_chunk(ci))
                for kt in range(KT):
                    for b, (t0, n) in zip(banks, spans):
                        P.op("pe", lambda e, b=b, kt=kt, t0=t0, n=n: e.matmul(
                            ps[b][:, 0:n], lhsT=wt[:, kt, :], rhs=hT[:, kt, t0:t0 + n],
                            start=(kt == 0), stop=(kt == KT - 1)), reads=[wtok, "hT"], writes=[PS[b]])

            SE = "pool"
            (I_ARE, I_AIM, I_LDT, I_DT, I_LRE, I_LIM, I_MAG, I_COS, I_SIN, I_AR, I_AI, I_DEN, I_NR, I_T1, I_T2, I_CFR,
             I_CFI, I_T3, I_T4, I_A8I) = range(20)
            P.dma("sp", ss[:, I_ARE, :], Are_d, writes=["ss"])
            P.dma("sp", ss[:, I_AIM, :], Aim_d, writes=["ss"])
            P.dma("sp", ss[:, I_LDT, :], ldt_d, writes=["ss"])
            P.dma("sp", CT_f[:, 0], CTre_d, writes=["CT_f"])
            P.dma("sp", CT_f[:, 1], CTim_d, writes=["CT_f"])

            def tt(o, a, b, op, eng=SE):
                P.op(eng, lambda e: e.tensor_tensor(out=o, in0=a, in1=b, op=op), reads=["ss"], writes=["ss"])

            def tsc(o, a, s1, op0, s2=None, op1=None, eng=SE):
                if op1 is None:
                    P.op(eng, lambda e: e.tensor_scalar(out=o, in0=a, scalar1=s1, scalar2=None, op0=op0),
                         reads=["ss"], writes=["ss"])
                else:
                    P.op(eng, lambda e: e.tensor_scalar(out=o, in0=a, scalar1=s1, scalar2=s2, op0=op0, op1=op1),
                         reads=["ss"], writes=["ss"])

            def act(o, a, func, **kw):
                P.op("act", lambda e: e.activation(out=o, in_=a, func=func, **kw), reads=["ss"], writes=["ss"])

            S_ = lambda i: ss[:, i, :]
            act(S_(I_DT), S_(I_LDT), AF.Exp)
            tt(S_(I_LRE), S_(I_DT), S_(I_ARE), ALU.mult)
            tt(S_(I_LIM), S_(I_DT), S_(I_AIM), ALU.mult)
            act(S_(I_MAG), S_(I_LRE), AF.Exp)
            hpi_c = consts[:, 4:5]
            P.op(SE, lambda e: e.memset(hpi_c, 0.5 * math.pi), writes=["hpi"])
            act(S_(I_SIN), S_(I_LIM), AF.Sin, scale=1.0 / 16)
            P.op("act", lambda e: e.activation(out=S_(I_COS), in_=S_(I_LIM), func=AF.Sin, scale=1.0 / 16, bias=hpi_c),
                 reads=["ss", "hpi"], writes=["ss"])
            for _ in range(4):
                tt(S_(I_T1), S_(I_COS), S_(I_COS), ALU.mult)
                tt(S_(I_T2), S_(I_SIN), S_(I_SIN), ALU.mult)
                tt(S_(I_T3), S_(I_COS), S_(I_SIN), ALU.mult)
                tt(S_(I_COS), S_(I_T1), S_(I_T2), ALU.subtract)
                tsc(S_(I_SIN), S_(I_T3), 2.0, ALU.mult)
            tt(S_(I_AR), S_(I_MAG), S_(I_COS), ALU.mult)
            tt(S_(I_AI), S_(I_MAG), S_(I_SIN), ALU.mult)
            tt(S_(I_T1), S_(I_ARE), S_(I_ARE), ALU.mult)
            tt(S_(I_T2), S_(I_AIM), S_(I_AIM), ALU.mult)
            tt(S_(I_DEN), S_(I_T1), S_(I_T2), ALU.add)
            P.op("dve", lambda e: e.reciprocal(out=S_(I_DEN), in_=S_(I_DEN)), reads=["ss"], writes=["ss"])
            tsc(S_(I_NR), S_(I_AR), -1.0, ALU.add)
            tt(S_(I_T1), S_(I_NR), S_(I_ARE), ALU.mult)
            tt(S_(I_T2), S_(I_AI), S_(I_AIM), ALU.mult)
            tt(S_(I_T1), S_(I_T1), S_(I_T2), ALU.add)
            tt(S_(I_CFR), S_(I_T1), S_(I_DEN), ALU.mult)
            tt(S_(I_T1), S_(I_AI), S_(I_ARE), ALU.mult)
            tt(S_(I_T2), S_(I_NR), S_(I_AIM), ALU.mult)
            tt(S_(I_T1), S_(I_T1), S_(I_T2), ALU.subtract)
            tt(S_(I_CFI), S_(I_T1), S_(I_DEN), ALU.mult)
            P.op(SE, lambda e: e.memset(pw_re[:, 0, :], 1.0), reads=["ss"], writes=["ss"])
            P.op(SE, lambda e: e.memset(pw_im[:, 0, :], 0.0), reads=["ss"], writes=["ss"])
            P.op(SE, lambda e: e.tensor_copy(out=pw_re[:, 1, :], in_=S_(I_AR)), reads=["ss"], writes=["ss"])
            P.op(SE, lambda e: e.tensor_copy(out=pw_im[:, 1, :], in_=S_(I_AI)), reads=["ss"], writes=["ss"])

            def cmul(o_re, o_im, a_re, a_im, b_re, b_im, eng=SE, tok="ss", t0=I_T1):
                t = [ss[:, t0 + i, :] for i in range(4)] if t0 != I_T1 else [S_(I_T1), S_(I_T2), S_(I_T3), S_(I_T4)]

                def o(out, a, b, op):
                    P.op(eng, lambda e: e.tensor_tensor(out=out, in0=a, in1=b, op=op), reads=[tok], writes=[tok])
                o(t[0], a_re, b_re, ALU.mult)
                o(t[1], a_im, b_im, ALU.mult)
                o(t[2], a_re, b_im, ALU.mult)
                o(t[3], a_im, b_re, ALU.mult)
                o(o_re, t[0], t[1], ALU.subtract)
                o(o_im, t[2], t[3], ALU.add)

            for k in range(2, 9):
                cmul(pw_re[:, k, :], pw_im[:, k, :], pw_re[:, k - 1, :], pw_im[:, k - 1, :], S_(I_AR), S_(I_AI))
            for k in range(8):
                cmul(wk_re[:, k, :], wk_im[:, k, :], pw_re[:, k, :], pw_im[:, k, :], S_(I_CFR), S_(I_CFI))
            P.op(SE, lambda e: e.tensor_copy(out=CTre_bf, in_=CT_f[:, 0]), reads=["CT_f"], writes=["CTbf"])
            P.op(SE, lambda e: e.tensor_scalar(out=nCTim_bf, in0=CT_f[:, 1], scalar1=-1.0, scalar2=None, op0=ALU.mult),
                 reads=["CT_f"], writes=["CTbf"])
            CA = ss[:, 24:26, :].rearrange("p a b -> p (a b)")
            a8i = ss[:, I_A8I, :]
            S0 = ss[:, 26:28, :].rearrange("p a b -> p (a b)")
            Fst = ss[:, 28:30, :].rearrange("p a b -> p (a b)")
            Z0 = ss[:, 30:32, :].rearrange("p a b -> p (a b)")
            E = SCAN_ENG
            P.op(E, lambda e: e.tensor_copy(out=ss[:, 24, :], in_=pw_re[:, 8, :]), reads=["ss"], writes=["sc"])
            P.op(E, lambda e: e.tensor_copy(out=ss[:, 25, :], in_=pw_re[:, 8, :]), reads=["ss"], writes=["sc"])
            P.op(E, lambda e: e.tensor_copy(out=a8i, in_=pw_im[:, 8, :]), reads=["ss"], writes=["sc"])
            P.op(E, lambda e: e.memset(Z0, 0.0), writes=["sc"])

            dbg("pw", pw_re[:], [128, 9, 32], ["ss"])
            dbg("wk", wk_re[:], [128, 8, 32], ["ss"])
            vt = [tmpA[:, i * 512:(i + 1) * 512].rearrange("p (a b) -> p a b", a=16) for i in range(4)]
            uTv = uT[:].rearrange("p j (c t) -> p j t c", t=8)
            Sloc_v2 = Sloc[:].rearrange("p c (r j q) -> p q j r c", r=2, q=4)
            def gen_V(half):
                P0 = half * 16
                P.dma("sp", BT_f[:, 0], BTre_d[:, P0:P0 + 16, :], writes=["BT_f"])
                P.dma("sp", BT_f[:, 1], BTim_d[:, P0:P0 + 16, :], writes=["BT_f"])
                for k in range(8):
                    wr = wk_re[:, k, P0:P0 + 16].unsqueeze(2).to_broadcast([128, 16, 32])
                    wi = wk_im[:, k, P0:P0 + 16].unsqueeze(2).to_broadcast([128, 16, 32])
                    VE = "dve"
                    sfx = VE
                    P.op(VE, lambda e, wr=wr: e.tensor_tensor(out=vt[0], in0=BT_f[:, 0], in1=wr, op=ALU.mult),
                         reads=["BT_f", "ss", "tA0"], writes=["tA0"])
                    P.op(VE, lambda e, wi=wi: e.tensor_tensor(out=vt[1], in0=BT_f[:, 1], in1=wi, op=ALU.mult),
                         reads=["BT_f", "ss", "tA0"], writes=["tA1"])
                    P.op(VE, lambda e, k=k: e.tensor_tensor(out=Vre[:, k], in0=vt[0], in1=vt[1], op=ALU.subtract),
                         reads=["tA0", "tA1"], writes=[("V", k)])
                    P.op(VE, lambda e, wr=wr: e.tensor_tensor(out=vt[2], in0=BT_f[:, 1], in1=wr, op=ALU.mult),
                         reads=["BT_f", "ss"], writes=["tA2"])
                    P.op(VE, lambda e, wi=wi: e.tensor_tensor(out=vt[3], in0=BT_f[:, 0], in1=wi, op=ALU.mult),
                         reads=["BT_f", "ss"], writes=["tA3"])
                    P.op(VE, lambda e, k=k: e.tensor_tensor(out=Vim[:, k], in0=vt[2], in1=vt[3], op=ALU.add),
                         reads=["tA2", "tA3"], writes=[("V", k)])
                    yield
                dbg("V", Vre, [128, 8, 16, 32], [("V", k) for k in range(8)])
                if stop == "S2":
                    return

            gv0 = gen_V(0)

            def normA(blk):
                xst = xsts[blk % 2]
                xn_bf = xns[blk % 2]
                XT = ("xst", blk % 2)
                XN = ("xn", blk % 2)
                P.dma("sp", xst, x_d[blk * 128:(blk + 1) * 128, :], reads=["tA0"], writes=[XT])
                P.op("act", lambda e: e.activation(out=xn_bf, in_=xst, func=AF.Square, accum_out=ssq_blk[:, blk:blk + 1]),
                     reads=[XT], writes=[XN, ("ssq", blk)])
                P.op("act", lambda e: e.activation(out=rstd_blk[:, blk:blk + 1], in_=ssq_blk[:, blk:blk + 1],
                                                   func=AF.Sqrt, scale=1.0 / D, bias=eps_c),
                     reads=[("ssq", blk), "eps"], writes=[("rstd", blk)])
                P.op("dve", lambda e: e.reciprocal(out=rstd_blk[:, blk:blk + 1], in_=rstd_blk[:, blk:blk + 1]),
                     reads=[("rstd", blk)], writes=[("rstd", blk)])
                P.op("dve", lambda e: e.scalar_tensor_tensor(out=xn_bf, in0=xst, scalar=rstd_blk[:, blk:blk + 1],
                                                             in1=normw, op0=ALU.mult, op1=ALU.mult),
                     reads=[XT, ("rstd", blk), "normw"], writes=[XN])

            def transA(blk):
                xn_bf = xns[blk % 2]
                XN = ("xn", blk % 2)
                for g4 in range(4):
                    bank = next_banks(1)[0]
                    for i in range(4):
                        kt = g4 * 4 + i
                        P.op("pe", lambda e, bank=bank, i=i, kt=kt: e.matmul(
                            ps[bank][:, i * 128:(i + 1) * 128], lhsT=xn_bf[:, kt * 128:(kt + 1) * 128], rhs=ident_bf[:],
                            start=True, stop=True), reads=[XN, "ident_bf"], writes=[PS[bank]])
                    if g4 % 2 == 0:
                        P.op("act", lambda e, bank=bank, g4=g4: e.activation(
                            out=hT[:, g4 * 4:g4 * 4 + 4, blk * 128:(blk + 1) * 128],
                            in_=ps[bank][:].rearrange("p (a b) -> p a b", a=4), func=AF.Copy),
                            reads=[PS[bank]], writes=["hT"])
                    else:
                        P.op("dve", lambda e, bank=bank, g4=g4: e.tensor_copy(
                            out=hT[:, g4 * 4:g4 * 4 + 4, blk * 128:(blk + 1) * 128],
                            in_=ps[bank][:].rearrange("p (a b) -> p a b", a=4)),
                            reads=[PS[bank]], writes=["hT"])

            normA(0)
            for blk in range(9):
                if blk + 1 < 9:
                    normA(blk + 1)
                transA(blk)
            dbg("hT", hT[:, 0:2, :], [128, 2, NH], ["hT"])

            def gk(half):
                P0 = half * 16
                for jl in range(4):
                    for tpair in range(4):
                        bank = next_banks(1)[0]
                        for t2 in range(2):
                            tau = tpair * 2 + t2
                            for ri in range(2):
                                reg = t2 * 2 + ri
                                Vsrc = Vre if ri == 0 else Vim
                                for Q in range(4):
                                    P.op("pe", lambda e, bank=bank, reg=reg, Vsrc=Vsrc, tau=tau, jl=jl, Q=Q: e.matmul(
                                        ps[bank][32 * Q:32 * Q + 32, reg * 128:(reg + 1) * 128],
                                        lhsT=Vsrc[:, 7 - tau, jl * 4 + Q, :], rhs=ident_bf[:], start=True, stop=True,
                                        tile_position=(0, 32 * Q)),
                                        reads=[("V", 7 - tau), "ident_bf"], writes=[PS[bank]])
                        gout = GinT[:, jl, tpair * 2:tpair * 2 + 2, :, :].rearrange("p a b c -> p (a b c)")
                        if tpair % 2 == 0:
                            P.op("act", lambda e, bank=bank, gout=gout: e.activation(out=gout, in_=ps[bank][:], func=AF.Copy),
                                 reads=[PS[bank]], writes=["GinT"])
                        else:
                            P.op("dve", lambda e, bank=bank, gout=gout: e.tensor_copy(out=gout, in_=ps[bank][:]),
                                 reads=[PS[bank]], writes=["GinT"])
                dbg("GinT", GinT, [128, 4, 8, 2, 128], ["GinT"])
                for jl in range(4):
                    j = half * 4 + jl
                    for lh in range(2):
                        bank = next_banks(1)[0]
                        P.op("pe", lambda e, bank=bank: e.matmul(
                            ps[bank][:].rearrange("p (a b) -> p a b", a=4), lhsT=zeros_bf[:],
                            rhs=zeros_bf[:].unsqueeze(1).to_broadcast([128, 4, 128]), start=True, stop=False),
                            reads=["zeros_bf"], writes=[PS[bank]])
                        for l4 in range(4):
                            lag = lh * 4 + l4
                            for Q in range(4):
                                Pl = jl * 4 + Q
                                Pg = P0 + Pl
                                last = (l4 == 3 and Q == 3)
                                oap = (slice(32 * Q, 32 * Q + 32), slice(l4 * 128 + 32 * Q, l4 * 128 + 32 * Q + 32))
                                P.op("pe", lambda e, bank=bank, oap=oap, lag=lag, Q=Q, Pl=Pl, Pg=Pg: e.matmul(
                                    ps[bank][oap[0], oap[1]], lhsT=Vre[:, lag, Pl, :], rhs=CTre_bf[:, Pg, :],
                                    start=False, stop=False, tile_position=(0, 32 * Q)),
                                    reads=[("V", lag), "CTbf"], writes=[PS[bank]])
                                P.op("pe", lambda e, bank=bank, oap=oap, lag=lag, Q=Q, Pl=Pl, Pg=Pg, last=last: e.matmul(
                                    ps[bank][oap[0], oap[1]], lhsT=Vim[:, lag, Pl, :], rhs=nCTim_bf[:, Pg, :],
                                    start=False, stop=False, tile_position=(0, 32 * Q)),
                                    reads=[("V", lag), "CTbf"], writes=[PS[bank]])
                        P.op("pe", lambda e, bank=bank: e.matmul(
                            ps[bank][:].rearrange("p (a b) -> p a b", a=4), lhsT=zeros_bf[:],
                            rhs=zeros_bf[:].unsqueeze(1).to_broadcast([128, 4, 128]), start=False, stop=True),
                            reads=["zeros_bf"], writes=[PS[bank]])
                        P.op("dve", lambda e, bank=bank, j=j, lh=lh: e.tensor_copy(
                            out=KBD[:, j, lh * 4:lh * 4 + 4, :].rearrange("p a b -> p (a b)"), in_=ps[bank][:]),
                            reads=[PS[bank]], writes=["KBD"])
                        if lh == 0:
                            P.op("dve", lambda e, bank=bank, j=j: e.scalar_tensor_tensor(
                                out=KBD[:, j, 0, :], in0=ident_f[:], scalar=Dfm[:, j:j + 1], in1=ps[bank][:, 0:128],
                                op0=ALU.mult, op1=ALU.add), reads=[PS[bank], "ident_f", "Dfm", "KBD"], writes=["KBD"])
                dbg("KBDh", KBD[:], [128, 8, 8, 128], ["KBD"])

            def sl(half, hook=None):
                P0 = half * 16
                for jp in range(2):
                    for Q in range(4):
                        bank = next_banks(1)[0]
                        for jl2 in range(2):
                            jl = jp * 2 + jl2
                            j = half * 4 + jl
                            for ri in range(2):
                                reg = jl2 * 2 + ri
                                for tau in range(8):
                                    P.op("pe", lambda e, bank=bank, reg=reg, Q=Q, jl=jl, j=j, tau=tau, ri=ri: e.matmul(
                                        ps[bank][:, reg * 128:(reg + 1) * 128],
                                        lhsT=GinT[32 * Q:32 * Q + 32, jl, tau, ri, :],
                                        rhs=uTv[32 * Q:32 * Q + 32, j, tau, :],
                                        start=(tau == 0), stop=(tau == 7), tile_position=(32 * Q, 0)),
                                        reads=["GinT", ("uT", j)], writes=[PS[bank]])
                        j0 = half * 4 + jp * 2
                        sl_out = Sloc_v2[:, Q, j0:j0 + 2, :, :]
                        ps_in = ps[bank][:].rearrange("p (j r c) -> p j r c", j=2, r=2)
                        if Q % 2 == 0:
                            P.op("dve", lambda e, sl_out=sl_out, ps_in=ps_in: e.tensor_copy(out=sl_out, in_=ps_in),
                                 reads=[PS[bank]], writes=["Sloc"])
                        else:
                            P.op("act", lambda e, sl_out=sl_out, ps_in=ps_in: e.activation(out=sl_out, in_=ps_in, func=AF.Copy),
                                 reads=[PS[bank]], writes=["Sloc"])
                        if hook is not None:
                            hook()

            UT = [("uT", j) for j in range(8)]
            for j in range(8):
                banks = next_banks(2)
                proj_fm(CU + j, banks, MAIN)
                for hidx, b in enumerate(banks):
                    if hidx == 0:
                        P.op("act", lambda e, b=b, j=j, hidx=hidx: e.activation(
                            out=uT[:, j, hidx * 512:(hidx + 1) * 512], in_=ps[b][:], func=AF.Copy),
                            reads=[PS[b]], writes=[("uT", j)])
                    else:
                        P.op("dve", lambda e, b=b, j=j, hidx=hidx: e.tensor_copy(
                            out=uT[:, j, hidx * 512:(hidx + 1) * 512], in_=ps[b][:]),
                            reads=[PS[b]], writes=[("uT", j)])
                next(gv0, None)
            for _ in gv0:
                pass
            dbg("uT", uT[:], [128, 8, NT], UT)

            gk(0)
            gv1 = gen_V(1)
            P.alias(["Sloc"], [("xst", 0), ("xst", 1), "normw", ("xn", 0), ("xn", 1)])
            sl(0, hook=lambda: next(gv1, None))
            for _ in gv1:
                pass
            gk(1)
            sl(1)
            dbg("KBD", KBD[:], [128, 8, 8, 128], ["KBD"])
            dbg("Sloc", Sloc[:], [128, 128, 64], ["Sloc"])
            if stop == "S":
                return

            T1 = scan_t[:, 0, :]
            if not TREE_SCAN:
                T2 = scan_t[:, 1, :]
                Uu = scan_t[:, 2, :]
                pp = [scan_t[:, 3, :], scan_t[:, 4, :]]

            def scan_pass(init, store):
                prev = init
                for c in range(128):
                    if store:
                        new = Sloc[:, c, :]
                    else:
                        new = pp[c % 2] if c < 127 else Fst
                    Bc = Sloc[:, c, :]
                    P.op(E, lambda e, prev=prev: e.tensor_tensor(out=T1, in0=CA, in1=prev, op=ALU.mult),
                         reads=["scanS", "sc"], writes=["scanT1"])
                    P.op(E, lambda e, prev=prev: e.tensor_tensor(out=T2[:, 0:32], in0=a8i, in1=prev[:, 32:64], op=ALU.mult),
                         reads=["scanS", "sc"], writes=["scanT2"])
                    P.op(E, lambda e, prev=prev: e.tensor_tensor(out=T2[:, 32:64], in0=a8i, in1=prev[:, 0:32], op=ALU.mult),
                         reads=["scanS", "sc"], writes=["scanT2"])
                    P.op(E, lambda e, Bc=Bc: e.tensor_tensor(out=Uu, in0=T1, in1=Bc, op=ALU.add),
                         reads=["scanT1", "Sloc"], writes=["scanU"])
                    wr_ = ["scanS", "Sloc"] if store else ["scanS"]
                    P.op(E, lambda e, new=new: e.tensor_tensor(out=new[:, 0:32], in0=Uu[:, 0:32], in1=T2[:, 0:32],
                                                               op=ALU.subtract), reads=["scanU", "scanT2"], writes=wr_)
                    P.op(E, lambda e, new=new: e.tensor_tensor(out=new[:, 32:64], in0=Uu[:, 32:64], in1=T2[:, 32:64],
                                                               op=ALU.add), reads=["scanU", "scanT2"], writes=wr_)
                    prev = new
                    yield

            def carry():
                P.dma("pool", cc_in, Fst, reads=["scanS"], writes=["cc_in"])
                if USE_CC:
                    P.custom("pool", lambda e: e.collective_compute(
                        "AllGather", ALU.bypass, replica_groups=[[0, 1, 2, 3], [4, 5, 6, 7]], ins=[cc_in], outs=[cc_out]),
                        N_DMA_SEMS, reads=["cc_in"], writes=["cc_out"])
                else:
                    for r_ in range(4):
                        P.dma("pool", cc_out[r_ * 128:(r_ + 1) * 128, :], cc_in, reads=["cc_in"], writes=["cc_out"])
                P.dma("pool", Gall[:], cc_out.rearrange("(r p) f -> p r f", p=128), reads=["cc_out"], writes=["Gall"])
                AKr = ss[:, 32, :]
                AKi = ss[:, 33, :]
                P.op(E, lambda e: e.tensor_copy(out=AKr, in_=pw_re[:, 8, :]), reads=["ss"], writes=["sc"])
                P.op(E, lambda e: e.tensor_copy(out=AKi, in_=pw_im[:, 8, :]), reads=["ss"], writes=["sc"])
                for _ in range(7):
                    cmul(ss[:, 34, :], ss[:, 35, :], AKr, AKi, AKr, AKi, eng=E, tok="sc", t0=21)
                    P.op(E, lambda e: e.tensor_copy(out=AKr, in_=ss[:, 34, :]), reads=["sc"], writes=["sc"])
                    P.op(E, lambda e: e.tensor_copy(out=AKi, in_=ss[:, 35, :]), reads=["sc"], writes=["sc"])
                Acc = ss[:, 38:40, :].rearrange("p a b -> p (a b)")
                P.op(E, lambda e: e.memset(Acc, 0.0), reads=["sc"], writes=["sc"])
                for d in (3, 2, 1):
                    if d != 3:
                        cmul(ss[:, 34, :], ss[:, 35, :], ss[:, 38, :], ss[:, 39, :], AKr, AKi, eng=E, tok="sc", t0=21)
                        P.op(E, lambda e: e.tensor_copy(out=ss[:, 38, :], in_=ss[:, 34, :]), reads=["sc"], writes=["sc"])
                        P.op(E, lambda e: e.tensor_copy(out=ss[:, 39, :], in_=ss[:, 35, :]), reads=["sc"], writes=["sc"])
                    for r in range(4):
                        P.op(E, lambda e, r=r, d=d: e.tensor_scalar(
                            out=T1, in0=Gall[:, r, :], scalar1=sel_sb[:, (d - 1) * 4 + r:(d - 1) * 4 + r + 1], scalar2=None,
                            op0=ALU.mult), reads=["sc", "Gall", "sel", "scanT1", "scanU"], writes=["scanT1"])
                        P.op(E, lambda e: e.tensor_tensor(out=Acc, in0=Acc, in1=T1, op=ALU.add),
                             reads=["sc", "scanT1"], writes=["sc"])
                P.op(E, lambda e: e.tensor_copy(out=S0, in_=Acc), reads=["sc"], writes=["scanS", "S0"])

            ctab = ss[:, 0:21, :]

            def tree_coefs():
                cr, ci = ss[:, 36, :], ss[:, 37, :]
                P.op(E, lambda e: e.tensor_copy(out=cr, in_=pw_re[:, 8, :]), reads=["ss", "sc"], writes=["sc"])
                P.op(E, lambda e: e.tensor_copy(out=ci, in_=pw_im[:, 8, :]), reads=["ss", "sc"], writes=["sc"])
                for d in range(7):
                    for rr in (0, 1):
                        P.op(E, lambda e, d=d, rr=rr: e.tensor_copy(out=ss[:, 3 * d + rr, :], in_=cr), reads=["sc"], writes=["sc"])
                    P.op(E, lambda e, d=d: e.tensor_copy(out=ss[:, 3 * d + 2, :], in_=ci), reads=["sc"], writes=["sc"])
                    if d < 6:
                        cmul(ss[:, 34, :], ss[:, 35, :], cr, ci, cr, ci, eng=E, tok="sc", t0=21)
                        P.op(E, lambda e: e.tensor_copy(out=cr, in_=ss[:, 34, :]), reads=["sc"], writes=["sc"])
                        P.op(E, lambda e: e.tensor_copy(out=ci, in_=ss[:, 35, :]), reads=["sc"], writes=["sc"])

            def tree_level(d, down):
                span = 2 ** (d + 1)
                nk_tot = 128 // span
                v = Sloc[:].rearrange("p (k s) f -> p k s f", s=span)
                CAd = ss[:, 3 * d:3 * d + 2, :].rearrange("p a b -> p (a b)")
                aid = ss[:, 3 * d + 2, :]
                for k0 in range(0, nk_tot, NKMAX):
                    nk = min(NKMAX, nk_tot - k0)
                    L = v[:, k0:k0 + nk, span // 2 - 1, :]
                    R = v[:, k0:k0 + nk, span - 1, :]
                    tv = [tree_t[:, i, 0:nk * 64].rearrange("p (k f) -> p k f", f=64) for i in range(3)]
                    CAb = CAd.unsqueeze(1).to_broadcast([128, nk, 64])
                    aib = aid.unsqueeze(1).to_broadcast([128, nk, 32])
                    src = R if down else L
                    oth = L if down else R
                    P.op(E, lambda e, src=src, CAb=CAb, tv=tv: e.tensor_tensor(out=tv[0], in0=src, in1=CAb, op=ALU.mult),
                         reads=["Sloc", "sc"], writes=["tr0"])
                    P.op(E, lambda e, src=src, aib=aib, tv=tv: e.tensor_tensor(out=tv[1][:, :, 0:32], in0=src[:, :, 32:64], in1=aib,
                                                                             op=ALU.mult), reads=["Sloc", "sc"], writes=["tr1"])
                    P.op(E, lambda e, src=src, aib=aib, tv=tv: e.tensor_tensor(out=tv[1][:, :, 32:64], in0=src[:, :, 0:32], in1=aib,
                                                                             op=ALU.mult), reads=["Sloc", "sc"], writes=["tr1"])
                    P.op(E, lambda e, oth=oth, tv=tv: e.tensor_tensor(out=tv[2], in0=tv[0], in1=oth, op=ALU.add),
                         reads=["tr0", "Sloc"], writes=["tr2"])
                    if down:
                        P.op(E, lambda e, L=L, R=R: e.tensor_copy(out=L, in_=R), reads=["Sloc", "tr2"], writes=["Sloc"])
                    P.op(E, lambda e, R=R, tv=tv: e.tensor_tensor(out=R[:, :, 0:32], in0=tv[2][:, :, 0:32], in1=tv[1][:, :, 0:32],
                                                                  op=ALU.subtract), reads=["tr2", "tr1"], writes=["Sloc"])
                    P.op(E, lambda e, R=R, tv=tv: e.tensor_tensor(out=R[:, :, 32:64], in0=tv[2][:, :, 32:64], in1=tv[1][:, :, 32:64],
                                                                  op=ALU.add), reads=["tr2", "tr1"], writes=["Sloc", "scanS"])
                    yield

            def tree_scan():
                tree_coefs()
                yield
                for d in range(7):
                    yield from tree_level(d, False)
                P.op(E, lambda e: e.tensor_copy(out=Fst, in_=Sloc[:, 127, :]), reads=["Sloc"], writes=["scanS"])
                carry()
                yield
                P.op(E, lambda e: e.tensor_copy(out=Sloc[:, 127, :], in_=S0), reads=["S0", "scanS"], writes=["Sloc"])
                for d in range(6, -1, -1):
                    yield from tree_level(d, True)

            def scan_all():
                if TREE_SCAN:
                    yield from tree_scan()
                    return
                yield from scan_pass(Z0, False)
                carry()
                yield
                yield from scan_pass(S0, True)

            scan_gen = scan_all()
            scan_done = {"d": False}

            def pump(n):
                import os
                if scan_done["d"] or (os.environ.get("NO_PUMP") and n < 1000):
                    return
                for _ in range(n):
                    try:
                        next(scan_gen)
                    except StopIteration:
                        scan_done["d"] = True
                        return

            P.alias(["sgate", "vaug", "kT"], [("V", k) for k in range(8)] + ["GinT"])
            P.alias(["qT"] + [("pT", a, b) for a in range(2) for b in range(2)], ["BT_f", "CT_f", "CTbf"])
            qn_state = {"i": 0}
            sq2 = tmpB[:].bitcast(BF16)

            def qk_norm_group(items, wcol, wtok, dtok):
                ctx = []
                for (b, dst, ntok) in items:
                    i = qn_state["i"] % 2
                    qn_state["i"] += 1
                    qf = tmpA[:, i * 1024:i * 1024 + 512]
                    rs = tmpA[:, i * 1024 + 512:i * 1024 + 1024]
                    TQ, TR = "tA%d" % (2 * i), "tA%d" % (2 * i + 1)
                    sq = sq_bf[:, 0:ntok] if i == 0 else sq2[:, 0:ntok]
                    TS = "sq" if i == 0 else "tmpB"
                    ctx.append((b, dst, ntok, qf, rs, TQ, TR, sq, TS))
                nbs = []
                for (b, dst, ntok, qf, rs, TQ, TR, sq, TS) in ctx:
                    P.op("act", lambda e, sq=sq, b=b, ntok=ntok: e.activation(out=sq, in_=ps[b][:, 0:ntok], func=AF.Square),
                         reads=[PS[b]], writes=[TS])
                    P.op("dve", lambda e, qf=qf, b=b, ntok=ntok: e.tensor_copy(out=qf[:, 0:ntok], in_=ps[b][:, 0:ntok]),
                         reads=[PS[b]], writes=[TQ])
                for (b, dst, ntok, qf, rs, TQ, TR, sq, TS) in ctx:
                    nb = next_banks(1)[0]
                    nbs.append(nb)
                    P.op("pe", lambda e, nb=nb, sq=sq, ntok=ntok: e.matmul(ps[nb][:, 0:ntok], lhsT=ones_blk[:], rhs=sq,
                                                                          start=True, stop=True),
                         reads=[TS, "ones_blk"], writes=[PS[nb]])
                for (b, dst, ntok, qf, rs, TQ, TR, sq, TS), nb in zip(ctx, nbs):
                    P.op("act", lambda e, rs=rs, nb=nb, ntok=ntok: e.activation(
                        out=rs[:, 0:ntok], in_=ps[nb][:, 0:ntok], func=AF.Sqrt, scale=1.0 / 64, bias=eps_c),
                        reads=[PS[nb], "eps"], writes=[TR])
                for (b, dst, ntok, qf, rs, TQ, TR, sq, TS) in ctx:
                    P.op("dve", lambda e, rs=rs, ntok=ntok: e.reciprocal(out=rs[:, 0:ntok], in_=rs[:, 0:ntok]),
                         reads=[TR], writes=[TR])
                for (b, dst, ntok, qf, rs, TQ, TR, sq, TS) in ctx:
                    P.op("dve", lambda e, dst=dst, qf=qf, rs=rs, ntok=ntok: e.scalar_tensor_tensor(
                        out=dst, in0=qf[:, 0:ntok], scalar=wcol, in1=rs[:, 0:ntok], op0=ALU.mult, op1=ALU.mult),
                        reads=[TQ, TR, wtok], writes=[dtok])

            for c in range(8):
                banks = next_banks(2)
                proj_fm(CQ + c, banks, MAIN)
                qk_norm_group([(b, qT[:, c, hidx * 512:(hidx + 1) * 512], 512) for hidx, b in enumerate(banks)],
                              qw8, "qw8", "qT")
                pump(PUMP_C)
            KSP = [(0, 512), (512, 512), (1024, 128)]
            for c in range(2):
                banks = next_banks(3)
                proj_fm(CK + c, banks, KSP)
                its = [(b, kT[:, c, t0:t0 + n], n) for (t0, n), b in zip(KSP, banks)]
                qk_norm_group(its[0:2], qkw[:, 1:2], "qkw", "kT")
                qk_norm_group(its[2:3], qkw[:, 1:2], "qkw", "kT")
                pump(PUMP_C)
            for c in range(8):
                banks = next_banks(2)
                proj_fm(CG + c, banks, MAIN)
                for hidx, b in enumerate(banks):
                    P.op("act", lambda e, b=b, c=c, hidx=hidx: e.activation(
                        out=sgate[:, c, hidx * 512:(hidx + 1) * 512], in_=ps[b][:], func=AF.Silu),
                        reads=[PS[b]], writes=["sgate"])
                pump(PUMP_C)
            P.op("dve", lambda e: e.memset(vaug.rearrange("p a b c -> p (a b c)"), 1.0), writes=["vaug"])
            for c in range(2):
                wt, wtok = load_w(win_chunk(CV + c))
                for blk in range(9):
                    b = next_banks(1)[0]
                    for kt in range(KT):
                        P.op("pe", lambda e, b=b, kt=kt, blk=blk, wt=wt: e.matmul(
                            ps[b][:, 0:128], lhsT=hT[:, kt, blk * 128:(blk + 1) * 128], rhs=wt[:, kt, :],
                            start=(kt == 0), stop=(kt == KT - 1)), reads=[wtok, "hT"], writes=[PS[b]])
                    P.op("dve", lambda e, b=b, blk=blk, c=c: e.tensor_copy(
                        out=vaug[:, blk, 2 * c, 0:64], in_=ps[b][:, 0:64]), reads=[PS[b]], writes=["vaug"])
                    P.op("act", lambda e, b=b, blk=blk, c=c: e.activation(
                        out=vaug[:, blk, 2 * c + 1, 64:128], in_=ps[b][:, 64:128], func=AF.Copy),
                        reads=[PS[b]], writes=["vaug"])
                pump(PUMP_C)
            dbg("qT", qT, [128, 8, NT], ["qT"])
            dbg("kT", kT, [128, 2, NH], ["kT"])
            dbg("vaug", vaug, [128, 9, 4, 128], ["vaug"])
            if stop == "C":
                return

            rden = tmpA[:, 1024:1536]
            onum = tmpA[:, 1536:2048]
            AG = [("ag", n) for n in range(8)]
            def att_qk(n, g):
                hf = g % 2
                rows = slice(64 * hf, 64 * hf + 64)
                drows = slice(64 * (1 - hf), 64 * (1 - hf) + 64)
                cbase = (g // 2) * 4
                kc = g // 2
                bS = next_banks(2)
                pb = (n * 4 + g) % 2
                qrhs = qT[rows, cbase:cbase + 4, n * 128:(n + 1) * 128]
                for kk, bb in enumerate(bS):
                    kblk = n + kk
                    mi = (2 if n == 0 else 0) if kk == 0 else 1
                    P.op("pe", lambda e, bb=bb, kblk=kblk, qrhs=qrhs, rows=rows, kc=kc: e.matmul(
                        ps[bb][:].rearrange("p (a b) -> p a b", a=4), lhsT=kT[rows, kc, kblk * 128:(kblk + 1) * 128],
                        rhs=qrhs, start=True, stop=False), reads=["qT", "kT"], writes=[PS[bb]])
                    P.op("pe", lambda e, bb=bb, mi=mi: e.matmul(
                        ps[bb][:].rearrange("p (a b) -> p a b", a=4), lhsT=ident_bf[:],
                        rhs=masks[:, mi, :].unsqueeze(1).to_broadcast([128, 4, 128]), start=False, stop=True),
                        reads=["masks", "ident_bf"], writes=[PS[bb]])
                    P.op("act", lambda e, bb=bb, kk=kk, pb=pb: e.activation(out=pT[:, pb, kk, :], in_=ps[bb][:], func=AF.Exp),
                         reads=[PS[bb]], writes=[("pT", pb, kk)])
                return (hf, rows, drows, cbase, pb)

            def att_rest(n, g, ctx):
                hf, rows, drows, cbase, pb = ctx
                bo = next_banks(1)[0]
                for kk in range(2):
                    kblk = n + kk
                    P.op("pe", lambda e, kk=kk, kblk=kblk, bo=bo, g=g, pb=pb: e.matmul(
                        ps[bo][:], lhsT=vaug[:, kblk, g, :], rhs=pT[:, pb, kk, :], start=(kk == 0), stop=False),
                        reads=[("pT", pb, kk), "vaug"], writes=[PS[bo]])
                P.op("pe", lambda e, bo=bo, g=g, hf=hf: e.matmul(
                    ps[bo][:].rearrange("p (a b) -> p a b", a=4), lhsT=dsel[0:1, hf, :],
                    rhs=eskb[0:1, 4 * g:4 * g + 4].unsqueeze(2).to_broadcast([1, 4, 128]), start=False, stop=True),
                    reads=["eskb", "dsel"], writes=[PS[bo]])
                P.op("dve", lambda e, bo=bo, drows=drows: e.reciprocal(out=rden[drows, :], in_=ps[bo][drows, :]),
                     reads=[PS[bo]], writes=["tA2"])
                P.op("dve", lambda e, bo=bo, rows=rows, drows=drows: e.tensor_tensor(
                    out=onum[rows, :], in0=ps[bo][rows, :], in1=rden[drows, :], op=ALU.mult),
                    reads=[PS[bo], "tA2"], writes=["tA3"])
                P.op("dve", lambda e, rows=rows, cbase=cbase, n=n: e.tensor_tensor(
                    out=qT[rows, cbase:cbase + 4, n * 128:(n + 1) * 128],
                    in0=onum[rows, :].rearrange("p (a b) -> p a b", a=4),
                    in1=sgate[rows, cbase:cbase + 4, n * 128:(n + 1) * 128], op=ALU.mult),
                    reads=["tA3", "sgate"], writes=[("ag", n)])

            its = [(n, g) for n in range(8) for g in range(4)]
            ctxs = {0: att_qk(*its[0])}
            for i, (n, g) in enumerate(its):
                if i + 1 < len(its):
                    ctxs[i + 1] = att_qk(*its[i + 1])
                att_rest(n, g, ctxs.pop(i))
                pump(PUMP_T)
            agT = qT
            dbg("ag", qT, [128, 8, NT], AG)
            if stop == "T":
                return

            w_ap_v = w_ap_d.rearrange("(kt p) c -> p kt c", p=128)
            w_glu_v = w_glu_d.rearrange("(kt p) c -> p kt c", p=128)
            w_sp_v = w_sp_d.rearrange("(kt p) c -> p kt c", p=128)
            P.alias(["Pa"], ["sgate", "vaug", "kT"])

            def proj8(wsrc, act_T, act_toks, banks):
                wt, wtok = load_w(wsrc)
                for kt in range(8):
                    for hidx, b in enumerate(banks):
                        P.op("pe", lambda e, b=b, kt=kt, hidx=hidx: e.matmul(
                            ps[b][:], lhsT=wt[:, kt, :], rhs=act_T[:, kt, hidx * 512:(hidx + 1) * 512],
                            start=(kt == 0), stop=(kt == 7)), reads=[wtok] + act_toks, writes=[PS[b]])

            for c in range(16):
                by = next_banks(2)
                proj8(w_ap_v[:, :, c * 128:(c + 1) * 128], agT, AG, by)
                bg = next_banks(2)
                proj_fm(CGA + c, bg, MAIN)
                for hidx in range(2):
                    sg = tmpA[:, hidx * 512:(hidx + 1) * 512]
                    P.op("act", lambda e, hidx=hidx, sg=sg, bg=bg: e.activation(out=sg, in_=ps[bg[hidx]][:], func=AF.Sigmoid),
                         reads=[PS[bg[hidx]], "tA0", "tA1"], writes=["tA%d" % hidx])
                    P.op("dve", lambda e, hidx=hidx, sg=sg, c=c, by=by: e.tensor_tensor(
                        out=Pa[:, c, hidx * 512:(hidx + 1) * 512], in0=ps[by[hidx]][:], in1=sg, op=ALU.mult),
                        reads=[PS[by[hidx]], "tA%d" % hidx], writes=["Pa"])
                pump(6)
            pump(100000)
            dbg("Pa", Pa, [128, 16, NT], ["Pa"])
            if stop == "D":
                return

            P.alias(["Sin_bf", ("Hout", 0), ("Hout", 1)], ["qT"] + AG + [("pT", a, b) for a in range(2) for b in range(2)])
            if TREE_SCAN:
                P.op("dve", lambda e: e.tensor_copy(out=Sin_bf, in_=Sloc[:].rearrange("p c (r q) -> p r q c", r=2)),
                     reads=["Sloc", "scanS"], writes=["Sin_bf"])
            else:
                P.op("dve", lambda e: e.tensor_copy(out=Sin_bf[:, :, :, 0], in_=S0.rearrange("p (r q) -> p r q", r=2)),
                     reads=["S0"], writes=["Sin_bf"])
                P.op("dve", lambda e: e.tensor_copy(out=Sin_bf[:, :, :, 1:128],
                                                    in_=Sloc[:, 0:127, :].rearrange("p c (r q) -> p r q c", r=2)),
                     reads=["Sloc", "scanS"], writes=["Sin_bf"])
            dbg("Sin", Sin_bf, [128, 2, 32, 128], ["Sin_bf"])
            P.alias(["sz", "yssm"], ["Sloc", "scanS"])

            for j in range(8):
                banks = next_banks(2)
                proj_fm(CZ + j, banks, MAIN)
                for hidx, b in enumerate(banks):
                    P.op("act", lambda e, b=b, j=j, hidx=hidx: e.activation(
                        out=sz[:, j, hidx * 512:(hidx + 1) * 512], in_=ps[b][:], func=AF.Silu),
                        reads=[PS[b]], writes=["sz"])

            GC = math.sqrt(2.0 / math.pi)
            ctj = tmpB[:, 0:256].rearrange("p (r a b) -> p r a b", r=2, a=4)
            h1 = tmpB[:, 256:384].rearrange("p (a b) -> p a b", a=4)
            h2 = tmpB[:, 384:512].rearrange("p (a b) -> p a b", a=4)
            def hout_gen(j):
                hb = j % 2
                Hj = Hout[:, hb]
                P.dma("sp", ctj[:, 0], CTre_d[:, 4 * j:4 * j + 4, :], writes=["tmpB"])
                P.dma("sp", ctj[:, 1], CTim_d[:, 4 * j:4 * j + 4, :], writes=["tmpB"])
                ar_ = pw_re[:, 1:9, 4 * j:4 * j + 4].rearrange("p t q -> p q t").unsqueeze(3).to_broadcast([128, 4, 8, 32])
                ai_ = pw_im[:, 1:9, 4 * j:4 * j + 4].rearrange("p t q -> p q t").unsqueeze(3).to_broadcast([128, 4, 8, 32])
                cr_ = ctj[:, 0].unsqueeze(2).to_broadcast([128, 4, 8, 32])
                ci_ = ctj[:, 1].unsqueeze(2).to_broadcast([128, 4, 8, 32])
                h1 = wstf[1][:, 0:1024].rearrange("p (q t h) -> p q t h", q=4, t=8)
                h2 = wstf[1][:, 1024:2048].rearrange("p (q t h) -> p q t h", q=4, t=8)
                TK1, TK2 = ("wst", 1), ("wst", 1)
                HE = "dve"
                P.op(HE, lambda e: e.tensor_tensor(out=h1, in0=cr_, in1=ar_, op=ALU.mult), reads=["tmpB", "ss"], writes=[TK1])
                P.op(HE, lambda e: e.tensor_tensor(out=h2, in0=ci_, in1=ai_, op=ALU.mult), reads=["tmpB", "ss"], writes=[TK2])
                P.op(HE, lambda e: e.tensor_tensor(out=Hj[:, :, :, 0, :], in0=h1, in1=h2, op=ALU.subtract),
                     reads=[TK1, TK2], writes=[("Hout", hb)])
                P.op(HE, lambda e: e.tensor_tensor(out=h1, in0=cr_, in1=ai_, op=ALU.mult), reads=["tmpB", "ss"], writes=[TK1])
                P.op(HE, lambda e: e.tensor_tensor(out=h2, in0=ci_, in1=ar_, op=ALU.mult), reads=["tmpB", "ss"], writes=[TK2])
                P.op(HE, lambda e: e.tensor_tensor(out=h1, in0=h1, in1=h2, op=ALU.add), reads=[TK1, TK2], writes=[TK1])
                P.op("act", lambda e: e.activation(out=Hj[:, :, :, 1, :], in_=h1, func=AF.Copy, scale=-1.0),
                     reads=[TK1], writes=[("Hout", hb)])

            def ssm_mm(j):
                hb = j % 2
                Hj = Hout[:, hb]
                yb = next_banks(2)
                for tau in range(8):
                    b = yb[tau // 4]
                    reg = tau % 4
                    for tp in range(tau + 1):
                        P.op("pe", lambda e, b=b, reg=reg, tau=tau, tp=tp, j=j: e.matmul(
                            ps[b][:, reg * 128:(reg + 1) * 128], lhsT=KBD[:, j, tau - tp, :], rhs=uTv[:, j, tp, :],
                            start=(tp == 0), stop=False), reads=["KBD", ("uT", j)], writes=[PS[b]])
                    for Q in range(4):
                        for ri in range(2):
                            P.op("pe", lambda e, b=b, reg=reg, tau=tau, Q=Q, ri=ri, Hj=Hj, j=j: e.matmul(
                                ps[b][32 * Q:32 * Q + 32, reg * 128:(reg + 1) * 128], lhsT=Hj[:, Q, tau, ri, :],
                                rhs=Sin_bf[:, ri, 4 * j + Q, :], start=False, stop=False, tile_position=(0, 32 * Q)),
                                reads=[("Hout", hb), "Sin_bf"], writes=[PS[b]])
                    P.op("pe", lambda e, b=b, reg=reg: e.matmul(
                        ps[b][:, reg * 128:(reg + 1) * 128], lhsT=zeros_bf[:], rhs=zeros_bf[:], start=False, stop=True),
                        reads=["zeros_bf"], writes=[PS[b]])
                return yb

            def ssm_gelu(j, yb):
                xs, ts, TXs, TTs, pins = [], [], [], [], []
                for half2, b in enumerate(yb):
                    xs.append(tmpA[:, half2 * 1024:half2 * 1024 + 512])
                    ts.append(tmpA[:, half2 * 1024 + 512:half2 * 1024 + 1024])
                    TXs.append("tA%d" % (2 * half2))
                    TTs.append("tA%d" % (2 * half2 + 1))
                    pins.append(ps[b][:].rearrange("p (t c) -> p t c", t=4))
                for h, b in enumerate(yb):
                    P.op("act", lambda e, h=h: e.activation(
                        out=xs[h].rearrange("p (c t) -> p t c", t=4), in_=pins[h], func=AF.Copy), reads=[PS[b]], writes=[TXs[h]])
                    P.op("act", lambda e, h=h: e.activation(
                        out=ts[h].rearrange("p (c t) -> p t c", t=4), in_=pins[h], func=AF.Square, scale=math.sqrt(0.044715)),
                        reads=[PS[b]], writes=[TTs[h]])
                for h in range(2):
                    P.op("dve", lambda e, h=h: e.scalar_tensor_tensor(
                        out=ts[h], in0=ts[h], scalar=1.0, in1=xs[h], op0=ALU.add, op1=ALU.mult),
                        reads=[TTs[h], TXs[h]], writes=[TTs[h]])
                for h in range(2):
                    P.op("act", lambda e, h=h: e.activation(out=ts[h], in_=ts[h], func=AF.Sigmoid, scale=2.0 * GC),
                         reads=[TTs[h]], writes=[TTs[h]])
                for h in range(2):
                    P.op("dve", lambda e, j=j, h=h: e.tensor_tensor(
                        out=yssm[:, j, :].rearrange("p (c h t) -> p c h t", h=2, t=4)[:, :, h, :],
                        in0=xs[h].rearrange("p (c t) -> p c t", t=4), in1=ts[h].rearrange("p (c t) -> p c t", t=4), op=ALU.mult),
                        reads=[TXs[h], TTs[h]], writes=["yssm"])

            hout_gen(0)
            ybs = {}
            for j in range(8):
                ybs[j] = ssm_mm(j)
                if j + 1 < 8:
                    hout_gen(j + 1)
                if j >= 1:
                    ssm_gelu(j - 1, ybs.pop(j - 1))
            ssm_gelu(7, ybs.pop(7))
            dbg("yssm", yssm, [128, 8, NT], ["yssm"])
            if stop == "Y":
                return

            P.alias(["vglu"], ["KBD"])
            for c in range(8):
                ba = next_banks(2)
                proj8(w_glu_v[:, :, c * 128:(c + 1) * 128], yssm, ["yssm"], ba)
                bb2 = next_banks(2)
                proj8(w_glu_v[:, :, (8 + c) * 128:(9 + c) * 128], yssm, ["yssm"], bb2)
                for hidx in range(2):
                    sgl = tmpA[:, hidx * 512:(hidx + 1) * 512]
                    tg = tmpA[:, 1024 + hidx * 512:1024 + (hidx + 1) * 512]
                    P.op("act", lambda e, hidx=hidx, sgl=sgl, c=c, bb2=bb2: e.activation(
                        out=sgl, in_=ps[bb2[hidx]][:], func=AF.Sigmoid, bias=bglu[:, 8 + c:9 + c]),
                        reads=[PS[bb2[hidx]], "bglu", "tA0", "tA1"], writes=["tA%d" % hidx])
                    P.op("dve", lambda e, hidx=hidx, sgl=sgl, tg=tg, c=c, ba=ba: e.scalar_tensor_tensor(
                        out=tg, in0=ps[ba[hidx]][:], scalar=bglu[:, c:c + 1], in1=sgl, op0=ALU.add, op1=ALU.mult),
                        reads=[PS[ba[hidx]], "tA%d" % hidx, "bglu", "tA2", "tA3"], writes=["tA%d" % (2 + hidx)])
                    P.op("dve", lambda e, hidx=hidx, tg=tg, c=c: e.tensor_tensor(
                        out=vglu[:, c, hidx * 512:(hidx + 1) * 512], in0=tg, in1=sz[:, c, hidx * 512:(hidx + 1) * 512],
                        op=ALU.mult), reads=["tA%d" % (2 + hidx), "sz"], writes=["vglu"])
            dbg("vglu", vglu, [128, 8, NT], ["vglu"])

            for c in range(16):
                by = next_banks(2)
                proj8(w_sp_v[:, :, c * 128:(c + 1) * 128], vglu, ["vglu"], by)
                bg = next_banks(2)
                proj_fm(CGS + c, bg, MAIN)
                for hidx in range(2):
                    sg = tmpA[:, hidx * 512:(hidx + 1) * 512]
                    tg = tmpA[:, 1024 + hidx * 512:1024 + (hidx + 1) * 512]
                    P.op("act", lambda e, hidx=hidx, sg=sg, bg=bg: e.activation(out=sg, in_=ps[bg[hidx]][:], func=AF.Sigmoid),
                         reads=[PS[bg[hidx]]], writes=["tA%d" % hidx])
                    P.op("dve", lambda e, hidx=hidx, sg=sg, tg=tg, by=by: e.tensor_tensor(
                        out=tg, in0=ps[by[hidx]][:], in1=sg, op=ALU.mult),
                        reads=[PS[by[hidx]], "tA%d" % hidx], writes=["tA%d" % (2 + hidx)])
                    P.op("dve", lambda e, hidx=hidx, tg=tg, c=c: e.tensor_tensor(
                        out=Pa[:, c, hidx * 512:(hidx + 1) * 512], in0=tg, in1=Pa[:, c, hidx * 512:(hidx + 1) * 512],
                        op=ALU.add), reads=["tA%d" % (2 + hidx), "Pa"], writes=["Pa"])
            dbg("merged", Pa, [128, 16, NT], ["Pa"])

            w_out_v = w_out_d.rearrange("(kt p) c -> p kt c", p=128)
            P.alias([("woutbf", 0), ("woutbf", 1)], ["hT"])
            P.alias([("xres", 0), ("xres", 1), ("ores", 0), ("ores", 1)], UT)
            def load_wout(q):
                wb = q % 2
                for k4 in range(4):
                    sl = state["w"] % 2
                    state["w"] += 1
                    P.dma("sp", wst4[sl], w_out_v[:, k4 * 4:k4 * 4 + 4, q * 512:(q + 1) * 512], writes=[("wst", sl)])
                    P.op("act", lambda e, sl=sl, wb=wb, k4=k4: e.activation(
                        out=woutbf[:, wb, k4 * 4:k4 * 4 + 4, :], in_=wst4[sl], func=AF.Copy),
                        reads=[("wst", sl)], writes=[("woutbf", wb)])

            load_wout(0)
            for q in range(4):
                wb = q % 2
                if q + 1 < 4:
                    load_wout(q + 1)
                for blk in range(8):
                    b = next_banks(1)[0]
                    for kt in range(KT):
                        P.op("pe", lambda e, b=b, kt=kt, blk=blk, wb=wb: e.matmul(
                            ps[b][:], lhsT=Pa[:, kt, blk * 128:(blk + 1) * 128], rhs=woutbf[:, wb, kt, :],
                            start=(kt == 0), stop=(kt == KT - 1)), reads=["Pa", ("woutbf", wb)], writes=[PS[b]])
                    xb = (q * 8 + blk) % 2
                    P.dma("sp", xres[:, xb, :], x_d[128 + blk * 128:128 + (blk + 1) * 128, q * 512:(q + 1) * 512],
                          writes=[("xres", xb)])
                    P.op("dve", lambda e, b=b, xb=xb: e.tensor_tensor(out=ores[:, xb, :], in0=ps[b][:], in1=xres[:, xb, :],
                                                                      op=ALU.add),
                         reads=[PS[b], ("xres", xb)], writes=[("ores", xb)])
                    ev = P.dma("pool", out_d[blk * 128:(blk + 1) * 128, q * 512:(q + 1) * 512], ores[:, xb, :],
                               reads=[("ores", xb)], writes=[("out", q, blk)])
                    out_evs.append(ev)
        body()
        P.wait_all("sp", out_evs + list(dbg_out.values()))
        P.emit(st)
    return nc, wseq_rec


_CACHE = {}


def kernel(**inputs):
    inp = {k: np.asarray(v) for k, v in inputs.items()}
    sh = _prep_shared(inp)
    in_maps = []
    for c in range(NCORES):
        m = dict(sh)
        m.update(_prep_core(inp, c))
        in_maps.append(m)
    if "nc" not in _CACHE:
        _CACHE["nc"] = build()
    nc = _CACHE["nc"]
    res = run_bass_kernel_spmd(nc, in_maps, core_ids=list(range(NCORES)))
    out = np.zeros((2, 4096, D), np.float32)
    for c in range(NCORES):
        b, j = c // 4, c % 4
        out[b, j * NT:(j + 1) * NT] = res.results[c]["out"]
    return out
```

```python
import contextlib
import math
import numpy as np
import concourse.bass as bass
import concourse.mybir as mybir
from concourse.bass_utils import run_bass_kernel_spmd

F32 = mybir.dt.float32
BF16 = mybir.dt.bfloat16
ALU = mybir.AluOpType
AF = mybir.ActivationFunctionType

N_DMA_SEMS = 24
N_POOL_SEMS = 6
NCORES = 8
D = 2048
NT = 1024
NH = 1152
KT = 16
IN_W = 8704
EPS = 1e-6
SCAN_ENG = "pool"
TREE_SCAN = True
NKMAX = 4
PUMP_C = 0
PUMP_T = 0
CC_INC = 1
USE_CC = True
CAST_ENGS = ("act",)


class Prog:
    ENGS = ("pe", "act", "dve", "pool", "sp")

    def __init__(self, nc):
        self.nc = nc
        self.ops = {e: [] for e in self.ENGS}
        self.cnt = {e: 0 for e in self.ENGS}
        self.seen = {e: {} for e in self.ENGS}
        self.last_w = {}
        self.readers = {}
        self.dma_cnt = [0] * (N_DMA_SEMS + 1 + N_POOL_SEMS)
        self.pool_rr = 0
        self.dma_rr = 0

    def _deps(self, eng, reads, writes):
        deps = []
        for t in reads:
            w = self.last_w.get(t)
            if w is not None:
                deps.append(w)
        for t in writes:
            w = self.last_w.get(t)
            if w is not None:
                deps.append(w)
            deps.extend(self.readers.get(t, ()))
        need = {}
        for (s, v) in deps:
            if s == "pe" and eng == "pe":
                continue
            if self.seen[eng].get(s, 0) >= v:
                continue
            if need.get(s, 0) < v:
                need[s] = v
        for s, v in need.items():
            self.seen[eng][s] = v
        return list(need.items())

    def _commit(self, ev, reads, writes):
        for t in reads:
            self.readers.setdefault(t, []).append(ev)
        for t in writes:
            self.last_w[t] = ev
            self.readers[t] = []

    def op(self, eng, fn, reads=(), writes=()):
        psr = [t for t in reads if isinstance(t, str) and t.startswith("ps")]
        if psr:
            reads = [t for t in reads if t not in psr]
            writes = list(writes) + psr
        waits = self._deps(eng, reads, writes)
        self.cnt[eng] += 1
        ev = (eng, self.cnt[eng])
        self.ops[eng].append(("op", waits, fn, None))
        self._commit(ev, reads, writes)
        return ev

    def dma(self, q, out, in_, reads=(), writes=(), **kw):
        if q == "pool":
            i = N_DMA_SEMS + 1 + self.pool_rr
            self.pool_rr = (self.pool_rr + 1) % N_POOL_SEMS
        else:
            i = self.dma_rr
            self.dma_rr = (self.dma_rr + 1) % N_DMA_SEMS
        sname = ("dma", i)
        waits = self._deps(q, reads, writes)
        prev = self.dma_cnt[i]
        if prev > 0 and self.seen[q].get(sname, 0) < prev:
            waits.append((sname, prev))
            self.seen[q][sname] = prev
        self.dma_cnt[i] += 16
        ev = (sname, self.dma_cnt[i])
        self.ops[q].append(("dma", waits, (out, in_, kw), i))
        self._commit(ev, reads, writes)
        return ev

    def custom(self, eng, fn, sem_i, reads=(), writes=()):
        sname = ("dma", sem_i)
        waits = self._deps(eng, reads, writes)
        self.dma_cnt[sem_i] += CC_INC
        ev = (sname, self.dma_cnt[sem_i])
        self.ops[eng].append(("custom", waits, fn, sem_i))
        self._commit(ev, reads, writes)
        return ev

    def alias(self, new_tokens, old_tokens):
        evs = []
        for t in old_tokens:
            w = self.last_w.get(t)
            if w is not None:
                evs.append(w)
            evs.extend(self.readers.get(t, ()))
        for t in new_tokens:
            self.last_w.pop(t, None)
            self.readers[t] = list(evs)

    def wait_all(self, eng, evs):
        waits = []
        for (s, v) in evs:
            if self.seen[eng].get(s, 0) < v:
                waits.append((s, v))
                self.seen[eng][s] = v
        self.ops[eng].append(("wait", waits, None, None))

    def emit(self, st):
        nc = self.nc
        sems = {}
        for e in self.ENGS:
            sems[e] = st.enter_context(nc.semaphore("s_" + e))
        for i in range(N_DMA_SEMS + 1 + N_POOL_SEMS):
            sems[("dma", i)] = st.enter_context(nc.semaphore("s_dma%d" % i))
        block = st.enter_context(nc.Block())

        def run(engname):
            def body(eng):
                for kind, waits, payload, di in self.ops[engname]:
                    for (s, v) in waits:
                        eng.wait_ge(sems[s], v)
                    if kind == "op":
                        payload(eng).then_inc(sems[engname], 1)
                    elif kind == "dma":
                        out, in_, kw = payload
                        eng.dma_start(out=out, in_=in_, **kw).then_inc(sems[("dma", di)], 16)
                    elif kind == "custom":
                        payload(eng).then_inc(sems[("dma", di)], CC_INC)
            return body

        block.tensor(run("pe"))
        block.scalar(run("act"))
        block.vector(run("dve"))
        block.gpsimd(run("pool"))
        block.sync(run("sp"))


def _q_head_order():
    order = []
    for c in range(8):
        if c < 4:
            order += [c, 4 + c]
        else:
            order += [8 + (c - 4), 12 + (c - 4)]
    return order


def _in_col_perm():
    qh = _q_head_order()
    attn_feat = np.concatenate([np.arange(h * 64, (h + 1) * 64) for h in qh])
    cols = [attn_feat, np.arange(1024, 1536), 1536 + attn_feat, np.arange(2560, IN_W)]
    return np.concatenate(cols), attn_feat


def _prep_shared(inp):
    f = np.float32
    perm, attn_feat = _in_col_perm()
    sh = {}
    sh["w_in"] = np.ascontiguousarray(inp["w_in"][:, perm])
    sh["w_ap"] = np.ascontiguousarray(inp["w_attn_proj"][attn_feat, :])
    sh["w_glu"] = np.ascontiguousarray(inp["w_glu"])
    sh["w_sp"] = np.ascontiguousarray(inp["w_ssm_proj"])
    sh["w_out"] = np.ascontiguousarray(inp["w_out"])
    sh["normw_row"] = np.ascontiguousarray(np.broadcast_to(inp["norm_w"][None, :], (128, D))).astype(f)
    sh["qkw"] = np.stack([np.tile(inp["q_norm_w"], 2), np.tile(inp["k_norm_w"], 2)], axis=1).astype(f)
    sh["sinks_rep"] = np.ascontiguousarray(np.broadcast_to(inp["sinks"][None, :], (128, 16))).astype(f)
    sh["bglu_fm"] = np.ascontiguousarray(inp["b_glu"].reshape(16, 128).T).astype(f)
    sh["ident"] = np.eye(128, dtype=f)
    s_idx = np.arange(128)[:, None]
    i_idx = np.arange(128)[None, :]
    NEG = -30000.0
    sh["maskP"] = np.where(s_idx > i_idx, 0.0, NEG).astype(f)
    sh["maskC"] = np.where(s_idx <= i_idx, 0.0, NEG).astype(f)

    def mp(a):
        return np.ascontiguousarray(a.reshape(32, 2, 64).transpose(1, 2, 0).reshape(128, 32)).astype(f)

    sh["Are"] = mp(inp["A_re"])
    sh["Aim"] = mp(inp["A_im"])
    sh["ldt"] = mp(np.broadcast_to(inp["log_dt"][:, None], (64, 64)))

    def bt(b):
        o = np.zeros((2, 64, 32, 2, 16), f)
        b4 = b.reshape(32, 2, 64, 16)
        for gi in range(2):
            o[gi, :, :, gi, :] = b4[:, gi].transpose(1, 0, 2)
        return np.ascontiguousarray(o.reshape(128, 32, 32))

    def ct(c):
        o = np.zeros((2, 64, 32, 2, 16), f)
        c4 = c.reshape(32, 2, 16, 64)
        for gi in range(2):
            o[gi, :, :, gi, :] = c4[:, gi].transpose(2, 0, 1)
        return np.ascontiguousarray(o.reshape(128, 32, 32))

    sh["BTre"] = bt(inp["B_re"])
    sh["BTim"] = bt(inp["B_im"])
    sh["CTre"] = ct(inp["C_re"])
    sh["CTim"] = ct(inp["C_im"])
    sh["Dfm"] = np.ascontiguousarray(inp["D_skip"].reshape(8, 128).T).astype(f)
    return sh


def _prep_core(inp, c):
    f = np.float32
    b, j = c // 4, c % 4
    x = inp["x"]
    xc = np.zeros((NH, D), f)
    t0 = j * NT
    if j > 0:
        xc[:] = x[b, t0 - 128:t0 + NT]
    else:
        xc[128:] = x[b, 0:NT]
    s_idx = np.arange(128)[:, None]
    i_idx = np.arange(128)[None, :]
    maskP0 = np.where(s_idx > i_idx, 0.0, -30000.0).astype(f) if j > 0 else np.full((128, 128), -30000.0, f)
    sel = np.zeros((3, 4), f)
    for d in range(1, 4):
        if j - d >= 0:
            sel[d - 1, j - d] = 1.0
    selr = np.ascontiguousarray(np.broadcast_to(sel.reshape(1, 12), (128, 12))).astype(f)
    return {"x": xc, "maskP0": maskP0, "sel": selr}


def build(debug=(), stop=None):
    wseq = _build(debug, stop, None)[1]
    return _build(debug, stop, wseq)[0]


def _build(debug, stop, wseq):
    wseq_rec = []
    nc = bass.Bass("TRN2", target_bir_lowering=False)

    def din(name, shape):
        return nc.dram_tensor(name, list(shape), F32, kind="ExternalInput").ap()

    x_d = din("x", [NH, D])
    w_in_d = din("w_in", [D, IN_W])
    w_ap_d = din("w_ap", [1024, D])
    w_glu_d = din("w_glu", [1024, D])
    w_sp_d = din("w_sp", [1024, D])
    w_out_d = din("w_out", [D, D])
    normw_d = din("normw_row", [128, D])
    qkw_d = din("qkw", [128, 2])
    sinks_d = din("sinks_rep", [128, 16])
    bglu_d = din("bglu_fm", [128, 16])
    ident_d = din("ident", [128, 128])
    maskP_d = din("maskP", [128, 128])
    maskC_d = din("maskC", [128, 128])
    maskP0_d = din("maskP0", [128, 128])
    Are_d = din("Are", [128, 32])
    Aim_d = din("Aim", [128, 32])
    ldt_d = din("ldt", [128, 32])
    BTre_d = din("BTre", [128, 32, 32])
    BTim_d = din("BTim", [128, 32, 32])
    CTre_d = din("CTre", [128, 32, 32])
    CTim_d = din("CTim", [128, 32, 32])
    Dfm_d = din("Dfm", [128, 8])
    sel_d = din("sel", [128, 12])
    out_d = nc.dram_tensor("out", [NT, D], F32, kind="ExternalOutput").ap()
    cc_in = nc.dram_tensor("cc_in", [128, 64], F32, kind="Internal").ap()
    cc_out = nc.dram_tensor("cc_out", [4 * 128, 64], F32, kind="Internal").ap()
    dbg_out = {}

    P = Prog(nc)
    st = contextlib.ExitStack()
    with st:
        def sb(name, shape, dt=F32):
            return st.enter_context(nc.sbuf_tensor(name, list(shape), dt))

        ps = [st.enter_context(nc.psum_tensor("ps%d" % i, [128, 512], F32)) for i in range(8)]
        PS = ["ps%d" % i for i in range(8)]

        def dbg(name, ap, shape, toks):
            if name not in debug:
                return
            d = nc.dram_tensor("dbg_" + name, list(shape), ap.dtype, kind="ExternalOutput").ap()
            dbg_out[name] = P.dma("sp", d, ap, reads=toks, writes=["dbg_" + name])

        hT = sb("hT", [128, KT, NH], BF16)
        uT = sb("uT", [128, 8, NT], BF16)
        Sloc = sb("Sloc", [128, 128, 64], F32)
        R1 = sb("R1", [128, 16384], BF16)
        R2 = sb("R2", [128, 12288], BF16)
        KBD = sb("KBD", [128, 8, 8, 128], BF16)
        wstf = [sb("wst%d" % i, [128, 2048], F32) for i in range(2)]
        wbf = [sb("wbf%d" % i, [128, KT, 128], BF16) for i in range(2)]
        tmpA = sb("tmpA", [128, 2048], F32)
        tmpB = sb("tmpB", [128, 512], F32)
        consts = sb("consts", [128, 128], F32)
        ident_bf = sb("ident_bf", [128, 128], BF16)
        ident_f = sb("ident_f", [128, 128], F32)
        masks = sb("masks", [128, 3, 128], BF16)
        sq_bf = sb("sq_bf", [128, 512], BF16)
        ss = sb("ssm_small", [128, 40, 32], F32)
        pw_re = sb("pw_re", [128, 9, 32], F32)
        pw_im = sb("pw_im", [128, 9, 32], F32)
        wk_re = sb("wk_re", [128, 8, 32], F32)
        wk_im = sb("wk_im", [128, 8, 32], F32)
        scan_t = sb("scan_t", [128, 1 if TREE_SCAN else 5, 64], F32)
        tree_t = sb("tree_t", [128, 3, NKMAX * 64], F32) if TREE_SCAN else None
        Dfm = sb("Dfm_sb", [128, 8], F32)
        sel_sb = sb("sel_sb", [128, 12], F32)
        Gall = sb("Gall", [128, 4, 64], F32)
        zeros_bf = sb("zeros_bf", [128, 128], BF16)
        ones_blk = sb("ones_blk", [128, 128], BF16)
        eskb = sb("eskb", [128, 16], BF16)
        dsel = sb("dsel", [1, 2, 128], BF16)
        wst = [w[:].rearrange("p (a b) -> p a b", a=KT) for w in wstf]
        wst4 = [w[:].rearrange("p (a b) -> p a b", a=4) for w in wstf]

        qkw = consts[:, 0:2]
        qw8 = consts[:, 2:3]
        eps_c = consts[:, 3:4]
        esk = consts[:, 16:32]
        bglu = consts[:, 32:48]
        rstd_blk = consts[:, 48:64]
        ssq_blk = consts[:, 64:80]

        def view(ar, off, shape, dt=BF16):
            n = int(np.prod(shape))
            if dt == BF16:
                a = ar[:, off:off + n]
            else:
                a = ar[:, off:off + 2 * n].bitcast(F32)
            if len(shape) == 1:
                return a
            names = " ".join("d%d" % i for i in range(len(shape)))
            kw = {"d%d" % i: shape[i] for i in range(len(shape))}
            return a.rearrange("p (%s) -> p %s" % (names, names), **kw)

        xst = view(R1, 0, [2048], F32)
        normw = view(R1, 4096, [2048], F32)
        xn_bf = view(R1, 8192, [2048])
        Vre = view(R1, 0, [8, 16, 32])
        Vim = view(R1, 4096, [8, 16, 32])
        GinT = view(R1, 8192, [4, 8, 2, 128])
        sgate = view(R1, 0, [8, NT])
        vaug = view(R1, 8192, [9, 4, 128])
        kT = view(R1, 12800, [2, NH])
        Pa = view(R1, 0, [16, NT])
        BT_f = view(R2, 0, [2, 16, 32], F32)
        CT_f = view(R2, 2048, [2, 32, 32], F32)
        CTre_bf = view(R2, 6144, [32, 32])
        nCTim_bf = view(R2, 7168, [32, 32])
        qT = view(R2, 0, [8, NT])
        pT = view(R2, 8192, [2, 2, 512])
        Sin_bf = view(R2, 0, [2, 32, 128])
        Hout = view(R2, 8192, [2, 4, 8, 2, 32])
        Sloc_bf = Sloc[:].rearrange("p c f -> p (c f)").bitcast(BF16)
        sz = view(Sloc_bf, 0, [8, NT])
        Sloc_f = Sloc[:].rearrange("p c f -> p (c f)")
        xsts = [Sloc_f[:, 0:2048], Sloc_f[:, 2048:4096]]
        normw = Sloc_f[:, 4096:6144]
        xns = [Sloc_bf[:, 12288:14336], Sloc_bf[:, 14336:16384]]
        yssm = view(Sloc_bf, 8192, [8, NT])
        vglu = KBD[:].rearrange("p a b c -> p (a b c)").rearrange("p (a b) -> p a b", a=8)
        hT_flat = hT[:].rearrange("p a b -> p (a b)")
        woutbf = view(hT_flat, 0, [2, KT, 512])
        uT_flat = uT[:].rearrange("p a b -> p (a b)")
        xres = view(uT_flat, 0, [2, 512], F32)
        ores = view(uT_flat, 2048, [2, 512], F32)

        out_evs = []

        def body():
            P.dma("sp", consts[:, 0:2], qkw_d, writes=["qkw"])
            P.dma("sp", bglu, bglu_d, writes=["bglu"])
            P.dma("sp", ident_f[:], ident_d, writes=["ident_f"])
            mf = tmpA[:, 0:384].rearrange("p (a b) -> p a b", a=3)
            P.dma("sp", mf[:, 0, :], maskP_d, writes=["tA0"])
            P.dma("sp", mf[:, 1, :], maskC_d, writes=["tA0"])
            P.dma("sp", mf[:, 2, :], maskP0_d, writes=["tA0"])
            P.dma("sp", tmpB[:, 0:16], sinks_d, writes=["tmpB"])
            P.dma("sp", Dfm[:], Dfm_d, writes=["Dfm"])
            P.dma("sp", sel_sb[:], sel_d, writes=["sel"])
            P.dma("sp", normw, normw_d, writes=["normw"])
            P.op("dve", lambda e: e.tensor_copy(out=ident_bf[:], in_=ident_f[:]), reads=["ident_f"], writes=["ident_bf"])
            P.op("dve", lambda e: e.memset(zeros_bf[:], 0.0), writes=["zeros_bf"])
            P.op("dve", lambda e: e.memset(eps_c, EPS), writes=["eps"])
            P.op("dve", lambda e: e.tensor_copy(out=masks[:], in_=mf), reads=["tA0"], writes=["masks"])
            P.op("act", lambda e: e.activation(out=esk, in_=tmpB[:, 0:16], func=AF.Exp), reads=["tmpB"], writes=["esk"])
            P.op("dve", lambda e: e.tensor_scalar(out=qw8, in0=consts[:, 0:1], scalar1=0.125, scalar2=None, op0=ALU.mult),
                 reads=["qkw"], writes=["qw8"])
            P.op("dve", lambda e: e.tensor_copy(out=eskb[:], in_=esk), reads=["esk"], writes=["eskb"])
            P.op("dve", lambda e: e.memset(dsel[0:1, 0, 0:64], 0.0), writes=["dsel"])
            P.op("dve", lambda e: e.memset(dsel[0:1, 0, 64:128], 1.0), writes=["dsel"])
            P.op("dve", lambda e: e.memset(dsel[0:1, 1, 0:64], 1.0), writes=["dsel"])
            P.op("dve", lambda e: e.memset(dsel[0:1, 1, 64:128], 0.0), writes=["dsel"])
            P.op("dve", lambda e: e.memset(ones_blk[:], 0.0), writes=["ones_blk"])
            P.op("dve", lambda e: e.memset(ones_blk[0:64, 0:64], 1.0), writes=["ones_blk"])
            P.op("dve", lambda e: e.memset(ones_blk[64:128, 64:128], 1.0), writes=["ones_blk"])

            bank_rr = {"i": 0}

            def next_banks(n):
                i = bank_rr["i"]
                bank_rr["i"] = (i + n) % 8
                return [(i + k) % 8 for k in range(n)]

            state = {"w": 0}

            def _issue_w(i, src):
                sl = i % 2
                ktn = src.shape[1]
                P.dma("sp", wst[sl][:, 0:ktn, :], src, writes=[("wst", sl)])
                P.op("act", lambda e: e.activation(out=wbf[sl][:, 0:ktn, :], in_=wst[sl][:, 0:ktn, :], func=AF.Copy),
                     reads=[("wst", sl)], writes=[("wbf", sl)])

            def load_w(src):
                i = state["w"]
                state["w"] += 1
                wseq_rec.append(src)
                if wseq is None:
                    _issue_w(i, src)
                else:
                    if i == 0:
                        _issue_w(0, wseq[0])
                    if i + 1 < len(wseq):
                        _issue_w(i + 1, wseq[i + 1])
                return wbf[i % 2], ("wbf", i % 2)

            w_in_v = w_in_d.rearrange("(kt p) c -> p kt c", p=128)

            def win_chunk(ci):
                return w_in_v[:, :, ci * 128:(ci + 1) * 128]

            CQ, CK, CV, CG, CU, CZ, CGA, CGS = 0, 8, 10, 12, 20, 28, 36, 52
            MAIN = [(128, 512), (640, 512)]

            def proj_fm(ci, banks, spans):
                wt, wtok = load_w(win_chunk(ci))
                for b, (t0, n) in zip(banks, spans):
                    for kt in range(KT):
                        P.op("pe", lambda e, b=b, kt=kt, t0=t0, n=n: e.matmul(
                            ps[b][:, 0:n], lhsT=wt[:, kt, :], rhs=hT[:, kt, t0:t0 + n],
                            start=(kt == 0), stop=(kt == KT - 1)), reads=[wtok, "hT"], writes=[PS[b]])

            SE = "pool"
            (I_ARE, I_AIM, I_LDT, I_DT, I_LRE, I_LIM, I_MAG, I_COS, I_SIN, I_AR, I_AI, I_DEN, I_NR, I_T1, I_T2, I_CFR,
             I_CFI, I_T3, I_T4, I_A8I) = range(20)
            P.dma("sp", ss[:, I_ARE, :], Are_d, writes=["ss"])
            P.dma("sp", ss[:, I_AIM, :], Aim_d, writes=["ss"])
            P.dma("sp", ss[:, I_LDT, :], ldt_d, writes=["ss"])
            P.dma("sp", CT_f[:, 0], CTre_d, writes=["CT_f"])
            P.dma("sp", CT_f[:, 1], CTim_d, writes=["CT_f"])

            def tt(o, a, b, op, eng=SE):
                P.op(eng, lambda e: e.tensor_tensor(out=o, in0=a, in1=b, op=op), reads=["ss"], writes=["ss"])

            def tsc(o, a, s1, op0, s2=None, op1=None, eng=SE):
                if op1 is None:
                    P.op(eng, lambda e: e.tensor_scalar(out=o, in0=a, scalar1=s1, scalar2=None, op0=op0),
                         reads=["ss"], writes=["ss"])
                else:
                    P.op(eng, lambda e: e.tensor_scalar(out=o, in0=a, scalar1=s1, scalar2=s2, op0=op0, op1=op1),
                         reads=["ss"], writes=["ss"])

            def act(o, a, func, **kw):
                P.op("act", lambda e: e.activation(out=o, in_=a, func=func, **kw), reads=["ss"], writes=["ss"])

            S_ = lambda i: ss[:, i, :]
            act(S_(I_DT), S_(I_LDT), AF.Exp)
            tt(S_(I_LRE), S_(I_DT), S_(I_ARE), ALU.mult)
            tt(S_(I_LIM), S_(I_DT), S_(I_AIM), ALU.mult)
            act(S_(I_MAG), S_(I_LRE), AF.Exp)
            hpi_c = consts[:, 4:5]
            P.op(SE, lambda e: e.memset(hpi_c, 0.5 * math.pi), writes=["hpi"])
            act(S_(I_SIN), S_(I_LIM), AF.Sin, scale=1.0 / 16)
            P.op("act", lambda e: e.activation(out=S_(I_COS), in_=S_(I_LIM), func=AF.Sin, scale=1.0 / 16, bias=hpi_c),
                 reads=["ss", "hpi"], writes=["ss"])
            for _ in range(4):
                tt(S_(I_T1), S_(I_COS), S_(I_COS), ALU.mult)
                tt(S_(I_T2), S_(I_SIN), S_(I_SIN), ALU.mult)
                tt(S_(I_T3), S_(I_COS), S_(I_SIN), ALU.mult)
                tt(S_(I_COS), S_(I_T1), S_(I_T2), ALU.subtract)
                tsc(S_(I_SIN), S_(I_T3), 2.0, ALU.mult)
            tt(S_(I_AR), S_(I_MAG), S_(I_COS), ALU.mult)
            tt(S_(I_AI), S_(I_MAG), S_(I_SIN), ALU.mult)
            tt(S_(I_T1), S_(I_ARE), S_(I_ARE), ALU.mult)
            tt(S_(I_T2), S_(I_AIM), S_(I_AIM), ALU.mult)
            tt(S_(I_DEN), S_(I_T1), S_(I_T2), ALU.add)
            P.op("dve", lambda e: e.reciprocal(out=S_(I_DEN), in_=S_(I_DEN)), reads=["ss"], writes=["ss"])
            tsc(S_(I_NR), S_(I_AR), -1.0, ALU.add)
            tt(S_(I_T1), S_(I_NR), S_(I_ARE), ALU.mult)
            tt(S_(I_T2), S_(I_AI), S_(I_AIM), ALU.mult)
            tt(S_(I_T1), S_(I_T1), S_(I_T2), ALU.add)
            tt(S_(I_CFR), S_(I_T1), S_(I_DEN), ALU.mult)
            tt(S_(I_T1), S_(I_AI), S_(I_ARE), ALU.mult)
            tt(S_(I_T2), S_(I_NR), S_(I_AIM), ALU.mult)
            tt(S_(I_T1), S_(I_T1), S_(I_T2), ALU.subtract)
            tt(S_(I_CFI), S_(I_T1), S_(I_DEN), ALU.mult)
            P.op(SE, lambda e: e.memset(pw_re[:, 0, :], 1.0), reads=["ss"], writes=["ss"])
            P.op(SE, lambda e: e.memset(pw_im[:, 0, :], 0.0), reads=["ss"], writes=["ss"])
            P.op(SE, lambda e: e.tensor_copy(out=pw_re[:, 1, :], in_=S_(I_AR)), reads=["ss"], writes=["ss"])
            P.op(SE, lambda e: e.tensor_copy(out=pw_im[:, 1, :], in_=S_(I_AI)), reads=["ss"], writes=["ss"])

            def cmul(o_re, o_im, a_re, a_im, b_re, b_im, eng=SE, tok="ss", t0=I_T1):
                t = [ss[:, t0 + i, :] for i in range(4)] if t0 != I_T1 else [S_(I_T1), S_(I_T2), S_(I_T3), S_(I_T4)]

                def o(out, a, b, op):
                    P.op(eng, lambda e: e.tensor_tensor(out=out, in0=a, in1=b, op=op), reads=[tok], writes=[tok])
                o(t[0], a_re, b_re, ALU.mult)
                o(t[1], a_im, b_im, ALU.mult)
                o(t[2], a_re, b_im, ALU.mult)
                o(t[3], a_im, b_re, ALU.mult)
                o(o_re, t[0], t[1], ALU.subtract)
                o(o_im, t[2], t[3], ALU.add)

            for k in range(2, 9):
                cmul(pw_re[:, k, :], pw_im[:, k, :], pw_re[:, k - 1, :], pw_im[:, k - 1, :], S_(I_AR), S_(I_AI))
            for k in range(8):
                cmul(wk_re[:, k, :], wk_im[:, k, :], pw_re[:, k, :], pw_im[:, k, :], S_(I_CFR), S_(I_CFI))
            P.op(SE, lambda e: e.tensor_copy(out=CTre_bf, in_=CT_f[:, 0]), reads=["CT_f"], writes=["CTbf"])
            P.op(SE, lambda e: e.tensor_scalar(out=nCTim_bf, in0=CT_f[:, 1], scalar1=-1.0, scalar2=None, op0=ALU.mult),
                 reads=["CT_f"], writes=["CTbf"])
            CA = ss[:, 24:26, :].rearrange("p a b -> p (a b)")
            a8i = ss[:, I_A8I, :]
            S0 = ss[:, 26:28, :].rearrange("p a b -> p (a b)")
            Fst = ss[:, 28:30, :].rearrange("p a b -> p (a b)")
            Z0 = ss[:, 30:32, :].rearrange("p a b -> p (a b)")
            E = SCAN_ENG
            P.op(E, lambda e: e.tensor_copy(out=ss[:, 24, :], in_=pw_re[:, 8, :]), reads=["ss"], writes=["sc"])
            P.op(E, lambda e: e.tensor_copy(out=ss[:, 25, :], in_=pw_re[:, 8, :]), reads=["ss"], writes=["sc"])
            P.op(E, lambda e: e.tensor_copy(out=a8i, in_=pw_im[:, 8, :]), reads=["ss"], writes=["sc"])
            P.op(E, lambda e: e.memset(Z0, 0.0), writes=["sc"])

            dbg("pw", pw_re[:], [128, 9, 32], ["ss"])
            dbg("wk", wk_re[:], [128, 8, 32], ["ss"])
            vt = [tmpA[:, i * 512:(i + 1) * 512].rearrange("p (a b) -> p a b", a=16) for i in range(4)]
            uTv = uT[:].rearrange("p j (c t) -> p j t c", t=8)
            Sloc_v2 = Sloc[:].rearrange("p c (r j q) -> p q j r c", r=2, q=4)
            def gen_V(half):
                P0 = half * 16
                P.dma("sp", BT_f[:, 0], BTre_d[:, P0:P0 + 16, :], writes=["BT_f"])
                P.dma("sp", BT_f[:, 1], BTim_d[:, P0:P0 + 16, :], writes=["BT_f"])
                for k in range(8):
                    wr = wk_re[:, k, P0:P0 + 16].unsqueeze(2).to_broadcast([128, 16, 32])
                    wi = wk_im[:, k, P0:P0 + 16].unsqueeze(2).to_broadcast([128, 16, 32])
                    VE = "dve" if k % 2 == 0 else "pool"
                    sfx = VE
                    P.op(VE, lambda e, wr=wr: e.tensor_tensor(out=vt[0], in0=BT_f[:, 0], in1=wr, op=ALU.mult),
                         reads=["BT_f", "ss", "tA0"], writes=["tA0"])
                    P.op(VE, lambda e, wi=wi: e.tensor_tensor(out=vt[1], in0=BT_f[:, 1], in1=wi, op=ALU.mult),
                         reads=["BT_f", "ss", "tA0"], writes=["tA1"])
                    P.op(VE, lambda e, k=k: e.tensor_tensor(out=Vre[:, k], in0=vt[0], in1=vt[1], op=ALU.subtract),
                         reads=["tA0", "tA1"], writes=[("V", k)])
                    P.op(VE, lambda e, wr=wr: e.tensor_tensor(out=vt[2], in0=BT_f[:, 1], in1=wr, op=ALU.mult),
                         reads=["BT_f", "ss"], writes=["tA2"])
                    P.op(VE, lambda e, wi=wi: e.tensor_tensor(out=vt[3], in0=BT_f[:, 0], in1=wi, op=ALU.mult),
                         reads=["BT_f", "ss"], writes=["tA3"])
                    P.op(VE, lambda e, k=k: e.tensor_tensor(out=Vim[:, k], in0=vt[2], in1=vt[3], op=ALU.add),
                         reads=["tA2", "tA3"], writes=[("V", k)])
                    yield
                dbg("V", Vre, [128, 8, 16, 32], [("V", k) for k in range(8)])
                if stop == "S2":
                    return

            gv0 = gen_V(0)

            def normA(blk):
                xst = xsts[blk % 2]
                xn_bf = xns[blk % 2]
                XT = ("xst", blk % 2)
                XN = ("xn", blk % 2)
                P.dma("sp", xst, x_d[blk * 128:(blk + 1) * 128, :], reads=["tA0"], writes=[XT])
                P.op("act", lambda e: e.activation(out=xn_bf, in_=xst, func=AF.Square, accum_out=ssq_blk[:, blk:blk + 1]),
                     reads=[XT], writes=[XN, ("ssq", blk)])
                P.op("act", lambda e: e.activation(out=rstd_blk[:, blk:blk + 1], in_=ssq_blk[:, blk:blk + 1],
                                                   func=AF.Sqrt, scale=1.0 / D, bias=eps_c),
                     reads=[("ssq", blk), "eps"], writes=[("rstd", blk)])
                P.op("dve", lambda e: e.reciprocal(out=rstd_blk[:, blk:blk + 1], in_=rstd_blk[:, blk:blk + 1]),
                     reads=[("rstd", blk)], writes=[("rstd", blk)])
                P.op("dve", lambda e: e.scalar_tensor_tensor(out=xn_bf, in0=xst, scalar=rstd_blk[:, blk:blk + 1],
                                                             in1=normw, op0=ALU.mult, op1=ALU.mult),
                     reads=[XT, ("rstd", blk), "normw"], writes=[XN])

            def transA(blk):
                xn_bf = xns[blk % 2]
                XN = ("xn", blk % 2)
                for g4 in range(4):
                    bank = next_banks(1)[0]
                    for i in range(4):
                        kt = g4 * 4 + i
                        P.op("pe", lambda e, bank=bank, i=i, kt=kt: e.matmul(
                            ps[bank][:, i * 128:(i + 1) * 128], lhsT=xn_bf[:, kt * 128:(kt + 1) * 128], rhs=ident_bf[:],
                            start=True, stop=True), reads=[XN, "ident_bf"], writes=[PS[bank]])
                    if g4 % 2 == 0:
                        P.op("act", lambda e, bank=bank, g4=g4: e.activation(
                            out=hT[:, g4 * 4:g4 * 4 + 4, blk * 128:(blk + 1) * 128],
                            in_=ps[bank][:].rearrange("p (a b) -> p a b", a=4), func=AF.Copy),
                            reads=[PS[bank]], writes=["hT"])
                    else:
                        P.op("dve", lambda e, bank=bank, g4=g4: e.tensor_copy(
                            out=hT[:, g4 * 4:g4 * 4 + 4, blk * 128:(blk + 1) * 128],
                            in_=ps[bank][:].rearrange("p (a b) -> p a b", a=4)),
                            reads=[PS[bank]], writes=["hT"])

            normA(0)
            for blk in range(9):
                if blk + 1 < 9:
                    normA(blk + 1)
                transA(blk)
            dbg("hT", hT[:, 0:2, :], [128, 2, NH], ["hT"])

            def gk(half):
                P0 = half * 16
                for jl in range(4):
                    for tpair in range(4):
                        bank = next_banks(1)[0]
                        for t2 in range(2):
                            tau = tpair * 2 + t2
                            for ri in range(2):
                                reg = t2 * 2 + ri
                                Vsrc = Vre if ri == 0 else Vim
                                for Q in range(4):
                                    P.op("pe", lambda e, bank=bank, reg=reg, Vsrc=Vsrc, tau=tau, jl=jl, Q=Q: e.matmul(
                                        ps[bank][32 * Q:32 * Q + 32, reg * 128:(reg + 1) * 128],
                                        lhsT=Vsrc[:, 7 - tau, jl * 4 + Q, :], rhs=ident_bf[:], start=True, stop=True,
                                        tile_position=(0, 32 * Q)),
                                        reads=[("V", 7 - tau), "ident_bf"], writes=[PS[bank]])
                        gout = GinT[:, jl, tpair * 2:tpair * 2 + 2, :, :].rearrange("p a b c -> p (a b c)")
                        if tpair % 2 == 0:
                            P.op("act", lambda e, bank=bank, gout=gout: e.activation(out=gout, in_=ps[bank][:], func=AF.Copy),
                                 reads=[PS[bank]], writes=["GinT"])
                        else:
                            P.op("dve", lambda e, bank=bank, gout=gout: e.tensor_copy(out=gout, in_=ps[bank][:]),
                                 reads=[PS[bank]], writes=["GinT"])
                dbg("GinT", GinT, [128, 4, 8, 2, 128], ["GinT"])
                for jl in range(4):
                    j = half * 4 + jl
                    for lh in range(2):
                        bank = next_banks(1)[0]
                        P.op("pe", lambda e, bank=bank: e.matmul(
                            ps[bank][:].rearrange("p (a b) -> p a b", a=4), lhsT=zeros_bf[:],
                            rhs=zeros_bf[:].unsqueeze(1).to_broadcast([128, 4, 128]), start=True, stop=False),
                            reads=["zeros_bf"], writes=[PS[bank]])
                        for l4 in range(4):
                            lag = lh * 4 + l4
                            for Q in range(4):
                                Pl = jl * 4 + Q
                                Pg = P0 + Pl
                                last = (l4 == 3 and Q == 3)
                                oap = (slice(32 * Q, 32 * Q + 32), slice(l4 * 128 + 32 * Q, l4 * 128 + 32 * Q + 32))
                                P.op("pe", lambda e, bank=bank, oap=oap, lag=lag, Q=Q, Pl=Pl, Pg=Pg: e.matmul(
                                    ps[bank][oap[0], oap[1]], lhsT=Vre[:, lag, Pl, :], rhs=CTre_bf[:, Pg, :],
                                    start=False, stop=False, tile_position=(0, 32 * Q)),
                                    reads=[("V", lag), "CTbf"], writes=[PS[bank]])
                                P.op("pe", lambda e, bank=bank, oap=oap, lag=lag, Q=Q, Pl=Pl, Pg=Pg, last=last: e.matmul(
                                    ps[bank][oap[0], oap[1]], lhsT=Vim[:, lag, Pl, :], rhs=nCTim_bf[:, Pg, :],
                                    start=False, stop=False, tile_position=(0, 32 * Q)),
                                    reads=[("V", lag), "CTbf"], writes=[PS[bank]])
                        P.op("pe", lambda e, bank=bank: e.matmul(
                            ps[bank][:].rearrange("p (a b) -> p a b", a=4), lhsT=zeros_bf[:],
                            rhs=zeros_bf[:].unsqueeze(1).to_broadcast([128, 4, 128]), start=False, stop=True),
                            reads=["zeros_bf"], writes=[PS[bank]])
                        P.op("dve", lambda e, bank=bank, j=j, lh=lh: e.tensor_copy(
                            out=KBD[:, j, lh * 4:lh * 4 + 4, :].rearrange("p a b -> p (a b)"), in_=ps[bank][:]),
                            reads=[PS[bank]], writes=["KBD"])
                        if lh == 0:
                            P.op("dve", lambda e, bank=bank, j=j: e.scalar_tensor_tensor(
                                out=KBD[:, j, 0, :], in0=ident_f[:], scalar=Dfm[:, j:j + 1], in1=ps[bank][:, 0:128],
                                op0=ALU.mult, op1=ALU.add), reads=[PS[bank], "ident_f", "Dfm", "KBD"], writes=["KBD"])
                dbg("KBDh", KBD[:], [128, 8, 8, 128], ["KBD"])

            def sl(half, hook=None):
                P0 = half * 16
                for jp in range(2):
                    for Q in range(4):
                        bank = next_banks(1)[0]
                        for jl2 in range(2):
                            jl = jp * 2 + jl2
                            j = half * 4 + jl
                            for ri in range(2):
                                reg = jl2 * 2 + ri
                                for tau in range(8):
                                    P.op("pe", lambda e, bank=bank, reg=reg, Q=Q, jl=jl, j=j, tau=tau, ri=ri: e.matmul(
                                        ps[bank][:, reg * 128:(reg + 1) * 128],
                                        lhsT=GinT[32 * Q:32 * Q + 32, jl, tau, ri, :],
                                        rhs=uTv[32 * Q:32 * Q + 32, j, tau, :],
                                        start=(tau == 0), stop=(tau == 7), tile_position=(32 * Q, 0)),
                                        reads=["GinT", ("uT", j)], writes=[PS[bank]])
                        j0 = half * 4 + jp * 2
                        sl_out = Sloc_v2[:, Q, j0:j0 + 2, :, :]
                        ps_in = ps[bank][:].rearrange("p (j r c) -> p j r c", j=2, r=2)
                        if Q % 2 == 0:
                            P.op("dve", lambda e, sl_out=sl_out, ps_in=ps_in: e.tensor_copy(out=sl_out, in_=ps_in),
                                 reads=[PS[bank]], writes=["Sloc"])
                        else:
                            P.op("act", lambda e, sl_out=sl_out, ps_in=ps_in: e.activation(out=sl_out, in_=ps_in, func=AF.Copy),
                                 reads=[PS[bank]], writes=["Sloc"])
                        if hook is not None:
                            hook()

            UT = [("uT", j) for j in range(8)]
            for j in range(8):
                banks = next_banks(2)
                proj_fm(CU + j, banks, MAIN)
                for hidx, b in enumerate(banks):
                    if hidx == 0:
                        P.op("act", lambda e, b=b, j=j, hidx=hidx: e.activation(
                            out=uT[:, j, hidx * 512:(hidx + 1) * 512], in_=ps[b][:], func=AF.Copy),
                            reads=[PS[b]], writes=[("uT", j)])
                    else:
                        P.op("dve", lambda e, b=b, j=j, hidx=hidx: e.tensor_copy(
                            out=uT[:, j, hidx * 512:(hidx + 1) * 512], in_=ps[b][:]),
                            reads=[PS[b]], writes=[("uT", j)])
                next(gv0, None)
            for _ in gv0:
                pass
            dbg("uT", uT[:], [128, 8, NT], UT)

            gk(0)
            gv1 = gen_V(1)
            P.alias(["Sloc"], [("xst", 0), ("xst", 1), "normw", ("xn", 0), ("xn", 1)])
            sl(0, hook=lambda: next(gv1, None))
            for _ in gv1:
                pass
            gk(1)
            sl(1)
            dbg("KBD", KBD[:], [128, 8, 8, 128], ["KBD"])
            dbg("Sloc", Sloc[:], [128, 128, 64], ["Sloc"])
            if stop == "S":
                return

            T1 = scan_t[:, 0, :]
            if not TREE_SCAN:
                T2 = scan_t[:, 1, :]
                Uu = scan_t[:, 2, :]
                pp = [scan_t[:, 3, :], scan_t[:, 4, :]]

            def scan_pass(init, store):
                prev = init
                for c in range(128):
                    if store:
                        new = Sloc[:, c, :]
                    else:
                        new = pp[c % 2] if c < 127 else Fst
                    Bc = Sloc[:, c, :]
                    P.op(E, lambda e, prev=prev: e.tensor_tensor(out=T1, in0=CA, in1=prev, op=ALU.mult),
                         reads=["scanS", "sc"], writes=["scanT1"])
                    P.op(E, lambda e, prev=prev: e.tensor_tensor(out=T2[:, 0:32], in0=a8i, in1=prev[:, 32:64], op=ALU.mult),
                         reads=["scanS", "sc"], writes=["scanT2"])
                    P.op(E, lambda e, prev=prev: e.tensor_tensor(out=T2[:, 32:64], in0=a8i, in1=prev[:, 0:32], op=ALU.mult),
                         reads=["scanS", "sc"], writes=["scanT2"])
                    P.op(E, lambda e, Bc=Bc: e.tensor_tensor(out=Uu, in0=T1, in1=Bc, op=ALU.add),
                         reads=["scanT1", "Sloc"], writes=["scanU"])
                    wr_ = ["scanS", "Sloc"] if store else ["scanS"]
                    P.op(E, lambda e, new=new: e.tensor_tensor(out=new[:, 0:32], in0=Uu[:, 0:32], in1=T2[:, 0:32],
                                                               op=ALU.subtract), reads=["scanU", "scanT2"], writes=wr_)
                    P.op(E, lambda e, new=new: e.tensor_tensor(out=new[:, 32:64], in0=Uu[:, 32:64], in1=T2[:, 32:64],
                                                               op=ALU.add), reads=["scanU", "scanT2"], writes=wr_)
                    prev = new
                    yield

            def carry():
                P.dma("pool", cc_in, Fst, reads=["scanS"], writes=["cc_in"])
                if USE_CC:
                    P.custom("pool", lambda e: e.collective_compute(
                        "AllGather", ALU.bypass, replica_groups=[[0, 1, 2, 3], [4, 5, 6, 7]], ins=[cc_in], outs=[cc_out]),
                        N_DMA_SEMS, reads=["cc_in"], writes=["cc_out"])
                else:
                    for r_ in range(4):
                        P.dma("pool", cc_out[r_ * 128:(r_ + 1) * 128, :], cc_in, reads=["cc_in"], writes=["cc_out"])
                P.dma("pool", Gall[:], cc_out.rearrange("(r p) f -> p r f", p=128), reads=["cc_out"], writes=["Gall"])
                AKr = ss[:, 32, :]
                AKi = ss[:, 33, :]
                P.op(E, lambda e: e.tensor_copy(out=AKr, in_=pw_re[:, 8, :]), reads=["ss"], writes=["sc"])
                P.op(E, lambda e: e.tensor_copy(out=AKi, in_=pw_im[:, 8, :]), reads=["ss"], writes=["sc"])
                for _ in range(7):
                    cmul(ss[:, 34, :], ss[:, 35, :], AKr, AKi, AKr, AKi, eng=E, tok="sc", t0=21)
                    P.op(E, lambda e: e.tensor_copy(out=AKr, in_=ss[:, 34, :]), reads=["sc"], writes=["sc"])
                    P.op(E, lambda e: e.tensor_copy(out=AKi, in_=ss[:, 35, :]), reads=["sc"], writes=["sc"])
                Acc = ss[:, 38:40, :].rearrange("p a b -> p (a b)")
                P.op(E, lambda e: e.memset(Acc, 0.0), reads=["sc"], writes=["sc"])
                for d in (3, 2, 1):
                    if d != 3:
                        cmul(ss[:, 34, :], ss[:, 35, :], ss[:, 38, :], ss[:, 39, :], AKr, AKi, eng=E, tok="sc", t0=21)
                        P.op(E, lambda e: e.tensor_copy(out=ss[:, 38, :], in_=ss[:, 34, :]), reads=["sc"], writes=["sc"])
                        P.op(E, lambda e: e.tensor_copy(out=ss[:, 39, :], in_=ss[:, 35, :]), reads=["sc"], writes=["sc"])
                    for r in range(4):
                        P.op(E, lambda e, r=r, d=d: e.tensor_scalar(
                            out=T1, in0=Gall[:, r, :], scalar1=sel_sb[:, (d - 1) * 4 + r:(d - 1) * 4 + r + 1], scalar2=None,
                            op0=ALU.mult), reads=["sc", "Gall", "sel", "scanT1", "scanU"], writes=["scanT1"])
                        P.op(E, lambda e: e.tensor_tensor(out=Acc, in0=Acc, in1=T1, op=ALU.add),
                             reads=["sc", "scanT1"], writes=["sc"])
                P.op(E, lambda e: e.tensor_copy(out=S0, in_=Acc), reads=["sc"], writes=["scanS", "S0"])

            ctab = ss[:, 0:21, :]

            def tree_coefs():
                cr, ci = ss[:, 36, :], ss[:, 37, :]
                P.op(E, lambda e: e.tensor_copy(out=cr, in_=pw_re[:, 8, :]), reads=["ss", "sc"], writes=["sc"])
                P.op(E, lambda e: e.tensor_copy(out=ci, in_=pw_im[:, 8, :]), reads=["ss", "sc"], writes=["sc"])
                for d in range(7):
                    for rr in (0, 1):
                        P.op(E, lambda e, d=d, rr=rr: e.tensor_copy(out=ss[:, 3 * d + rr, :], in_=cr), reads=["sc"], writes=["sc"])
                    P.op(E, lambda e, d=d: e.tensor_copy(out=ss[:, 3 * d + 2, :], in_=ci), reads=["sc"], writes=["sc"])
                    if d < 6:
                        cmul(ss[:, 34, :], ss[:, 35, :], cr, ci, cr, ci, eng=E, tok="sc", t0=21)
                        P.op(E, lambda e: e.tensor_copy(out=cr, in_=ss[:, 34, :]), reads=["sc"], writes=["sc"])
                        P.op(E, lambda e: e.tensor_copy(out=ci, in_=ss[:, 35, :]), reads=["sc"], writes=["sc"])

            def tree_level(d, down):
                span = 2 ** (d + 1)
                nk_tot = 128 // span
                v = Sloc[:].rearrange("p (k s) f -> p k s f", s=span)
                CAd = ss[:, 3 * d:3 * d + 2, :].rearrange("p a b -> p (a b)")
                aid = ss[:, 3 * d + 2, :]
                for k0 in range(0, nk_tot, NKMAX):
                    nk = min(NKMAX, nk_tot - k0)
                    L = v[:, k0:k0 + nk, span // 2 - 1, :]
                    R = v[:, k0:k0 + nk, span - 1, :]
                    tv = [tree_t[:, i, 0:nk * 64].rearrange("p (k f) -> p k f", f=64) for i in range(3)]
                    CAb = CAd.unsqueeze(1).to_broadcast([128, nk, 64])
                    aib = aid.unsqueeze(1).to_broadcast([128, nk, 32])
                    src = R if down else L
                    oth = L if down else R
                    P.op(E, lambda e, src=src, CAb=CAb, tv=tv: e.tensor_tensor(out=tv[0], in0=src, in1=CAb, op=ALU.mult),
                         reads=["Sloc", "sc"], writes=["tr0"])
                    P.op(E, lambda e, src=src, aib=aib, tv=tv: e.tensor_tensor(out=tv[1][:, :, 0:32], in0=src[:, :, 32:64], in1=aib,
                                                                             op=ALU.mult), reads=["Sloc", "sc"], writes=["tr1"])
                    P.op(E, lambda e, src=src, aib=aib, tv=tv: e.tensor_tensor(out=tv[1][:, :, 32:64], in0=src[:, :, 0:32], in1=aib,
                                                                             op=ALU.mult), reads=["Sloc", "sc"], writes=["tr1"])
                    P.op(E, lambda e, oth=oth, tv=tv: e.tensor_tensor(out=tv[2], in0=tv[0], in1=oth, op=ALU.add),
                         reads=["tr0", "Sloc"], writes=["tr2"])
                    if down:
                        P.op(E, lambda e, L=L, R=R: e.tensor_copy(out=L, in_=R), reads=["Sloc", "tr2"], writes=["Sloc"])
                    P.op(E, lambda e, R=R, tv=tv: e.tensor_tensor(out=R[:, :, 0:32], in0=tv[2][:, :, 0:32], in1=tv[1][:, :, 0:32],
                                                                  op=ALU.subtract), reads=["tr2", "tr1"], writes=["Sloc"])
                    P.op(E, lambda e, R=R, tv=tv: e.tensor_tensor(out=R[:, :, 32:64], in0=tv[2][:, :, 32:64], in1=tv[1][:, :, 32:64],
                                                                  op=ALU.add), reads=["tr2", "tr1"], writes=["Sloc", "scanS"])
                    yield

            def tree_scan():
                tree_coefs()
                yield
                for d in range(7):
                    yield from tree_level(d, False)
                P.op(E, lambda e: e.tensor_copy(out=Fst, in_=Sloc[:, 127, :]), reads=["Sloc"], writes=["scanS"])
                carry()
                yield
                P.op(E, lambda e: e.tensor_copy(out=Sloc[:, 127, :], in_=S0), reads=["S0", "scanS"], writes=["Sloc"])
                for d in range(6, -1, -1):
                    yield from tree_level(d, True)

            def scan_all():
                if TREE_SCAN:
                    yield from tree_scan()
                    return
                yield from scan_pass(Z0, False)
                carry()
                yield
                yield from scan_pass(S0, True)

            scan_gen = scan_all()
            scan_done = {"d": False}

            def pump(n):
                import os
                if scan_done["d"] or (os.environ.get("NO_PUMP") and n < 1000):
                    return
                for _ in range(n):
                    try:
                        next(scan_gen)
                    except StopIteration:
                        scan_done["d"] = True
                        return

            P.alias(["sgate", "vaug", "kT"], [("V", k) for k in range(8)] + ["GinT"])
            P.alias(["qT"] + [("pT", a, b) for a in range(2) for b in range(2)], ["BT_f", "CT_f", "CTbf"])
            qn_state = {"i": 0}
            sq2 = tmpB[:].bitcast(BF16)

            def qk_norm_group(items, wcol, wtok, dtok):
                ctx = []
                for (b, dst, ntok) in items:
                    i = qn_state["i"] % 2
                    qn_state["i"] += 1
                    qf = tmpA[:, i * 1024:i * 1024 + 512]
                    rs = tmpA[:, i * 1024 + 512:i * 1024 + 1024]
                    TQ, TR = "tA%d" % (2 * i), "tA%d" % (2 * i + 1)
                    sq = sq_bf[:, 0:ntok] if i == 0 else sq2[:, 0:ntok]
                    TS = "sq" if i == 0 else "tmpB"
                    ctx.append((b, dst, ntok, qf, rs, TQ, TR, sq, TS))
                nbs = []
                for (b, dst, ntok, qf, rs, TQ, TR, sq, TS) in ctx:
                    P.op("act", lambda e, sq=sq, b=b, ntok=ntok: e.activation(out=sq, in_=ps[b][:, 0:ntok], func=AF.Square),
                         reads=[PS[b]], writes=[TS])
                    P.op("dve", lambda e, qf=qf, b=b, ntok=ntok: e.tensor_copy(out=qf[:, 0:ntok], in_=ps[b][:, 0:ntok]),
                         reads=[PS[b]], writes=[TQ])
                for (b, dst, ntok, qf, rs, TQ, TR, sq, TS) in ctx:
                    nb = next_banks(1)[0]
                    nbs.append(nb)
                    P.op("pe", lambda e, nb=nb, sq=sq, ntok=ntok: e.matmul(ps[nb][:, 0:ntok], lhsT=ones_blk[:], rhs=sq,
                                                                          start=True, stop=True),
                         reads=[TS, "ones_blk"], writes=[PS[nb]])
                for (b, dst, ntok, qf, rs, TQ, TR, sq, TS), nb in zip(ctx, nbs):
                    P.op("act", lambda e, rs=rs, nb=nb, ntok=ntok: e.activation(
                        out=rs[:, 0:ntok], in_=ps[nb][:, 0:ntok], func=AF.Sqrt, scale=1.0 / 64, bias=eps_c),
                        reads=[PS[nb], "eps"], writes=[TR])
                for (b, dst, ntok, qf, rs, TQ, TR, sq, TS) in ctx:
                    P.op("dve", lambda e, rs=rs, ntok=ntok: e.reciprocal(out=rs[:, 0:ntok], in_=rs[:, 0:ntok]),
                         reads=[TR], writes=[TR])
                for (b, dst, ntok, qf, rs, TQ, TR, sq, TS) in ctx:
                    P.op("dve", lambda e, dst=dst, qf=qf, rs=rs, ntok=ntok: e.scalar_tensor_tensor(
                        out=dst, in0=qf[:, 0:ntok], scalar=wcol, in1=rs[:, 0:ntok], op0=ALU.mult, op1=ALU.mult),
                        reads=[TQ, TR, wtok], writes=[dtok])

            for c in range(8):
                banks = next_banks(2)
                proj_fm(CQ + c, banks, MAIN)
                qk_norm_group([(b, qT[:, c, hidx * 512:(hidx + 1) * 512], 512) for hidx, b in enumerate(banks)],
                              qw8, "qw8", "qT")
                pump(PUMP_C)
            KSP = [(0, 512), (512, 512), (1024, 128)]
            for c in range(2):
                banks = next_banks(3)
                proj_fm(CK + c, banks, KSP)
                its = [(b, kT[:, c, t0:t0 + n], n) for (t0, n), b in zip(KSP, banks)]
                qk_norm_group(its[0:2], qkw[:, 1:2], "qkw", "kT")
                qk_norm_group(its[2:3], qkw[:, 1:2], "qkw", "kT")
                pump(PUMP_C)
            for c in range(8):
                banks = next_banks(2)
                proj_fm(CG + c, banks, MAIN)
                for hidx, b in enumerate(banks):
                    P.op("act", lambda e, b=b, c=c, hidx=hidx: e.activation(
                        out=sgate[:, c, hidx * 512:(hidx + 1) * 512], in_=ps[b][:], func=AF.Silu),
                        reads=[PS[b]], writes=["sgate"])
                pump(PUMP_C)
            P.op("dve", lambda e: e.memset(vaug.rearrange("p a b c -> p (a b c)"), 1.0), writes=["vaug"])
            for c in range(2):
                wt, wtok = load_w(win_chunk(CV + c))
                for blk in range(9):
                    b = next_banks(1)[0]
                    for kt in range(KT):
                        P.op("pe", lambda e, b=b, kt=kt, blk=blk, wt=wt: e.matmul(
                            ps[b][:, 0:128], lhsT=hT[:, kt, blk * 128:(blk + 1) * 128], rhs=wt[:, kt, :],
                            start=(kt == 0), stop=(kt == KT - 1)), reads=[wtok, "hT"], writes=[PS[b]])
                    P.op("dve", lambda e, b=b, blk=blk, c=c: e.tensor_copy(
                        out=vaug[:, blk, 2 * c, 0:64], in_=ps[b][:, 0:64]), reads=[PS[b]], writes=["vaug"])
                    P.op("act", lambda e, b=b, blk=blk, c=c: e.activation(
                        out=vaug[:, blk, 2 * c + 1, 64:128], in_=ps[b][:, 64:128], func=AF.Copy),
                        reads=[PS[b]], writes=["vaug"])
                pump(PUMP_C)
            dbg("qT", qT, [128, 8, NT], ["qT"])
            dbg("kT", kT, [128, 2, NH], ["kT"])
            dbg("vaug", vaug, [128, 9, 4, 128], ["vaug"])
            if stop == "C":
                return

            rden = tmpA[:, 1024:1536]
            onum = tmpA[:, 1536:2048]
            AG = [("ag", n) for n in range(8)]
            def att_qk(n, g):
                hf = g % 2
                rows = slice(64 * hf, 64 * hf + 64)
                drows = slice(64 * (1 - hf), 64 * (1 - hf) + 64)
                cbase = (g // 2) * 4
                kc = g // 2
                bS = next_banks(2)
                pb = (n * 4 + g) % 2
                qrhs = qT[rows, cbase:cbase + 4, n * 128:(n + 1) * 128]
                for kk, bb in enumerate(bS):
                    kblk = n + kk
                    mi = (2 if n == 0 else 0) if kk == 0 else 1
                    P.op("pe", lambda e, bb=bb, kblk=kblk, qrhs=qrhs, rows=rows, kc=kc: e.matmul(
                        ps[bb][:].rearrange("p (a b) -> p a b", a=4), lhsT=kT[rows, kc, kblk * 128:(kblk + 1) * 128],
                        rhs=qrhs, start=True, stop=False), reads=["qT", "kT"], writes=[PS[bb]])
                    P.op("pe", lambda e, bb=bb, mi=mi: e.matmul(
                        ps[bb][:].rearrange("p (a b) -> p a b", a=4), lhsT=ident_bf[:],
                        rhs=masks[:, mi, :].unsqueeze(1).to_broadcast([128, 4, 128]), start=False, stop=True),
                        reads=["masks", "ident_bf"], writes=[PS[bb]])
                    P.op("act", lambda e, bb=bb, kk=kk, pb=pb: e.activation(out=pT[:, pb, kk, :], in_=ps[bb][:], func=AF.Exp),
                         reads=[PS[bb]], writes=[("pT", pb, kk)])
                return (hf, rows, drows, cbase, pb)

            def att_rest(n, g, ctx):
                hf, rows, drows, cbase, pb = ctx
                bo = next_banks(1)[0]
                for kk in range(2):
                    kblk = n + kk
                    P.op("pe", lambda e, kk=kk, kblk=kblk, bo=bo, g=g, pb=pb: e.matmul(
                        ps[bo][:], lhsT=vaug[:, kblk, g, :], rhs=pT[:, pb, kk, :], start=(kk == 0), stop=False),
                        reads=[("pT", pb, kk), "vaug"], writes=[PS[bo]])
                P.op("pe", lambda e, bo=bo, g=g, hf=hf: e.matmul(
                    ps[bo][:].rearrange("p (a b) -> p a b", a=4), lhsT=dsel[0:1, hf, :],
                    rhs=eskb[0:1, 4 * g:4 * g + 4].unsqueeze(2).to_broadcast([1, 4, 128]), start=False, stop=True),
                    reads=["eskb", "dsel"], writes=[PS[bo]])
                P.op("dve", lambda e, bo=bo, drows=drows: e.reciprocal(out=rden[drows, :], in_=ps[bo][drows, :]),
                     reads=[PS[bo]], writes=["tA2"])
                P.op("dve", lambda e, bo=bo, rows=rows, drows=drows: e.tensor_tensor(
                    out=onum[rows, :], in0=ps[bo][rows, :], in1=rden[drows, :], op=ALU.mult),
                    reads=[PS[bo], "tA2"], writes=["tA3"])
                P.op("dve", lambda e, rows=rows, cbase=cbase, n=n: e.tensor_tensor(
                    out=qT[rows, cbase:cbase + 4, n * 128:(n + 1) * 128],
                    in0=onum[rows, :].rearrange("p (a b) -> p a b", a=4),
                    in1=sgate[rows, cbase:cbase + 4, n * 128:(n + 1) * 128], op=ALU.mult),
                    reads=["tA3", "sgate"], writes=[("ag", n)])

            its = [(n, g) for n in range(8) for g in range(4)]
            ctxs = {0: att_qk(*its[0])}
            for i, (n, g) in enumerate(its):
                if i + 1 < len(its):
                    ctxs[i + 1] = att_qk(*its[i + 1])
                att_rest(n, g, ctxs.pop(i))
                pump(PUMP_T)
            agT = qT
            dbg("ag", qT, [128, 8, NT], AG)
            if stop == "T":
                return

            w_ap_v = w_ap_d.rearrange("(kt p) c -> p kt c", p=128)
            w_glu_v = w_glu_d.rearrange("(kt p) c -> p kt c", p=128)
            w_sp_v = w_sp_d.rearrange("(kt p) c -> p kt c", p=128)
            P.alias(["Pa"], ["sgate", "vaug", "kT"])

            def proj8(wsrc, act_T, act_toks, banks):
                wt, wtok = load_w(wsrc)
                for kt in range(8):
                    for hidx, b in enumerate(banks):
                        P.op("pe", lambda e, b=b, kt=kt, hidx=hidx: e.matmul(
                            ps[b][:], lhsT=wt[:, kt, :], rhs=act_T[:, kt, hidx * 512:(hidx + 1) * 512],
                            start=(kt == 0), stop=(kt == 7)), reads=[wtok] + act_toks, writes=[PS[b]])

            for c in range(16):
                by = next_banks(2)
                proj8(w_ap_v[:, :, c * 128:(c + 1) * 128], agT, AG, by)
                bg = next_banks(2)
                proj_fm(CGA + c, bg, MAIN)
                for hidx in range(2):
                    sg = tmpA[:, hidx * 512:(hidx + 1) * 512]
                    P.op("act", lambda e, hidx=hidx, sg=sg, bg=bg: e.activation(out=sg, in_=ps[bg[hidx]][:], func=AF.Sigmoid),
                         reads=[PS[bg[hidx]], "tA0", "tA1"], writes=["tA%d" % hidx])
                    P.op("dve", lambda e, hidx=hidx, sg=sg, c=c, by=by: e.tensor_tensor(
                        out=Pa[:, c, hidx * 512:(hidx + 1) * 512], in0=ps[by[hidx]][:], in1=sg, op=ALU.mult),
                        reads=[PS[by[hidx]], "tA%d" % hidx], writes=["Pa"])
                pump(6)
            pump(100000)
            dbg("Pa", Pa, [128, 16, NT], ["Pa"])
            if stop == "D":
                return

            P.alias(["Sin_bf", ("Hout", 0), ("Hout", 1)], ["qT"] + AG + [("pT", a, b) for a in range(2) for b in range(2)])
            if TREE_SCAN:
                P.op("dve", lambda e: e.tensor_copy(out=Sin_bf, in_=Sloc[:].rearrange("p c (r q) -> p r q c", r=2)),
                     reads=["Sloc", "scanS"], writes=["Sin_bf"])
            else:
                P.op("dve", lambda e: e.tensor_copy(out=Sin_bf[:, :, :, 0], in_=S0.rearrange("p (r q) -> p r q", r=2)),
                     reads=["S0"], writes=["Sin_bf"])
                P.op("dve", lambda e: e.tensor_copy(out=Sin_bf[:, :, :, 1:128],
                                                    in_=Sloc[:, 0:127, :].rearrange("p c (r q) -> p r q c", r=2)),
                     reads=["Sloc", "scanS"], writes=["Sin_bf"])
            dbg("Sin", Sin_bf, [128, 2, 32, 128], ["Sin_bf"])
            P.alias(["sz", "yssm"], ["Sloc", "scanS"])

            for j in range(8):
                banks = next_banks(2)
                proj_fm(CZ + j, banks, MAIN)
                for hidx, b in enumerate(banks):
                    P.op("act", lambda e, b=b, j=j, hidx=hidx: e.activation(
                        out=sz[:, j, hidx * 512:(hidx + 1) * 512], in_=ps[b][:], func=AF.Silu),
                        reads=[PS[b]], writes=["sz"])

            GC = math.sqrt(2.0 / math.pi)
            ctj = tmpB[:, 0:256].rearrange("p (r a b) -> p r a b", r=2, a=4)
            h1 = tmpB[:, 256:384].rearrange("p (a b) -> p a b", a=4)
            h2 = tmpB[:, 384:512].rearrange("p (a b) -> p a b", a=4)
            def hout_gen(j):
                hb = j % 2
                Hj = Hout[:, hb]
                P.dma("sp", ctj[:, 0], CTre_d[:, 4 * j:4 * j + 4, :], writes=["tmpB"])
                P.dma("sp", ctj[:, 1], CTim_d[:, 4 * j:4 * j + 4, :], writes=["tmpB"])
                ar_ = pw_re[:, 1:9, 4 * j:4 * j + 4].rearrange("p t q -> p q t").unsqueeze(3).to_broadcast([128, 4, 8, 32])
                ai_ = pw_im[:, 1:9, 4 * j:4 * j + 4].rearrange("p t q -> p q t").unsqueeze(3).to_broadcast([128, 4, 8, 32])
                cr_ = ctj[:, 0].unsqueeze(2).to_broadcast([128, 4, 8, 32])
                ci_ = ctj[:, 1].unsqueeze(2).to_broadcast([128, 4, 8, 32])
                h1 = wstf[1][:, 0:1024].rearrange("p (q t h) -> p q t h", q=4, t=8)
                h2 = wstf[1][:, 1024:2048].rearrange("p (q t h) -> p q t h", q=4, t=8)
                TK1, TK2 = ("wst", 1), ("wst", 1)
                HE = "dve"
                P.op(HE, lambda e: e.tensor_tensor(out=h1, in0=cr_, in1=ar_, op=ALU.mult), reads=["tmpB", "ss"], writes=[TK1])
                P.op(HE, lambda e: e.tensor_tensor(out=h2, in0=ci_, in1=ai_, op=ALU.mult), reads=["tmpB", "ss"], writes=[TK2])
                P.op(HE, lambda e: e.tensor_tensor(out=Hj[:, :, :, 0, :], in0=h1, in1=h2, op=ALU.subtract),
                     reads=[TK1, TK2], writes=[("Hout", hb)])
                P.op(HE, lambda e: e.tensor_tensor(out=h1, in0=cr_, in1=ai_, op=ALU.mult), reads=["tmpB", "ss"], writes=[TK1])
                P.op(HE, lambda e: e.tensor_tensor(out=h2, in0=ci_, in1=ar_, op=ALU.mult), reads=["tmpB", "ss"], writes=[TK2])
                P.op(HE, lambda e: e.tensor_tensor(out=h1, in0=h1, in1=h2, op=ALU.add), reads=[TK1, TK2], writes=[TK1])
                P.op("act", lambda e: e.activation(out=Hj[:, :, :, 1, :], in_=h1, func=AF.Copy, scale=-1.0),
                     reads=[TK1], writes=[("Hout", hb)])

            def ssm_mm(j):
                hb = j % 2
                Hj = Hout[:, hb]
                yb = next_banks(2)
                for tau in range(8):
                    b = yb[tau // 4]
                    reg = tau % 4
                    for tp in range(tau + 1):
                        P.op("pe", lambda e, b=b, reg=reg, tau=tau, tp=tp, j=j: e.matmul(
                            ps[b][:, reg * 128:(reg + 1) * 128], lhsT=KBD[:, j, tau - tp, :], rhs=uTv[:, j, tp, :],
                            start=(tp == 0), stop=False), reads=["KBD", ("uT", j)], writes=[PS[b]])
                    for Q in range(4):
                        for ri in range(2):
                            P.op("pe", lambda e, b=b, reg=reg, tau=tau, Q=Q, ri=ri, Hj=Hj, j=j: e.matmul(
                                ps[b][32 * Q:32 * Q + 32, reg * 128:(reg + 1) * 128], lhsT=Hj[:, Q, tau, ri, :],
                                rhs=Sin_bf[:, ri, 4 * j + Q, :], start=False, stop=False, tile_position=(0, 32 * Q)),
                                reads=[("Hout", hb), "Sin_bf"], writes=[PS[b]])
                    P.op("pe", lambda e, b=b, reg=reg: e.matmul(
                        ps[b][:, reg * 128:(reg + 1) * 128], lhsT=zeros_bf[:], rhs=zeros_bf[:], start=False, stop=True),
                        reads=["zeros_bf"], writes=[PS[b]])
                return yb

            def ssm_gelu(j, yb):
                xs, ts, TXs, TTs, pins = [], [], [], [], []
                for half2, b in enumerate(yb):
                    xs.append(tmpA[:, half2 * 1024:half2 * 1024 + 512])
                    ts.append(tmpA[:, half2 * 1024 + 512:half2 * 1024 + 1024])
                    TXs.append("tA%d" % (2 * half2))
                    TTs.append("tA%d" % (2 * half2 + 1))
                    pins.append(ps[b][:].rearrange("p (t c) -> p t c", t=4))
                for h, b in enumerate(yb):
                    P.op("act", lambda e, h=h: e.activation(
                        out=xs[h].rearrange("p (c t) -> p t c", t=4), in_=pins[h], func=AF.Copy), reads=[PS[b]], writes=[TXs[h]])
                    P.op("act", lambda e, h=h: e.activation(
                        out=ts[h].rearrange("p (c t) -> p t c", t=4), in_=pins[h], func=AF.Square, scale=math.sqrt(0.044715)),
                        reads=[PS[b]], writes=[TTs[h]])
                for h in range(2):
                    P.op("dve", lambda e, h=h: e.scalar_tensor_tensor(
                        out=ts[h], in0=ts[h], scalar=1.0, in1=xs[h], op0=ALU.add, op1=ALU.mult),
                        reads=[TTs[h], TXs[h]], writes=[TTs[h]])
                for h in range(2):
                    P.op("act", lambda e, h=h: e.activation(out=ts[h], in_=ts[h], func=AF.Sigmoid, scale=2.0 * GC),
                         reads=[TTs[h]], writes=[TTs[h]])
                for h in range(2):
                    P.op("dve", lambda e, j=j, h=h: e.tensor_tensor(
                        out=yssm[:, j, :].rearrange("p (c h t) -> p c h t", h=2, t=4)[:, :, h, :],
                        in0=xs[h].rearrange("p (c t) -> p c t", t=4), in1=ts[h].rearrange("p (c t) -> p c t", t=4), op=ALU.mult),
                        reads=[TXs[h], TTs[h]], writes=["yssm"])

            hout_gen(0)
            ybs = {}
            for j in range(8):
                ybs[j] = ssm_mm(j)
                if j + 1 < 8:
                    hout_gen(j + 1)
                if j >= 1:
                    ssm_gelu(j - 1, ybs.pop(j - 1))
            ssm_gelu(7, ybs.pop(7))
            dbg("yssm", yssm, [128, 8, NT], ["yssm"])
            if stop == "Y":
                return

            P.alias(["vglu"], ["KBD"])
            for c in range(8):
                ba = next_banks(2)
                proj8(w_glu_v[:, :, c * 128:(c + 1) * 128], yssm, ["yssm"], ba)
                bb2 = next_banks(2)
                proj8(w_glu_v[:, :, (8 + c) * 128:(9 + c) * 128], yssm, ["yssm"], bb2)
                for hidx in range(2):
                    sgl = tmpA[:, hidx * 512:(hidx + 1) * 512]
                    tg = tmpA[:, 1024 + hidx * 512:1024 + (hidx + 1) * 512]
                    P.op("act", lambda e, hidx=hidx, sgl=sgl, c=c, bb2=bb2: e.activation(
                        out=sgl, in_=ps[bb2[hidx]][:], func=AF.Sigmoid, bias=bglu[:, 8 + c:9 + c]),
                        reads=[PS[bb2[hidx]], "bglu", "tA0", "tA1"], writes=["tA%d" % hidx])
                    P.op("dve", lambda e, hidx=hidx, sgl=sgl, tg=tg, c=c, ba=ba: e.scalar_tensor_tensor(
                        out=tg, in0=ps[ba[hidx]][:], scalar=bglu[:, c:c + 1], in1=sgl, op0=ALU.add, op1=ALU.mult),
                        reads=[PS[ba[hidx]], "tA%d" % hidx, "bglu", "tA2", "tA3"], writes=["tA%d" % (2 + hidx)])
                    P.op("dve", lambda e, hidx=hidx, tg=tg, c=c: e.tensor_tensor(
                        out=vglu[:, c, hidx * 512:(hidx + 1) * 512], in0=tg, in1=sz[:, c, hidx * 512:(hidx + 1) * 512],
                        op=ALU.mult), reads=["tA%d" % (2 + hidx), "sz"], writes=["vglu"])
            dbg("vglu", vglu, [128, 8, NT], ["vglu"])

            for c in range(16):
                by = next_banks(2)
                proj8(w_sp_v[:, :, c * 128:(c + 1) * 128], vglu, ["vglu"], by)
                bg = next_banks(2)
                proj_fm(CGS + c, bg, MAIN)
                for hidx in range(2):
                    sg = tmpA[:, hidx * 512:(hidx + 1) * 512]
                    tg = tmpA[:, 1024 + hidx * 512:1024 + (hidx + 1) * 512]
                    P.op("act", lambda e, hidx=hidx, sg=sg, bg=bg: e.activation(out=sg, in_=ps[bg[hidx]][:], func=AF.Sigmoid),
                         reads=[PS[bg[hidx]]], writes=["tA%d" % hidx])
                    P.op("dve", lambda e, hidx=hidx, sg=sg, tg=tg, by=by: e.tensor_tensor(
                        out=tg, in0=ps[by[hidx]][:], in1=sg, op=ALU.mult),
                        reads=[PS[by[hidx]], "tA%d" % hidx], writes=["tA%d" % (2 + hidx)])
                    P.op("dve", lambda e, hidx=hidx, tg=tg, c=c: e.tensor_tensor(
                        out=Pa[:, c, hidx * 512:(hidx + 1) * 512], in0=tg, in1=Pa[:, c, hidx * 512:(hidx + 1) * 512],
                        op=ALU.add), reads=["tA%d" % (2 + hidx), "Pa"], writes=["Pa"])
            dbg("merged", Pa, [128, 16, NT], ["Pa"])

            w_out_v = w_out_d.rearrange("(kt p) c -> p kt c", p=128)
            P.alias([("woutbf", 0), ("woutbf", 1)], ["hT"])
            P.alias([("xres", 0), ("xres", 1), ("ores", 0), ("ores", 1)], UT)
            def load_wout(q):
                wb = q % 2
                for k4 in range(4):
                    sl = state["w"] % 2
                    state["w"] += 1
                    P.dma("sp", wst4[sl], w_out_v[:, k4 * 4:k4 * 4 + 4, q * 512:(q + 1) * 512], writes=[("wst", sl)])
                    P.op("act", lambda e, sl=sl, wb=wb, k4=k4: e.activation(
                        out=woutbf[:, wb, k4 * 4:k4 * 4 + 4, :], in_=wst4[sl], func=AF.Copy),
                        reads=[("wst", sl)], writes=[("woutbf", wb)])

            load_wout(0)
            for q in range(4):
                wb = q % 2
                if q + 1 < 4:
                    load_wout(q + 1)
                for blk in range(8):
                    b = next_banks(1)[0]
                    for kt in range(KT):
                        P.op("pe", lambda e, b=b, kt=kt, blk=blk, wb=wb: e.matmul(
                            ps[b][:], lhsT=Pa[:, kt, blk * 128:(blk + 1) * 128], rhs=woutbf[:, wb, kt, :],
                            start=(kt == 0), stop=(kt == KT - 1)), reads=["Pa", ("woutbf", wb)], writes=[PS[b]])
                    xb = (q * 8 + blk) % 2
                    P.dma("sp", xres[:, xb, :], x_d[128 + blk * 128:128 + (blk + 1) * 128, q * 512:(q + 1) * 512],
                          writes=[("xres", xb)])
                    P.op("dve", lambda e, b=b, xb=xb: e.tensor_tensor(out=ores[:, xb, :], in0=ps[b][:], in1=xres[:, xb, :],
                                                                      op=ALU.add),
                         reads=[PS[b], ("xres", xb)], writes=[("ores", xb)])
                    ev = P.dma("pool", out_d[blk * 128:(blk + 1) * 128, q * 512:(q + 1) * 512], ores[:, xb, :],
                               reads=[("ores", xb)], writes=[("out", q, blk)])
                    out_evs.append(ev)
        body()
        P.wait_all("sp", out_evs + list(dbg_out.values()))
        P.emit(st)
    return nc, wseq_rec


_CACHE = {}


def kernel(**inputs):
    inp = {k: np.asarray(v) for k, v in inputs.items()}
    sh = _prep_shared(inp)
    in_maps = []
    for c in range(NCORES):
        m = dict(sh)
        m.update(_prep_core(inp, c))
        in_maps.append(m)
    if "nc" not in _CACHE:
        _CACHE["nc"] = build()
    nc = _CACHE["nc"]
    res = run_bass_kernel_spmd(nc, in_maps, core_ids=list(range(NCORES)))
    out = np.zeros((2, 4096, D), np.float32)
    for c in range(NCORES):
        b, j = c // 4, c % 4
        out[b, j * NT:(j + 1) * NT] = res.results[c]["out"]
    return out
```

```python
import contextlib
import math
import numpy as np
import concourse.bass as bass
import concourse.mybir as mybir
from concourse.bass_utils import run_bass_kernel_spmd

F32 = mybir.dt.float32
BF16 = mybir.dt.bfloat16
ALU = mybir.AluOpType
AF = mybir.ActivationFunctionType

N_DMA_SEMS = 24
N_POOL_SEMS = 6
NCORES = 8
D = 2048
NT = 1024
NH = 1152
KT = 16
IN_W = 8704
EPS = 1e-6
SCAN_ENG = "pool"
TREE_SCAN = True
NKMAX = 4
PUMP_C = 0
PUMP_T = 0
CC_INC = 1
USE_CC = True
CAST_ENGS = ("act",)


class Prog:
    ENGS = ("pe", "act", "dve", "pool", "sp")

    def __init__(self, nc):
        self.nc = nc
        self.ops = {e: [] for e in self.ENGS}
        self.cnt = {e: 0 for e in self.ENGS}
        self.seen = {e: {} for e in self.ENGS}
        self.last_w = {}
        self.readers = {}
        self.dma_cnt = [0] * (N_DMA_SEMS + 1 + N_POOL_SEMS)
        self.pool_rr = 0
        self.dma_rr = 0

    def _deps(self, eng, reads, writes):
        deps = []
        for t in reads:
            w = self.last_w.get(t)
            if w is not None:
                deps.append(w)
        for t in writes:
            w = self.last_w.get(t)
            if w is not None:
                deps.append(w)
            deps.extend(self.readers.get(t, ()))
        need = {}
        for (s, v) in deps:
            if s == "pe" and eng == "pe":
                continue
            if self.seen[eng].get(s, 0) >= v:
                continue
            if need.get(s, 0) < v:
                need[s] = v
        for s, v in need.items():
            self.seen[eng][s] = v
        return list(need.items())

    def _commit(self, ev, reads, writes):
        for t in reads:
            self.readers.setdefault(t, []).append(ev)
        for t in writes:
            self.last_w[t] = ev
            self.readers[t] = []

    def op(self, eng, fn, reads=(), writes=()):
        psr = [t for t in reads if isinstance(t, str) and t.startswith("ps")]
        if psr:
            reads = [t for t in reads if t not in psr]
            writes = list(writes) + psr
        waits = self._deps(eng, reads, writes)
        self.cnt[eng] += 1
        ev = (eng, self.cnt[eng])
        self.ops[eng].append(("op", waits, fn, None))
        self._commit(ev, reads, writes)
        return ev

    def dma(self, q, out, in_, reads=(), writes=(), **kw):
        if q == "pool":
            i = N_DMA_SEMS + 1 + self.pool_rr
            self.pool_rr = (self.pool_rr + 1) % N_POOL_SEMS
        else:
            i = self.dma_rr
            self.dma_rr = (self.dma_rr + 1) % N_DMA_SEMS
        sname = ("dma", i)
        waits = self._deps(q, reads, writes)
        prev = self.dma_cnt[i]
        if prev > 0 and self.seen[q].get(sname, 0) < prev:
            waits.append((sname, prev))
            self.seen[q][sname] = prev
        self.dma_cnt[i] += 16
        ev = (sname, self.dma_cnt[i])
        self.ops[q].append(("dma", waits, (out, in_, kw), i))
        self._commit(ev, reads, writes)
        return ev

    def custom(self, eng, fn, sem_i, reads=(), writes=()):
        sname = ("dma", sem_i)
        waits = self._deps(eng, reads, writes)
        self.dma_cnt[sem_i] += CC_INC
        ev = (sname, self.dma_cnt[sem_i])
        self.ops[eng].append(("custom", waits, fn, sem_i))
        self._commit(ev, reads, writes)
        return ev

    def alias(self, new_tokens, old_tokens):
        evs = []
        for t in old_tokens:
            w = self.last_w.get(t)
            if w is not None:
                evs.append(w)
            evs.extend(self.readers.get(t, ()))
        for t in new_tokens:
            self.last_w.pop(t, None)
            self.readers[t] = list(evs)

    def wait_all(self, eng, evs):
        waits = []
        for (s, v) in evs:
            if self.seen[eng].get(s, 0) < v:
                waits.append((s, v))
                self.seen[eng][s] = v
        self.ops[eng].append(("wait", waits, None, None))

    def emit(self, st):
        nc = self.nc
        sems = {}
        for e in self.ENGS:
            sems[e] = st.enter_context(nc.semaphore("s_" + e))
        for i in range(N_DMA_SEMS + 1 + N_POOL_SEMS):
            sems[("dma", i)] = st.enter_context(nc.semaphore("s_dma%d" % i))
        block = st.enter_context(nc.Block())

        def run(engname):
            def body(eng):
                for kind, waits, payload, di in self.ops[engname]:
                    for (s, v) in waits:
                        eng.wait_ge(sems[s], v)
                    if kind == "op":
                        payload(eng).then_inc(sems[engname], 1)
                    elif kind == "dma":
                        out, in_, kw = payload
                        eng.dma_start(out=out, in_=in_, **kw).then_inc(sems[("dma", di)], 16)
                    elif kind == "custom":
                        payload(eng).then_inc(sems[("dma", di)], CC_INC)
            return body

        block.tensor(run("pe"))
        block.scalar(run("act"))
        block.vector(run("dve"))
        block.gpsimd(run("pool"))
        block.sync(run("sp"))


def _q_head_order():
    order = []
    for c in range(8):
        if c < 4:
            order += [c, 4 + c]
        else:
            order += [8 + (c - 4), 12 + (c - 4)]
    return order


def _in_col_perm():
    qh = _q_head_order()
    attn_feat = np.concatenate([np.arange(h * 64, (h + 1) * 64) for h in qh])
    cols = [attn_feat, np.arange(1024, 1536), 1536 + attn_feat, np.arange(2560, IN_W)]
    return np.concatenate(cols), attn_feat


def _prep_shared(inp):
    f = np.float32
    perm, attn_feat = _in_col_perm()
    sh = {}
    sh["w_in"] = np.ascontiguousarray(inp["w_in"][:, perm])
    sh["w_ap"] = np.ascontiguousarray(inp["w_attn_proj"][attn_feat, :])
    sh["w_glu"] = np.ascontiguousarray(inp["w_glu"])
    sh["w_sp"] = np.ascontiguousarray(inp["w_ssm_proj"])
    sh["w_out"] = np.ascontiguousarray(inp["w_out"])
    sh["normw_row"] = np.ascontiguousarray(np.broadcast_to(inp["norm_w"][None, :], (128, D))).astype(f)
    sh["qkw"] = np.stack([np.tile(inp["q_norm_w"], 2), np.tile(inp["k_norm_w"], 2)], axis=1).astype(f)
    sh["sinks_rep"] = np.ascontiguousarray(np.broadcast_to(inp["sinks"][None, :], (128, 16))).astype(f)
    sh["bglu_fm"] = np.ascontiguousarray(inp["b_glu"].reshape(16, 128).T).astype(f)
    sh["ident"] = np.eye(128, dtype=f)
    s_idx = np.arange(128)[:, None]
    i_idx = np.arange(128)[None, :]
    NEG = -30000.0
    sh["maskP"] = np.where(s_idx > i_idx, 0.0, NEG).astype(f)
    sh["maskC"] = np.where(s_idx <= i_idx, 0.0, NEG).astype(f)

    def mp(a):
        return np.ascontiguousarray(a.reshape(32, 2, 64).transpose(1, 2, 0).reshape(128, 32)).astype(f)

    sh["Are"] = mp(inp["A_re"])
    sh["Aim"] = mp(inp["A_im"])
    sh["ldt"] = mp(np.broadcast_to(inp["log_dt"][:, None], (64, 64)))

    def bt(b):
        o = np.zeros((2, 64, 32, 2, 16), f)
        b4 = b.reshape(32, 2, 64, 16)
        for gi in range(2):
            o[gi, :, :, gi, :] = b4[:, gi].transpose(1, 0, 2)
        return np.ascontiguousarray(o.reshape(128, 32, 32))

    def ct(c):
        o = np.zeros((2, 64, 32, 2, 16), f)
        c4 = c.reshape(32, 2, 16, 64)
        for gi in range(2):
            o[gi, :, :, gi, :] = c4[:, gi].transpose(2, 0, 1)
        return np.ascontiguousarray(o.reshape(128, 32, 32))

    sh["BTre"] = bt(inp["B_re"])
    sh["BTim"] = bt(inp["B_im"])
    sh["CTre"] = ct(inp["C_re"])
    sh["CTim"] = ct(inp["C_im"])
    sh["Dfm"] = np.ascontiguousarray(inp["D_skip"].reshape(8, 128).T).astype(f)
    return sh


def _prep_core(inp, c):
    f = np.float32
    b, j = c // 4, c % 4
    x = inp["x"]
    xc = np.zeros((NH, D), f)
    t0 = j * NT
    if j > 0:
        xc[:] = x[b, t0 - 128:t0 + NT]
    else:
        xc[128:] = x[b, 0:NT]
    s_idx = np.arange(128)[:, None]
    i_idx = np.arange(128)[None, :]
    maskP0 = np.where(s_idx > i_idx, 0.0, -30000.0).astype(f) if j > 0 else np.full((128, 128), -30000.0, f)
    sel = np.zeros((3, 4), f)
    for d in range(1, 4):
        if j - d >= 0:
            sel[d - 1, j - d] = 1.0
    selr = np.ascontiguousarray(np.broadcast_to(sel.reshape(1, 12), (128, 12))).astype(f)
    return {"x": xc, "maskP0": maskP0, "sel": selr}


def build(debug=(), stop=None):
    wseq = _build(debug, stop, None)[1]
    return _build(debug, stop, wseq)[0]


def _build(debug, stop, wseq):
    wseq_rec = []
    nc = bass.Bass("TRN2", target_bir_lowering=False)

    def din(name, shape):
        return nc.dram_tensor(name, list(shape), F32, kind="ExternalInput").ap()

    x_d = din("x", [NH, D])
    w_in_d = din("w_in", [D, IN_W])
    w_ap_d = din("w_ap", [1024, D])
    w_glu_d = din("w_glu", [1024, D])
    w_sp_d = din("w_sp", [1024, D])
    w_out_d = din("w_out", [D, D])
    normw_d = din("normw_row", [128, D])
    qkw_d = din("qkw", [128, 2])
    sinks_d = din("sinks_rep", [128, 16])
    bglu_d = din("bglu_fm", [128, 16])
    ident_d = din("ident", [128, 128])
    maskP_d = din("maskP", [128, 128])
    maskC_d = din("maskC", [128, 128])
    maskP0_d = din("maskP0", [128, 128])
    Are_d = din("Are", [128, 32])
    Aim_d = din("Aim", [128, 32])
    ldt_d = din("ldt", [128, 32])
    BTre_d = din("BTre", [128, 32, 32])
    BTim_d = din("BTim", [128, 32, 32])
    CTre_d = din("CTre", [128, 32, 32])
    CTim_d = din("CTim", [128, 32, 32])
    Dfm_d = din("Dfm", [128, 8])
    sel_d = din("sel", [128, 12])
    out_d = nc.dram_tensor("out", [NT, D], F32, kind="ExternalOutput").ap()
    cc_in = nc.dram_tensor("cc_in", [128, 64], F32, kind="Internal").ap()
    cc_out = nc.dram_tensor("cc_out", [4 * 128, 64], F32, kind="Internal").ap()
    dbg_out = {}

    P = Prog(nc)
    st = contextlib.ExitStack()
    with st:
        def sb(name, shape, dt=F32):
            return st.enter_context(nc.sbuf_tensor(name, list(shape), dt))

        ps = [st.enter_context(nc.psum_tensor("ps%d" % i, [128, 512], F32)) for i in range(8)]
        PS = ["ps%d" % i for i in range(8)]

        def dbg(name, ap, shape, toks):
            if name not in debug:
                return
            d = nc.dram_tensor("dbg_" + name, list(shape), ap.dtype, kind="ExternalOutput").ap()
            dbg_out[name] = P.dma("sp", d, ap, reads=toks, writes=["dbg_" + name])

        hT = sb("hT", [128, KT, NH], BF16)
        uT = sb("uT", [128, 8, NT], BF16)
        Sloc = sb("Sloc", [128, 128, 64], F32)
        R1 = sb("R1", [128, 16384], BF16)
        R2 = sb("R2", [128, 12288], BF16)
        KBD = sb("KBD", [128, 8, 8, 128], BF16)
        wstf = [sb("wst%d" % i, [128, 2048], F32) for i in range(2)]
        wbf = [sb("wbf%d" % i, [128, KT, 128], BF16) for i in range(2)]
        tmpA = sb("tmpA", [128, 2048], F32)
        tmpB = sb("tmpB", [128, 512], F32)
        consts = sb("consts", [128, 128], F32)
        ident_bf = sb("ident_bf", [128, 128], BF16)
        ident_f = sb("ident_f", [128, 128], F32)
        masks = sb("masks", [128, 3, 128], BF16)
        sq_bf = sb("sq_bf", [128, 512], BF16)
        ss = sb("ssm_small", [128, 40, 32], F32)
        pw_re = sb("pw_re", [128, 9, 32], F32)
        pw_im = sb("pw_im", [128, 9, 32], F32)
        wk_re = sb("wk_re", [128, 8, 32], F32)
        wk_im = sb("wk_im", [128, 8, 32], F32)
        scan_t = sb("scan_t", [128, 1 if TREE_SCAN else 5, 64], F32)
        tree_t = sb("tree_t", [128, 3, NKMAX * 64], F32) if TREE_SCAN else None
        Dfm = sb("Dfm_sb", [128, 8], F32)
        sel_sb = sb("sel_sb", [128, 12], F32)
        Gall = sb("Gall", [128, 4, 64], F32)
        zeros_bf = sb("zeros_bf", [128, 128], BF16)
        ones_blk = sb("ones_blk", [128, 128], BF16)
        eskb = sb("eskb", [128, 16], BF16)
        dsel = sb("dsel", [1, 2, 128], BF16)
        wst = [w[:].rearrange("p (a b) -> p a b", a=KT) for w in wstf]
        wst4 = [w[:].rearrange("p (a b) -> p a b", a=4) for w in wstf]

        qkw = consts[:, 0:2]
        qw8 = consts[:, 2:3]
        eps_c = consts[:, 3:4]
        esk = consts[:, 16:32]
        bglu = consts[:, 32:48]
        rstd_blk = consts[:, 48:64]
        ssq_blk = consts[:, 64:80]

        def view(ar, off, shape, dt=BF16):
            n = int(np.prod(shape))
            if dt == BF16:
                a = ar[:, off:off + n]
            else:
                a = ar[:, off:off + 2 * n].bitcast(F32)
            if len(shape) == 1:
                return a
            names = " ".join("d%d" % i for i in range(len(shape)))
            kw = {"d%d" % i: shape[i] for i in range(len(shape))}
            return a.rearrange("p (%s) -> p %s" % (names, names), **kw)

        xst = view(R1, 0, [2048], F32)
        normw = view(R1, 4096, [2048], F32)
        xn_bf = view(R1, 8192, [2048])
        Vre = view(R1, 0, [8, 16, 32])
        Vim = view(R1, 4096, [8, 16, 32])
        GinT = view(R1, 8192, [4, 8, 2, 128])
        sgate = view(R1, 0, [8, NT])
        vaug = view(R1, 8192, [9, 4, 128])
        kT = view(R1, 12800, [2, NH])
        Pa = view(R1, 0, [16, NT])
        BT_f = view(R2, 0, [2, 16, 32], F32)
        CT_f = view(R2, 2048, [2, 32, 32], F32)
        CTre_bf = view(R2, 6144, [32, 32])
        nCTim_bf = view(R2, 7168, [32, 32])
        qT = view(R2, 0, [8, NT])
        pT = view(R2, 8192, [2, 2, 512])
        Sin_bf = view(R2, 0, [2, 32, 128])
        Hout = view(R2, 8192, [2, 4, 8, 2, 32])
        Sloc_bf = Sloc[:].rearrange("p c f -> p (c f)").bitcast(BF16)
        sz = view(Sloc_bf, 0, [8, NT])
        Sloc_f = Sloc[:].rearrange("p c f -> p (c f)")
        xsts = [Sloc_f[:, 0:2048], Sloc_f[:, 2048:4096]]
        normw = Sloc_f[:, 4096:6144]
        xns = [Sloc_bf[:, 12288:14336], Sloc_bf[:, 14336:16384]]
        yssm = view(Sloc_bf, 8192, [8, NT])
        vglu = KBD[:].rearrange("p a b c -> p (a b c)").rearrange("p (a b) -> p a b", a=8)
        hT_flat = hT[:].rearrange("p a b -> p (a b)")
        woutbf = view(hT_flat, 0, [2, KT, 512])
        uT_flat = uT[:].rearrange("p a b -> p (a b)")
        xres = view(uT_flat, 0, [2, 512], F32)
        ores = view(uT_flat, 2048, [2, 512], F32)

        out_evs = []

        def body():
            P.dma("sp", consts[:, 0:2], qkw_d, writes=["qkw"])
            P.dma("sp", bglu, bglu_d, writes=["bglu"])
            P.dma("sp", ident_f[:], ident_d, writes=["ident_f"])
            mf = tmpA[:, 0:384].rearrange("p (a b) -> p a b", a=3)
            P.dma("sp", mf[:, 0, :], maskP_d, writes=["tA0"])
            P.dma("sp", mf[:, 1, :], maskC_d, writes=["tA0"])
            P.dma("sp", mf[:, 2, :], maskP0_d, writes=["tA0"])
            P.dma("sp", tmpB[:, 0:16], sinks_d, writes=["tmpB"])
            P.dma("sp", Dfm[:], Dfm_d, writes=["Dfm"])
            P.dma("sp", sel_sb[:], sel_d, writes=["sel"])
            P.dma("sp", normw, normw_d, writes=["normw"])
            P.op("dve", lambda e: e.tensor_copy(out=ident_bf[:], in_=ident_f[:]), reads=["ident_f"], writes=["ident_bf"])
            P.op("dve", lambda e: e.memset(zeros_bf[:], 0.0), writes=["zeros_bf"])
            P.op("dve", lambda e: e.memset(eps_c, EPS), writes=["eps"])
            P.op("dve", lambda e: e.tensor_copy(out=masks[:], in_=mf), reads=["tA0"], writes=["masks"])
            P.op("act", lambda e: e.activation(out=esk, in_=tmpB[:, 0:16], func=AF.Exp), reads=["tmpB"], writes=["esk"])
            P.op("dve", lambda e: e.tensor_scalar(out=qw8, in0=consts[:, 0:1], scalar1=0.125, scalar2=None, op0=ALU.mult),
                 reads=["qkw"], writes=["qw8"])
            P.op("dve", lambda e: e.tensor_copy(out=eskb[:], in_=esk), reads=["esk"], writes=["eskb"])
            P.op("dve", lambda e: e.memset(dsel[0:1, 0, 0:64], 0.0), writes=["dsel"])
            P.op("dve", lambda e: e.memset(dsel[0:1, 0, 64:128], 1.0), writes=["dsel"])
            P.op("dve", lambda e: e.memset(dsel[0:1, 1, 0:64], 1.0), writes=["dsel"])
            P.op("dve", lambda e: e.memset(dsel[0:1, 1, 64:128], 0.0), writes=["dsel"])
            P.op("dve", lambda e: e.memset(ones_blk[:], 0.0), writes=["ones_blk"])
            P.op("dve", lambda e: e.memset(ones_blk[0:64, 0:64], 1.0), writes=["ones_blk"])
            P.op("dve", lambda e: e.memset(ones_blk[64:128, 64:128], 1.0), writes=["ones_blk"])

            bank_rr = {"i": 0}

            def next_banks(n):
                i = bank_rr["i"]
                bank_rr["i"] = (i + n) % 8
                return [(i + k) % 8 for k in range(n)]

            state = {"w": 0}

            def _issue_w(i, src):
                sl = i % 2
                ktn = src.shape[1]
                P.dma("sp", wst[sl][:, 0:ktn, :], src, writes=[("wst", sl)])
                P.op("act", lambda e: e.activation(out=wbf[sl][:, 0:ktn, :], in_=wst[sl][:, 0:ktn, :], func=AF.Copy),
                     reads=[("wst", sl)], writes=[("wbf", sl)])

            def load_w(src):
                i = state["w"]
                state["w"] += 1
                wseq_rec.append(src)
                if wseq is None:
                    _issue_w(i, src)
                else:
                    if i == 0:
                        _issue_w(0, wseq[0])
                    if i + 1 < len(wseq):
                        _issue_w(i + 1, wseq[i + 1])
                return wbf[i % 2], ("wbf", i % 2)

            w_in_v = w_in_d.rearrange("(kt p) c -> p kt c", p=128)

            def win_chunk(ci):
                return w_in_v[:, :, ci * 128:(ci + 1) * 128]

            CQ, CK, CV, CG, CU, CZ, CGA, CGS = 0, 8, 10, 12, 20, 28, 36, 52
            MAIN = [(128, 512), (640, 512)]

            def proj_fm(ci, banks, spans):
                wt, wtok = load_w(win_chunk(ci))
                for b, (t0, n) in zip(banks, spans):
                    for kt in range(KT):
                        P.op("pe", lambda e, b=b, kt=kt, t0=t0, n=n: e.matmul(
                            ps[b][:, 0:n], lhsT=wt[:, kt, :], rhs=hT[:, kt, t0:t0 + n],
                            start=(kt == 0), stop=(kt == KT - 1)), reads=[wtok, "hT"], writes=[PS[b]])

            SE = "pool"
            (I_ARE, I_AIM, I_LDT, I_DT, I_LRE, I_LIM, I_MAG, I_COS, I_SIN, I_AR, I_AI, I_DEN, I_NR, I_T1, I_T2, I_CFR,
             I_CFI, I_T3, I_T4, I_A8I) = range(20)
            P.dma("sp", ss[:, I_ARE, :], Are_d, writes=["ss"])
            P.dma("sp", ss[:, I_AIM, :], Aim_d, writes=["ss"])
            P.dma("sp", ss[:, I_LDT, :], ldt_d, writes=["ss"])
            P.dma("sp", CT_f[:, 0], CTre_d, writes=["CT_f"])
            P.dma("sp", CT_f[:, 1], CTim_d, writes=["CT_f"])

            def tt(o, a, b, op, eng=SE):
                P.op(eng, lambda e: e.tensor_tensor(out=o, in0=a, in1=b, op=op), reads=["ss"], writes=["ss"])

            def tsc(o, a, s1, op0, s2=None, op1=None, eng=SE):
                if op1 is None:
                    P.op(eng, lambda e: e.tensor_scalar(out=o, in0=a, scalar1=s1, scalar2=None, op0=op0),
                         reads=["ss"], writes=["ss"])
                else:
                    P.op(eng, lambda e: e.tensor_scalar(out=o, in0=a, scalar1=s1, scalar2=s2, op0=op0, op1=op1),
                         reads=["ss"], writes=["ss"])

            def act(o, a, func, **kw):
                P.op("act", lambda e: e.activation(out=o, in_=a, func=func, **kw), reads=["ss"], writes=["ss"])

            S_ = lambda i: ss[:, i, :]
            act(S_(I_DT), S_(I_LDT), AF.Exp)
            tt(S_(I_LRE), S_(I_DT), S_(I_ARE), ALU.mult)
            tt(S_(I_LIM), S_(I_DT), S_(I_AIM), ALU.mult)
            act(S_(I_MAG), S_(I_LRE), AF.Exp)
            hpi_c = consts[:, 4:5]
            P.op(SE, lambda e: e.memset(hpi_c, 0.5 * math.pi), writes=["hpi"])
            act(S_(I_SIN), S_(I_LIM), AF.Sin, scale=1.0 / 16)
            P.op("act", lambda e: e.activation(out=S_(I_COS), in_=S_(I_LIM), func=AF.Sin, scale=1.0 / 16, bias=hpi_c),
                 reads=["ss", "hpi"], writes=["ss"])
            for _ in range(4):
                tt(S_(I_T1), S_(I_COS), S_(I_COS), ALU.mult)
                tt(S_(I_T2), S_(I_SIN), S_(I_SIN), ALU.mult)
                tt(S_(I_T3), S_(I_COS), S_(I_SIN), ALU.mult)
                tt(S_(I_COS), S_(I_T1), S_(I_T2), ALU.subtract)
                tsc(S_(I_SIN), S_(I_T3), 2.0, ALU.mult)
            tt(S_(I_AR), S_(I_MAG), S_(I_COS), ALU.mult)
            tt(S_(I_AI), S_(I_MAG), S_(I_SIN), ALU.mult)
            tt(S_(I_T1), S_(I_ARE), S_(I_ARE), ALU.mult)
            tt(S_(I_T2), S_(I_AIM), S_(I_AIM), ALU.mult)
            tt(S_(I_DEN), S_(I_T1), S_(I_T2), ALU.add)
            P.op("dve", lambda e: e.reciprocal(out=S_(I_DEN), in_=S_(I_DEN)), reads=["ss"], writes=["ss"])
            tsc(S_(I_NR), S_(I_AR), -1.0, ALU.add)
            tt(S_(I_T1), S_(I_NR), S_(I_ARE), ALU.mult)
            tt(S_(I_T2), S_(I_AI), S_(I_AIM), ALU.mult)
            tt(S_(I_T1), S_(I_T1), S_(I_T2), ALU.add)
            tt(S_(I_CFR), S_(I_T1), S_(I_DEN), ALU.mult)
            tt(S_(I_T1), S_(I_AI), S_(I_ARE), ALU.mult)
            tt(S_(I_T2), S_(I_NR), S_(I_AIM), ALU.mult)
            tt(S_(I_T1), S_(I_T1), S_(I_T2), ALU.subtract)
            tt(S_(I_CFI), S_(I_T1), S_(I_DEN), ALU.mult)
            P.op(SE, lambda e: e.memset(pw_re[:, 0, :], 1.0), reads=["ss"], writes=["ss"])
            P.op(SE, lambda e: e.memset(pw_im[:, 0, :], 0.0), reads=["ss"], writes=["ss"])
            P.op(SE, lambda e: e.tensor_copy(out=pw_re[:, 1, :], in_=S_(I_AR)), reads=["ss"], writes=["ss"])
            P.op(SE, lambda e: e.tensor_copy(out=pw_im[:, 1, :], in_=S_(I_AI)), reads=["ss"], writes=["ss"])

            def cmul(o_re, o_im, a_re, a_im, b_re, b_im, eng=SE, tok="ss", t0=I_T1):
                t = [ss[:, t0 + i, :] for i in range(4)] if t0 != I_T1 else [S_(I_T1), S_(I_T2), S_(I_T3), S_(I_T4)]

                def o(out, a, b, op):
                    P.op(eng, lambda e: e.tensor_tensor(out=out, in0=a, in1=b, op=op), reads=[tok], writes=[tok])
                o(t[0], a_re, b_re, ALU.mult)
                o(t[1], a_im, b_im, ALU.mult)
                o(t[2], a_re, b_im, ALU.mult)
                o(t[3], a_im, b_re, ALU.mult)
                o(o_re, t[0], t[1], ALU.subtract)
                o(o_im, t[2], t[3], ALU.add)

            for k in range(2, 9):
                cmul(pw_re[:, k, :], pw_im[:, k, :], pw_re[:, k - 1, :], pw_im[:, k - 1, :], S_(I_AR), S_(I_AI))
            for k in range(8):
                cmul(wk_re[:, k, :], wk_im[:, k, :], pw_re[:, k, :], pw_im[:, k, :], S_(I_CFR), S_(I_CFI))
            P.op(SE, lambda e: e.tensor_copy(out=CTre_bf, in_=CT_f[:, 0]), reads=["CT_f"], writes=["CTbf"])
            P.op(SE, lambda e: e.tensor_scalar(out=nCTim_bf, in0=CT_f[:, 1], scalar1=-1.0, scalar2=None, op0=ALU.mult),
                 reads=["CT_f"], writes=["CTbf"])
            CA = ss[:, 24:26, :].rearrange("p a b -> p (a b)")
            a8i = ss[:, I_A8I, :]
            S0 = ss[:, 26:28, :].rearrange("p a b -> p (a b)")
            Fst = ss[:, 28:30, :].rearrange("p a b -> p (a b)")
            Z0 = ss[:, 30:32, :].rearrange("p a b -> p (a b)")
            E = SCAN_ENG
            P.op(E, lambda e: e.tensor_copy(out=ss[:, 24, :], in_=pw_re[:, 8, :]), reads=["ss"], writes=["sc"])
            P.op(E, lambda e: e.tensor_copy(out=ss[:, 25, :], in_=pw_re[:, 8, :]), reads=["ss"], writes=["sc"])
            P.op(E, lambda e: e.tensor_copy(out=a8i, in_=pw_im[:, 8, :]), reads=["ss"], writes=["sc"])
            P.op(E, lambda e: e.memset(Z0, 0.0), writes=["sc"])

            dbg("pw", pw_re[:], [128, 9, 32], ["ss"])
            dbg("wk", wk_re[:], [128, 8, 32], ["ss"])
            vt = [tmpA[:, i * 512:(i + 1) * 512].rearrange("p (a b) -> p a b", a=16) for i in range(4)]
            uTv = uT[:].rearrange("p j (c t) -> p j t c", t=8)
            Sloc_v2 = Sloc[:].rearrange("p c (r j q) -> p q j r c", r=2, q=4)
            def gen_V(half):
                P0 = half * 16
                P.dma("sp", BT_f[:, 0], BTre_d[:, P0:P0 + 16, :], writes=["BT_f"])
                P.dma("sp", BT_f[:, 1], BTim_d[:, P0:P0 + 16, :], writes=["BT_f"])
                for k in range(8):
                    wr = wk_re[:, k, P0:P0 + 16].unsqueeze(2).to_broadcast([128, 16, 32])
                    wi = wk_im[:, k, P0:P0 + 16].unsqueeze(2).to_broadcast([128, 16, 32])
                    VE = "dve" if k % 2 == 0 else "pool"
                    sfx = VE
                    P.op(VE, lambda e, wr=wr: e.tensor_tensor(out=vt[0], in0=BT_f[:, 0], in1=wr, op=ALU.mult),
                         reads=["BT_f", "ss", "tA0"], writes=["tA0"])
                    P.op(VE, lambda e, wi=wi: e.tensor_tensor(out=vt[1], in0=BT_f[:, 1], in1=wi, op=ALU.mult),
                         reads=["BT_f", "ss", "tA0"], writes=["tA1"])
                    P.op(VE, lambda e, k=k: e.tensor_tensor(out=Vre[:, k], in0=vt[0], in1=vt[1], op=ALU.subtract),
                         reads=["tA0", "tA1"], writes=[("V", k)])
                    P.op(VE, lambda e, wr=wr: e.tensor_tensor(out=vt[2], in0=BT_f[:, 1], in1=wr, op=ALU.mult),
                         reads=["BT_f", "ss"], writes=["tA2"])
                    P.op(VE, lambda e, wi=wi: e.tensor_tensor(out=vt[3], in0=BT_f[:, 0], in1=wi, op=ALU.mult),
                         reads=["BT_f", "ss"], writes=["tA3"])
                    P.op(VE, lambda e, k=k: e.tensor_tensor(out=Vim[:, k], in0=vt[2], in1=vt[3], op=ALU.add),
                         reads=["tA2", "tA3"], writes=[("V", k)])
                    yield
                dbg("V", Vre, [128, 8, 16, 32], [("V", k) for k in range(8)])
                if stop == "S2":
                    return

            gv0 = gen_V(0)

            def normA(blk):
                xst = xsts[blk % 2]
                xn_bf = xns[blk % 2]
                XT = ("xst", blk % 2)
                XN = ("xn", blk % 2)
                P.dma("sp", xst, x_d[blk * 128:(blk + 1) * 128, :], reads=["tA0"], writes=[XT])
                P.op("act", lambda e: e.activation(out=xn_bf, in_=xst, func=AF.Square, accum_out=ssq_blk[:, blk:blk + 1]),
                     reads=[XT], writes=[XN, ("ssq", blk)])
                P.op("act", lambda e: e.activation(out=rstd_blk[:, blk:blk + 1], in_=ssq_blk[:, blk:blk + 1],
                                                   func=AF.Sqrt, scale=1.0 / D, bias=eps_c),
                     reads=[("ssq", blk), "eps"], writes=[("rstd", blk)])
                P.op("dve", lambda e: e.reciprocal(out=rstd_blk[:, blk:blk + 1], in_=rstd_blk[:, blk:blk + 1]),
                     reads=[("rstd", blk)], writes=[("rstd", blk)])
                P.op("dve", lambda e: e.scalar_tensor_tensor(out=xn_bf, in0=xst, scalar=rstd_blk[:, blk:blk + 1],
                                                             in1=normw, op0=ALU.mult, op1=ALU.mult),
                     reads=[XT, ("rstd", blk), "normw"], writes=[XN])

            def transA(blk):
                xn_bf = xns[blk % 2]
                XN = ("xn", blk % 2)
                for g4 in range(4):
                    bank = next_banks(1)[0]
                    for i in range(4):
                        kt = g4 * 4 + i
                        P.op("pe", lambda e, bank=bank, i=i, kt=kt: e.matmul(
                            ps[bank][:, i * 128:(i + 1) * 128], lhsT=xn_bf[:, kt * 128:(kt + 1) * 128], rhs=ident_bf[:],
                            start=True, stop=True), reads=[XN, "ident_bf"], writes=[PS[bank]])
                    if g4 % 2 == 0:
                        P.op("act", lambda e, bank=bank, g4=g4: e.activation(
                            out=hT[:, g4 * 4:g4 * 4 + 4, blk * 128:(blk + 1) * 128],
                            in_=ps[bank][:].rearrange("p (a b) -> p a b", a=4), func=AF.Copy),
                            reads=[PS[bank]], writes=["hT"])
                    else:
                        P.op("dve", lambda e, bank=bank, g4=g4: e.tensor_copy(
                            out=hT[:, g4 * 4:g4 * 4 + 4, blk * 128:(blk + 1) * 128],
                            in_=ps[bank][:].rearrange("p (a b) -> p a b", a=4)),
                            reads=[PS[bank]], writes=["hT"])

            normA(0)
            for blk in range(9):
                if blk + 1 < 9:
                    normA(blk + 1)
                transA(blk)
            dbg("hT", hT[:, 0:2, :], [128, 2, NH], ["hT"])

            def gk(half):
                P0 = half * 16
                for jl in range(4):
                    for tpair in range(4):
                        bank = next_banks(1)[0]
                        for t2 in range(2):
                            tau = tpair * 2 + t2
                            for ri in range(2):
                                reg = t2 * 2 + ri
                                Vsrc = Vre if ri == 0 else Vim
                                for Q in range(4):
                                    P.op("pe", lambda e, bank=bank, reg=reg, Vsrc=Vsrc, tau=tau, jl=jl, Q=Q: e.matmul(
                                        ps[bank][32 * Q:32 * Q + 32, reg * 128:(reg + 1) * 128],
                                        lhsT=Vsrc[:, 7 - tau, jl * 4 + Q, :], rhs=ident_bf[:], start=True, stop=True,
                                        tile_position=(0, 32 * Q)),
                                        reads=[("V", 7 - tau), "ident_bf"], writes=[PS[bank]])
                        gout = GinT[:, jl, tpair * 2:tpair * 2 + 2, :, :].rearrange("p a b c -> p (a b c)")
                        if tpair % 2 == 0:
                            P.op("act", lambda e, bank=bank, gout=gout: e.activation(out=gout, in_=ps[bank][:], func=AF.Copy),
                                 reads=[PS[bank]], writes=["GinT"])
                        else:
                            P.op("dve", lambda e, bank=bank, gout=gout: e.tensor_copy(out=gout, in_=ps[bank][:]),
                                 reads=[PS[bank]], writes=["GinT"])
                dbg("GinT", GinT, [128, 4, 8, 2, 128], ["GinT"])
                for jl in range(4):
                    j = half * 4 + jl
                    for lh in range(2):
                        bank = next_banks(1)[0]
                        P.op("pe", lambda e, bank=bank: e.matmul(
                            ps[bank][:].rearrange("p (a b) -> p a b", a=4), lhsT=zeros_bf[:],
                            rhs=zeros_bf[:].unsqueeze(1).to_broadcast([128, 4, 128]), start=True, stop=False),
                            reads=["zeros_bf"], writes=[PS[bank]])
                        for l4 in range(4):
                            lag = lh * 4 + l4
                            for Q in range(4):
                                Pl = jl * 4 + Q
                                Pg = P0 + Pl
                                last = (l4 == 3 and Q == 3)
                                oap = (slice(32 * Q, 32 * Q + 32), slice(l4 * 128 + 32 * Q, l4 * 128 + 32 * Q + 32))
                                P.op("pe", lambda e, bank=bank, oap=oap, lag=lag, Q=Q, Pl=Pl, Pg=Pg: e.matmul(
                                    ps[bank][oap[0], oap[1]], lhsT=Vre[:, lag, Pl, :], rhs=CTre_bf[:, Pg, :],
                                    start=False, stop=False, tile_position=(0, 32 * Q)),
                                    reads=[("V", lag), "CTbf"], writes=[PS[bank]])
                                P.op("pe", lambda e, bank=bank, oap=oap, lag=lag, Q=Q, Pl=Pl, Pg=Pg, last=last: e.matmul(
                                    ps[bank][oap[0], oap[1]], lhsT=Vim[:, lag, Pl, :], rhs=nCTim_bf[:, Pg, :],
                                    start=False, stop=False, tile_position=(0, 32 * Q)),
                                    reads=[("V", lag), "CTbf"], writes=[PS[bank]])
                        P.op("pe", lambda e, bank=bank: e.matmul(
                            ps[bank][:].rearrange("p (a b) -> p a b", a=4), lhsT=zeros_bf[:],
                            rhs=zeros_bf[:].unsqueeze(1).to_broadcast([128, 4, 128]), start=False, stop=True),
                            reads=["zeros_bf"], writes=[PS[bank]])
                        P.op("dve", lambda e, bank=bank, j=j, lh=lh: e.tensor_copy(
                            out=KBD[:, j, lh * 4:lh * 4 + 4, :].rearrange("p a b -> p (a b)"), in_=ps[bank][:]),
                            reads=[PS[bank]], writes=["KBD"])
                        if lh == 0:
                            P.op("dve", lambda e, bank=bank, j=j: e.scalar_tensor_tensor(
                                out=KBD[:, j, 0, :], in0=ident_f[:], scalar=Dfm[:, j:j + 1], in1=ps[bank][:, 0:128],
                                op0=ALU.mult, op1=ALU.add), reads=[PS[bank], "ident_f", "Dfm", "KBD"], writes=["KBD"])
                dbg("KBDh", KBD[:], [128, 8, 8, 128], ["KBD"])

            def sl(half, hook=None):
                P0 = half * 16
                for jp in range(2):
                    for Q in range(4):
                        bank = next_banks(1)[0]
                        for jl2 in range(2):
                            jl = jp * 2 + jl2
                            j = half * 4 + jl
                            for ri in range(2):
                                reg = jl2 * 2 + ri
                                for tau in range(8):
                                    P.op("pe", lambda e, bank=bank, reg=reg, Q=Q, jl=jl, j=j, tau=tau, ri=ri: e.matmul(
                                        ps[bank][:, reg * 128:(reg + 1) * 128],
                                        lhsT=GinT[32 * Q:32 * Q + 32, jl, tau, ri, :],
                                        rhs=uTv[32 * Q:32 * Q + 32, j, tau, :],
                                        start=(tau == 0), stop=(tau == 7), tile_position=(32 * Q, 0)),
                                        reads=["GinT", ("uT", j)], writes=[PS[bank]])
                        j0 = half * 4 + jp * 2
                        sl_out = Sloc_v2[:, Q, j0:j0 + 2, :, :]
                        ps_in = ps[bank][:].rearrange("p (j r c) -> p j r c", j=2, r=2)
                        if Q % 2 == 0:
                            P.op("dve", lambda e, sl_out=sl_out, ps_in=ps_in: e.tensor_copy(out=sl_out, in_=ps_in),
                                 reads=[PS[bank]], writes=["Sloc"])
                        else:
                            P.op("act", lambda e, sl_out=sl_out, ps_in=ps_in: e.activation(out=sl_out, in_=ps_in, func=AF.Copy),
                                 reads=[PS[bank]], writes=["Sloc"])
                        if hook is not None:
                            hook()

            UT = [("uT", j) for j in range(8)]
            for j in range(8):
                banks = next_banks(2)
                proj_fm(CU + j, banks, MAIN)
                for hidx, b in enumerate(banks):
                    if hidx == 0:
                        P.op("act", lambda e, b=b, j=j, hidx=hidx: e.activation(
                            out=uT[:, j, hidx * 512:(hidx + 1) * 512], in_=ps[b][:], func=AF.Copy),
                            reads=[PS[b]], writes=[("uT", j)])
                    else:
                        P.op("dve", lambda e, b=b, j=j, hidx=hidx: e.tensor_copy(
                            out=uT[:, j, hidx * 512:(hidx + 1) * 512], in_=ps[b][:]),
                            reads=[PS[b]], writes=[("uT", j)])
                next(gv0, None)
            for _ in gv0:
                pass
            dbg("uT", uT[:], [128, 8, NT], UT)

            gk(0)
            gv1 = gen_V(1)
            P.alias(["Sloc"], [("xst", 0), ("xst", 1), "normw", ("xn", 0), ("xn", 1)])
            sl(0, hook=lambda: next(gv1, None))
            for _ in gv1:
                pass
            gk(1)
            sl(1)
            dbg("KBD", KBD[:], [128, 8, 8, 128], ["KBD"])
            dbg("Sloc", Sloc[:], [128, 128, 64], ["Sloc"])
            if stop == "S":
                return

            T1 = scan_t[:, 0, :]
            if not TREE_SCAN:
                T2 = scan_t[:, 1, :]
                Uu = scan_t[:, 2, :]
                pp = [scan_t[:, 3, :], scan_t[:, 4, :]]

            def scan_pass(init, store):
                prev = init
                for c in range(128):
                    if store:
                        new = Sloc[:, c, :]
                    else:
                        new = pp[c % 2] if c < 127 else Fst
                    Bc = Sloc[:, c, :]
                    P.op(E, lambda e, prev=prev: e.tensor_tensor(out=T1, in0=CA, in1=prev, op=ALU.mult),
                         reads=["scanS", "sc"], writes=["scanT1"])
                    P.op(E, lambda e, prev=prev: e.tensor_tensor(out=T2[:, 0:32], in0=a8i, in1=prev[:, 32:64], op=ALU.mult),
                         reads=["scanS", "sc"], writes=["scanT2"])
                    P.op(E, lambda e, prev=prev: e.tensor_tensor(out=T2[:, 32:64], in0=a8i, in1=prev[:, 0:32], op=ALU.mult),
                         reads=["scanS", "sc"], writes=["scanT2"])
                    P.op(E, lambda e, Bc=Bc: e.tensor_tensor(out=Uu, in0=T1, in1=Bc, op=ALU.add),
                         reads=["scanT1", "Sloc"], writes=["scanU"])
                    wr_ = ["scanS", "Sloc"] if store else ["scanS"]
                    P.op(E, lambda e, new=new: e.tensor_tensor(out=new[:, 0:32], in0=Uu[:, 0:32], in1=T2[:, 0:32],
                                                               op=ALU.subtract), reads=["scanU", "scanT2"], writes=wr_)
                    P.op(E, lambda e, new=new: e.tensor_tensor(out=new[:, 32:64], in0=Uu[:, 32:64], in1=T2[:, 32:64],
                                                               op=ALU.add), reads=["scanU", "scanT2"], writes=wr_)
                    prev = new
                    yield

            def carry():
                P.dma("pool", cc_in, Fst, reads=["scanS"], writes=["cc_in"])
                if USE_CC:
                    P.custom("pool", lambda e: e.collective_compute(
                        "AllGather", ALU.bypass, replica_groups=[[0, 1, 2, 3], [4, 5, 6, 7]], ins=[cc_in], outs=[cc_out]),
                        N_DMA_SEMS, reads=["cc_in"], writes=["cc_out"])
                else:
                    for r_ in range(4):
                        P.dma("pool", cc_out[r_ * 128:(r_ + 1) * 128, :], cc_in, reads=["cc_in"], writes=["cc_out"])
                P.dma("pool", Gall[:], cc_out.rearrange("(r p) f -> p r f", p=128), reads=["cc_out"], writes=["Gall"])
                AKr = ss[:, 32, :]
                AKi = ss[:, 33, :]
                P.op(E, lambda e: e.tensor_copy(out=AKr, in_=pw_re[:, 8, :]), reads=["ss"], writes=["sc"])
                P.op(E, lambda e: e.tensor_copy(out=AKi, in_=pw_im[:, 8, :]), reads=["ss"], writes=["sc"])
                for _ in range(7):
                    cmul(ss[:, 34, :], ss[:, 35, :], AKr, AKi, AKr, AKi, eng=E, tok="sc", t0=21)
                    P.op(E, lambda e: e.tensor_copy(out=AKr, in_=ss[:, 34, :]), reads=["sc"], writes=["sc"])
                    P.op(E, lambda e: e.tensor_copy(out=AKi, in_=ss[:, 35, :]), reads=["sc"], writes=["sc"])
                Acc = ss[:, 38:40, :].rearrange("p a b -> p (a b)")
                P.op(E, lambda e: e.memset(Acc, 0.0), reads=["sc"], writes=["sc"])
                for d in (3, 2, 1):
                    if d != 3:
                        cmul(ss[:, 34, :], ss[:, 35, :], ss[:, 38, :], ss[:, 39, :], AKr, AKi, eng=E, tok="sc", t0=21)
                        P.op(E, lambda e: e.tensor_copy(out=ss[:, 38, :], in_=ss[:, 34, :]), reads=["sc"], writes=["sc"])
                        P.op(E, lambda e: e.tensor_copy(out=ss[:, 39, :], in_=ss[:, 35, :]), reads=["sc"], writes=["sc"])
                    for r in range(4):
                        P.op(E, lambda e, r=r, d=d: e.tensor_scalar(
                            out=T1, in0=Gall[:, r, :], scalar1=sel_sb[:, (d - 1) * 4 + r:(d - 1) * 4 + r + 1], scalar2=None,
                            op0=ALU.mult), reads=["sc", "Gall", "sel", "scanT1", "scanU"], writes=["scanT1"])
                        P.op(E, lambda e: e.tensor_tensor(out=Acc, in0=Acc, in1=T1, op=ALU.add),
                             reads=["sc", "scanT1"], writes=["sc"])
                P.op(E, lambda e: e.tensor_copy(out=S0, in_=Acc), reads=["sc"], writes=["scanS", "S0"])

            ctab = ss[:, 0:21, :]

            def tree_coefs():
                cr, ci = ss[:, 36, :], ss[:, 37, :]
                P.op(E, lambda e: e.tensor_copy(out=cr, in_=pw_re[:, 8, :]), reads=["ss", "sc"], writes=["sc"])
                P.op(E, lambda e: e.tensor_copy(out=ci, in_=pw_im[:, 8, :]), reads=["ss", "sc"], writes=["sc"])
                for d in range(7):
                    for rr in (0, 1):
                        P.op(E, lambda e, d=d, rr=rr: e.tensor_copy(out=ss[:, 3 * d + rr, :], in_=cr), reads=["sc"], writes=["sc"])
                    P.op(E, lambda e, d=d: e.tensor_copy(out=ss[:, 3 * d + 2, :], in_=ci), reads=["sc"], writes=["sc"])
                    if d < 6:
                        cmul(ss[:, 34, :], ss[:, 35, :], cr, ci, cr, ci, eng=E, tok="sc", t0=21)
                        P.op(E, lambda e: e.tensor_copy(out=cr, in_=ss[:, 34, :]), reads=["sc"], writes=["sc"])
                        P.op(E, lambda e: e.tensor_copy(out=ci, in_=ss[:, 35, :]), reads=["sc"], writes=["sc"])

            def tree_level(d, down):
                span = 2 ** (d + 1)
                nk_tot = 128 // span
                v = Sloc[:].rearrange("p (k s) f -> p k s f", s=span)
                CAd = ss[:, 3 * d:3 * d + 2, :].rearrange("p a b -> p (a b)")
                aid = ss[:, 3 * d + 2, :]
                for k0 in range(0, nk_tot, NKMAX):
                    nk = min(NKMAX, nk_tot - k0)
                    L = v[:, k0:k0 + nk, span // 2 - 1, :]
                    R = v[:, k0:k0 + nk, span - 1, :]
                    tv = [tree_t[:, i, 0:nk * 64].rearrange("p (k f) -> p k f", f=64) for i in range(3)]
                    CAb = CAd.unsqueeze(1).to_broadcast([128, nk, 64])
                    aib = aid.unsqueeze(1).to_broadcast([128, nk, 32])
                    src = R if down else L
                    oth = L if down else R
                    P.op(E, lambda e, src=src, CAb=CAb, tv=tv: e.tensor_tensor(out=tv[0], in0=src, in1=CAb, op=ALU.mult),
                         reads=["Sloc", "sc"], writes=["tr0"])
                    P.op(E, lambda e, src=src, aib=aib, tv=tv: e.tensor_tensor(out=tv[1][:, :, 0:32], in0=src[:, :, 32:64], in1=aib,
                                                                             op=ALU.mult), reads=["Sloc", "sc"], writes=["tr1"])
                    P.op(E, lambda e, src=src, aib=aib, tv=tv: e.tensor_tensor(out=tv[1][:, :, 32:64], in0=src[:, :, 0:32], in1=aib,
                                                                             op=ALU.mult), reads=["Sloc", "sc"], writes=["tr1"])
                    P.op(E, lambda e, oth=oth, tv=tv: e.tensor_tensor(out=tv[2], in0=tv[0], in1=oth, op=ALU.add),
                         reads=["tr0", "Sloc"], writes=["tr2"])
                    if down:
                        P.op(E, lambda e, L=L, R=R: e.tensor_copy(out=L, in_=R), reads=["Sloc", "tr2"], writes=["Sloc"])
                    P.op(E, lambda e, R=R, tv=tv: e.tensor_tensor(out=R[:, :, 0:32], in0=tv[2][:, :, 0:32], in1=tv[1][:, :, 0:32],
                                                                  op=ALU.subtract), reads=["tr2", "tr1"], writes=["Sloc"])
                    P.op(E, lambda e, R=R, tv=tv: e.tensor_tensor(out=R[:, :, 32:64], in0=tv[2][:, :, 32:64], in1=tv[1][:, :, 32:64],
                                                                  op=ALU.add), reads=["tr2", "tr1"], writes=["Sloc", "scanS"])
                    yield

            def tree_scan():
                tree_coefs()
                yield
                for d in range(7):
                    yield from tree_level(d, False)
                P.op(E, lambda e: e.tensor_copy(out=Fst, in_=Sloc[:, 127, :]), reads=["Sloc"], writes=["scanS"])
                carry()
                yield
                P.op(E, lambda e: e.tensor_copy(out=Sloc[:, 127, :], in_=S0), reads=["S0", "scanS"], writes=["Sloc"])
                for d in range(6, -1, -1):
                    yield from tree_level(d, True)

            def scan_all():
                if TREE_SCAN:
                    yield from tree_scan()
                    return
                yield from scan_pass(Z0, False)
                carry()
                yield
                yield from scan_pass(S0, True)

            scan_gen = scan_all()
            scan_done = {"d": False}

            def pump(n):
                import os
                if scan_done["d"] or (os.environ.get("NO_PUMP") and n < 1000):
                    return
                for _ in range(n):
                    try:
                        next(scan_gen)
                    except StopIteration:
                        scan_done["d"] = True
                        return

            P.alias(["sgate", "vaug", "kT"], [("V", k) for k in range(8)] + ["GinT"])
            P.alias(["qT"] + [("pT", a, b) for a in range(2) for b in range(2)], ["BT_f", "CT_f", "CTbf"])
            qn_state = {"i": 0}
            sq2 = tmpB[:].bitcast(BF16)

            def qk_norm_group(items, wcol, wtok, dtok):
                ctx = []
                for (b, dst, ntok) in items:
                    i = qn_state["i"] % 2
                    qn_state["i"] += 1
                    qf = tmpA[:, i * 1024:i * 1024 + 512]
                    rs = tmpA[:, i * 1024 + 512:i * 1024 + 1024]
                    TQ, TR = "tA%d" % (2 * i), "tA%d" % (2 * i + 1)
                    sq = sq_bf[:, 0:ntok] if i == 0 else sq2[:, 0:ntok]
                    TS = "sq" if i == 0 else "tmpB"
                    ctx.append((b, dst, ntok, qf, rs, TQ, TR, sq, TS))
                nbs = []
                for (b, dst, ntok, qf, rs, TQ, TR, sq, TS) in ctx:
                    P.op("act", lambda e, sq=sq, b=b, ntok=ntok: e.activation(out=sq, in_=ps[b][:, 0:ntok], func=AF.Square),
                         reads=[PS[b]], writes=[TS])
                    P.op("dve", lambda e, qf=qf, b=b, ntok=ntok: e.tensor_copy(out=qf[:, 0:ntok], in_=ps[b][:, 0:ntok]),
                         reads=[PS[b]], writes=[TQ])
                for (b, dst, ntok, qf, rs, TQ, TR, sq, TS) in ctx:
                    nb = next_banks(1)[0]
                    nbs.append(nb)
                    P.op("pe", lambda e, nb=nb, sq=sq, ntok=ntok: e.matmul(ps[nb][:, 0:ntok], lhsT=ones_blk[:], rhs=sq,
                                                                          start=True, stop=True),
                         reads=[TS, "ones_blk"], writes=[PS[nb]])
                for (b, dst, ntok, qf, rs, TQ, TR, sq, TS), nb in zip(ctx, nbs):
                    P.op("act", lambda e, rs=rs, nb=nb, ntok=ntok: e.activation(
                        out=rs[:, 0:ntok], in_=ps[nb][:, 0:ntok], func=AF.Sqrt, scale=1.0 / 64, bias=eps_c),
                        reads=[PS[nb], "eps"], writes=[TR])
                for (b, dst, ntok, qf, rs, TQ, TR, sq, TS) in ctx:
                    P.op("dve", lambda e, rs=rs, ntok=ntok: e.reciprocal(out=rs[:, 0:ntok], in_=rs[:, 0:ntok]),
                         reads=[TR], writes=[TR])
                for (b, dst, ntok, qf, rs, TQ, TR, sq, TS) in ctx:
                    P.op("dve", lambda e, dst=dst, qf=qf, rs=rs, ntok=ntok: e.scalar_tensor_tensor(
                        out=dst, in0=qf[:, 0:ntok], scalar=wcol, in1=rs[:, 0:ntok], op0=ALU.mult, op1=ALU.mult),
                        reads=[TQ, TR, wtok], writes=[dtok])

            for c in range(8):
                banks = next_banks(2)
                proj_fm(CQ + c, banks, MAIN)
                qk_norm_group([(b, qT[:, c, hidx * 512:(hidx + 1) * 512], 512) for hidx, b in enumerate(banks)],
                              qw8, "qw8", "qT")
                pump(PUMP_C)
            KSP = [(0, 512), (512, 512), (1024, 128)]
            for c in range(2):
                banks = next_banks(3)
                proj_fm(CK + c, banks, KSP)
                its = [(b, kT[:, c, t0:t0 + n], n) for (t0, n), b in zip(KSP, banks)]
                qk_norm_group(its[0:2], qkw[:, 1:2], "qkw", "kT")
                qk_norm_group(its[2:3], qkw[:, 1:2], "qkw", "kT")
                pump(PUMP_C)
            for c in range(8):
                banks = next_banks(2)
                proj_fm(CG + c, banks, MAIN)
                for hidx, b in enumerate(banks):
                    P.op("act", lambda e, b=b, c=c, hidx=hidx: e.activation(
                        out=sgate[:, c, hidx * 512:(hidx + 1) * 512], in_=ps[b][:], func=AF.Silu),
                        reads=[PS[b]], writes=["sgate"])
                pump(PUMP_C)
            P.op("dve", lambda e: e.memset(vaug.rearrange("p a b c -> p (a b c)"), 1.0), writes=["vaug"])
            for c in range(2):
                wt, wtok = load_w(win_chunk(CV + c))
                for blk in range(9):
                    b = next_banks(1)[0]
                    for kt in range(KT):
                        P.op("pe", lambda e, b=b, kt=kt, blk=blk, wt=wt: e.matmul(
                            ps[b][:, 0:128], lhsT=hT[:, kt, blk * 128:(blk + 1) * 128], rhs=wt[:, kt, :],
                            start=(kt == 0), stop=(kt == KT - 1)), reads=[wtok, "hT"], writes=[PS[b]])
                    P.op("dve", lambda e, b=b, blk=blk, c=c: e.tensor_copy(
                        out=vaug[:, blk, 2 * c, 0:64], in_=ps[b][:, 0:64]), reads=[PS[b]], writes=["vaug"])
                    P.op("act", lambda e, b=b, blk=blk, c=c: e.activation(
                        out=vaug[:, blk, 2 * c + 1, 64:128], in_=ps[b][:, 64:128], func=AF.Copy),
                        reads=[PS[b]], writes=["vaug"])
                pump(PUMP_C)
            dbg("qT", qT, [128, 8, NT], ["qT"])
            dbg("kT", kT, [128, 2, NH], ["kT"])
            dbg("vaug", vaug, [128, 9, 4, 128], ["vaug"])
            if stop == "C":
                return

            rden = tmpA[:, 1024:1536]
            onum = tmpA[:, 1536:2048]
            AG = [("ag", n) for n in range(8)]
            def att_qk(n, g):
                hf = g % 2
                rows = slice(64 * hf, 64 * hf + 64)
                drows = slice(64 * (1 - hf), 64 * (1 - hf) + 64)
                cbase = (g // 2) * 4
                kc = g // 2
                bS = next_banks(2)
                pb = (n * 4 + g) % 2
                qrhs = qT[rows, cbase:cbase + 4, n * 128:(n + 1) * 128]
                for kk, bb in enumerate(bS):
                    kblk = n + kk
                    mi = (2 if n == 0 else 0) if kk == 0 else 1
                    P.op("pe", lambda e, bb=bb, kblk=kblk, qrhs=qrhs, rows=rows, kc=kc: e.matmul(
                        ps[bb][:].rearrange("p (a b) -> p a b", a=4), lhsT=kT[rows, kc, kblk * 128:(kblk + 1) * 128],
                        rhs=qrhs, start=True, stop=False), reads=["qT", "kT"], writes=[PS[bb]])
                    P.op("pe", lambda e, bb=bb, mi=mi: e.matmul(
                        ps[bb][:].rearrange("p (a b) -> p a b", a=4), lhsT=ident_bf[:],
                        rhs=masks[:, mi, :].unsqueeze(1).to_broadcast([128, 4, 128]), start=False, stop=True),
                        reads=["masks", "ident_bf"], writes=[PS[bb]])
                    P.op("act", lambda e, bb=bb, kk=kk, pb=pb: e.activation(out=pT[:, pb, kk, :], in_=ps[bb][:], func=AF.Exp),
                         reads=[PS[bb]], writes=[("pT", pb, kk)])
                return (hf, rows, drows, cbase, pb)

            def att_rest(n, g, ctx):
                hf, rows, drows, cbase, pb = ctx
                bo = next_banks(1)[0]
                for kk in range(2):
                    kblk = n + kk
                    P.op("pe", lambda e, kk=kk, kblk=kblk, bo=bo, g=g, pb=pb: e.matmul(
                        ps[bo][:], lhsT=vaug[:, kblk, g, :], rhs=pT[:, pb, kk, :], start=(kk == 0), stop=False),
                        reads=[("pT", pb, kk), "vaug"], writes=[PS[bo]])
                P.op("pe", lambda e, bo=bo, g=g, hf=hf: e.matmul(
                    ps[bo][:].rearrange("p (a b) -> p a b", a=4), lhsT=dsel[0:1, hf, :],
                    rhs=eskb[0:1, 4 * g:4 * g + 4].unsqueeze(2).to_broadcast([1, 4, 128]), start=False, stop=True),
                    reads=["eskb", "dsel"], writes=[PS[bo]])
                P.op("dve", lambda e, bo=bo, drows=drows: e.reciprocal(out=rden[drows, :], in_=ps[bo][drows, :]),
                     reads=[PS[bo]], writes=["tA2"])
                P.op("dve", lambda e, bo=bo, rows=rows, drows=drows: e.tensor_tensor(
                    out=onum[rows, :], in0=ps[bo][rows, :], in1=rden[drows, :], op=ALU.mult),
                    reads=[PS[bo], "tA2"], writes=["tA3"])
                P.op("dve", lambda e, rows=rows, cbase=cbase, n=n: e.tensor_tensor(
                    out=qT[rows, cbase:cbase + 4, n * 128:(n + 1) * 128],
                    in0=onum[rows, :].rearrange("p (a b) -> p a b", a=4),
                    in1=sgate[rows, cbase:cbase + 4, n * 128:(n + 1) * 128], op=ALU.mult),
                    reads=["tA3", "sgate"], writes=[("ag", n)])

            its = [(n, g) for n in range(8) for g in range(4)]
            ctxs = {0: att_qk(*its[0])}
            for i, (n, g) in enumerate(its):
                if i + 1 < len(its):
                    ctxs[i + 1] = att_qk(*its[i + 1])
                att_rest(n, g, ctxs.pop(i))
                pump(PUMP_T)
            agT = qT
            dbg("ag", qT, [128, 8, NT], AG)
            if stop == "T":
                return

            w_ap_v = w_ap_d.rearrange("(kt p) c -> p kt c", p=128)
            w_glu_v = w_glu_d.rearrange("(kt p) c -> p kt c", p=128)
            w_sp_v = w_sp_d.rearrange("(kt p) c -> p kt c", p=128)
            P.alias(["Pa"], ["sgate", "vaug", "kT"])

            def proj8(wsrc, act_T, act_toks, banks):
                wt, wtok = load_w(wsrc)
                for hidx, b in enumerate(banks):
                    for kt in range(8):
                        P.op("pe", lambda e, b=b, kt=kt, hidx=hidx: e.matmul(
                            ps[b][:], lhsT=wt[:, kt, :], rhs=act_T[:, kt, hidx * 512:(hidx + 1) * 512],
                            start=(kt == 0), stop=(kt == 7)), reads=[wtok] + act_toks, writes=[PS[b]])

            for c in range(16):
                by = next_banks(2)
                proj8(w_ap_v[:, :, c * 128:(c + 1) * 128], agT, AG, by)
                bg = next_banks(2)
                proj_fm(CGA + c, bg, MAIN)
                for hidx in range(2):
                    sg = tmpA[:, hidx * 512:(hidx + 1) * 512]
                    P.op("act", lambda e, hidx=hidx, sg=sg, bg=bg: e.activation(out=sg, in_=ps[bg[hidx]][:], func=AF.Sigmoid),
                         reads=[PS[bg[hidx]], "tA0", "tA1"], writes=["tA%d" % hidx])
                    P.op("dve", lambda e, hidx=hidx, sg=sg, c=c, by=by: e.tensor_tensor(
                        out=Pa[:, c, hidx * 512:(hidx + 1) * 512], in0=ps[by[hidx]][:], in1=sg, op=ALU.mult),
                        reads=[PS[by[hidx]], "tA%d" % hidx], writes=["Pa"])
                pump(6)
            pump(100000)
            dbg("Pa", Pa, [128, 16, NT], ["Pa"])
            if stop == "D":
                return

            P.alias(["Sin_bf", ("Hout", 0), ("Hout", 1)], ["qT"] + AG + [("pT", a, b) for a in range(2) for b in range(2)])
            if TREE_SCAN:
                P.op("dve", lambda e: e.tensor_copy(out=Sin_bf, in_=Sloc[:].rearrange("p c (r q) -> p r q c", r=2)),
                     reads=["Sloc", "scanS"], writes=["Sin_bf"])
            else:
                P.op("dve", lambda e: e.tensor_copy(out=Sin_bf[:, :, :, 0], in_=S0.rearrange("p (r q) -> p r q", r=2)),
                     reads=["S0"], writes=["Sin_bf"])
                P.op("dve", lambda e: e.tensor_copy(out=Sin_bf[:, :, :, 1:128],
                                                    in_=Sloc[:, 0:127, :].rearrange("p c (r q) -> p r q c", r=2)),
                     reads=["Sloc", "scanS"], writes=["Sin_bf"])
            dbg("Sin", Sin_bf, [128, 2, 32, 128], ["Sin_bf"])
            P.alias(["sz", "yssm"], ["Sloc", "scanS"])

            for j in range(8):
                banks = next_banks(2)
                proj_fm(CZ + j, banks, MAIN)
                for hidx, b in enumerate(banks):
                    P.op("act", lambda e, b=b, j=j, hidx=hidx: e.activation(
                        out=sz[:, j, hidx * 512:(hidx + 1) * 512], in_=ps[b][:], func=AF.Silu),
                        reads=[PS[b]], writes=["sz"])

            GC = math.sqrt(2.0 / math.pi)
            ctj = tmpB[:, 0:256].rearrange("p (r a b) -> p r a b", r=2, a=4)
            h1 = tmpB[:, 256:384].rearrange("p (a b) -> p a b", a=4)
            h2 = tmpB[:, 384:512].rearrange("p (a b) -> p a b", a=4)
            def hout_gen(j):
                hb = j % 2
                Hj = Hout[:, hb]
                P.dma("sp", ctj[:, 0], CTre_d[:, 4 * j:4 * j + 4, :], writes=["tmpB"])
                P.dma("sp", ctj[:, 1], CTim_d[:, 4 * j:4 * j + 4, :], writes=["tmpB"])
                ar_ = pw_re[:, 1:9, 4 * j:4 * j + 4].rearrange("p t q -> p q t").unsqueeze(3).to_broadcast([128, 4, 8, 32])
                ai_ = pw_im[:, 1:9, 4 * j:4 * j + 4].rearrange("p t q -> p q t").unsqueeze(3).to_broadcast([128, 4, 8, 32])
                cr_ = ctj[:, 0].unsqueeze(2).to_broadcast([128, 4, 8, 32])
                ci_ = ctj[:, 1].unsqueeze(2).to_broadcast([128, 4, 8, 32])
                h1 = wstf[1][:, 0:1024].rearrange("p (q t h) -> p q t h", q=4, t=8)
                h2 = wstf[1][:, 1024:2048].rearrange("p (q t h) -> p q t h", q=4, t=8)
                TK1, TK2 = ("wst", 1), ("wst", 1)
                HE = "dve"
                P.op(HE, lambda e: e.tensor_tensor(out=h1, in0=cr_, in1=ar_, op=ALU.mult), reads=["tmpB", "ss"], writes=[TK1])
                P.op(HE, lambda e: e.tensor_tensor(out=h2, in0=ci_, in1=ai_, op=ALU.mult), reads=["tmpB", "ss"], writes=[TK2])
                P.op(HE, lambda e: e.tensor_tensor(out=Hj[:, :, :, 0, :], in0=h1, in1=h2, op=ALU.subtract),
                     reads=[TK1, TK2], writes=[("Hout", hb)])
                P.op(HE, lambda e: e.tensor_tensor(out=h1, in0=cr_, in1=ai_, op=ALU.mult), reads=["tmpB", "ss"], writes=[TK1])
                P.op(HE, lambda e: e.tensor_tensor(out=h2, in0=ci_, in1=ar_, op=ALU.mult), reads=["tmpB", "ss"], writes=[TK2])
                P.op(HE, lambda e: e.tensor_tensor(out=h1, in0=h1, in1=h2, op=ALU.add), reads=[TK1, TK2], writes=[TK1])
                P.op("act", lambda e: e.activation(out=Hj[:, :, :, 1, :], in_=h1, func=AF.Copy, scale=-1.0),
                     reads=[TK1], writes=[("Hout", hb)])

            def ssm_mm(j):
                hb = j % 2
                Hj = Hout[:, hb]
                yb = next_banks(2)
                for tau in range(8):
                    b = yb[tau // 4]
                    reg = tau % 4
                    for tp in range(tau + 1):
                        P.op("pe", lambda e, b=b, reg=reg, tau=tau, tp=tp, j=j: e.matmul(
                            ps[b][:, reg * 128:(reg + 1) * 128], lhsT=KBD[:, j, tau - tp, :], rhs=uTv[:, j, tp, :],
                            start=(tp == 0), stop=False), reads=["KBD", ("uT", j)], writes=[PS[b]])
                    for Q in range(4):
                        for ri in range(2):
                            P.op("pe", lambda e, b=b, reg=reg, tau=tau, Q=Q, ri=ri, Hj=Hj, j=j: e.matmul(
                                ps[b][32 * Q:32 * Q + 32, reg * 128:(reg + 1) * 128], lhsT=Hj[:, Q, tau, ri, :],
                                rhs=Sin_bf[:, ri, 4 * j + Q, :], start=False, stop=False, tile_position=(0, 32 * Q)),
                                reads=[("Hout", hb), "Sin_bf"], writes=[PS[b]])
                    P.op("pe", lambda e, b=b, reg=reg: e.matmul(
                        ps[b][:, reg * 128:(reg + 1) * 128], lhsT=zeros_bf[:], rhs=zeros_bf[:], start=False, stop=True),
                        reads=["zeros_bf"], writes=[PS[b]])
                return yb

            def ssm_gelu(j, yb):
                xs, ts, TXs, TTs, pins = [], [], [], [], []
                for half2, b in enumerate(yb):
                    xs.append(tmpA[:, half2 * 1024:half2 * 1024 + 512])
                    ts.append(tmpA[:, half2 * 1024 + 512:half2 * 1024 + 1024])
                    TXs.append("tA%d" % (2 * half2))
                    TTs.append("tA%d" % (2 * half2 + 1))
                    pins.append(ps[b][:].rearrange("p (t c) -> p t c", t=4))
                for h, b in enumerate(yb):
                    P.op("act", lambda e, h=h: e.activation(
                        out=xs[h].rearrange("p (c t) -> p t c", t=4), in_=pins[h], func=AF.Copy), reads=[PS[b]], writes=[TXs[h]])
                    P.op("act", lambda e, h=h: e.activation(
                        out=ts[h].rearrange("p (c t) -> p t c", t=4), in_=pins[h], func=AF.Square, scale=math.sqrt(0.044715)),
                        reads=[PS[b]], writes=[TTs[h]])
                for h in range(2):
                    P.op("dve", lambda e, h=h: e.scalar_tensor_tensor(
                        out=ts[h], in0=ts[h], scalar=1.0, in1=xs[h], op0=ALU.add, op1=ALU.mult),
                        reads=[TTs[h], TXs[h]], writes=[TTs[h]])
                for h in range(2):
                    P.op("act", lambda e, h=h: e.activation(out=ts[h], in_=ts[h], func=AF.Sigmoid, scale=2.0 * GC),
                         reads=[TTs[h]], writes=[TTs[h]])
                for h in range(2):
                    P.op("dve", lambda e, j=j, h=h: e.tensor_tensor(
                        out=yssm[:, j, :].rearrange("p (c h t) -> p c h t", h=2, t=4)[:, :, h, :],
                        in0=xs[h].rearrange("p (c t) -> p c t", t=4), in1=ts[h].rearrange("p (c t) -> p c t", t=4), op=ALU.mult),
                        reads=[TXs[h], TTs[h]], writes=["yssm"])

            hout_gen(0)
            ybs = {}
            for j in range(8):
                ybs[j] = ssm_mm(j)
                if j + 1 < 8:
                    hout_gen(j + 1)
                if j >= 1:
                    ssm_gelu(j - 1, ybs.pop(j - 1))
            ssm_gelu(7, ybs.pop(7))
            dbg("yssm", yssm, [128, 8, NT], ["yssm"])
            if stop == "Y":
                return

            P.alias(["vglu"], ["KBD"])
            for c in range(8):
                ba = next_banks(2)
                proj8(w_glu_v[:, :, c * 128:(c + 1) * 128], yssm, ["yssm"], ba)
                bb2 = next_banks(2)
                proj8(w_glu_v[:, :, (8 + c) * 128:(9 + c) * 128], yssm, ["yssm"], bb2)
                for hidx in range(2):
                    sgl = tmpA[:, hidx * 512:(hidx + 1) * 512]
                    tg = tmpA[:, 1024 + hidx * 512:1024 + (hidx + 1) * 512]
                    P.op("act", lambda e, hidx=hidx, sgl=sgl, c=c, bb2=bb2: e.activation(
                        out=sgl, in_=ps[bb2[hidx]][:], func=AF.Sigmoid, bias=bglu[:, 8 + c:9 + c]),
                        reads=[PS[bb2[hidx]], "bglu", "tA0", "tA1"], writes=["tA%d" % hidx])
                    P.op("dve", lambda e, hidx=hidx, sgl=sgl, tg=tg, c=c, ba=ba: e.scalar_tensor_tensor(
                        out=tg, in0=ps[ba[hidx]][:], scalar=bglu[:, c:c + 1], in1=sgl, op0=ALU.add, op1=ALU.mult),
                        reads=[PS[ba[hidx]], "tA%d" % hidx, "bglu", "tA2", "tA3"], writes=["tA%d" % (2 + hidx)])
                    P.op("dve", lambda e, hidx=hidx, tg=tg, c=c: e.tensor_tensor(
                        out=vglu[:, c, hidx * 512:(hidx + 1) * 512], in0=tg, in1=sz[:, c, hidx * 512:(hidx + 1) * 512],
                        op=ALU.mult), reads=["tA%d" % (2 + hidx), "sz"], writes=["vglu"])
            dbg("vglu", vglu, [128, 8, NT], ["vglu"])

            for c in range(16):
                by = next_banks(2)
                proj8(w_sp_v[:, :, c * 128:(c + 1) * 128], vglu, ["vglu"], by)
                bg = next_banks(2)
                proj_fm(CGS + c, bg, MAIN)
                for hidx in range(2):
                    sg = tmpA[:, hidx * 512:(hidx + 1) * 512]
                    tg = tmpA[:, 1024 + hidx * 512:1024 + (hidx + 1) * 512]
                    P.op("act", lambda e, hidx=hidx, sg=sg, bg=bg: e.activation(out=sg, in_=ps[bg[hidx]][:], func=AF.Sigmoid),
                         reads=[PS[bg[hidx]]], writes=["tA%d" % hidx])
                    P.op("dve", lambda e, hidx=hidx, sg=sg, tg=tg, by=by: e.tensor_tensor(
                        out=tg, in0=ps[by[hidx]][:], in1=sg, op=ALU.mult),
                        reads=[PS[by[hidx]], "tA%d" % hidx], writes=["tA%d" % (2 + hidx)])
                    P.op("dve", lambda e, hidx=hidx, tg=tg, c=c: e.tensor_tensor(
                        out=Pa[:, c, hidx * 512:(hidx + 1) * 512], in0=tg, in1=Pa[:, c, hidx * 512:(hidx + 1) * 512],
                        op=ALU.add), reads=["tA%d" % (2 + hidx), "Pa"], writes=["Pa"])
            dbg("merged", Pa, [128, 16, NT], ["Pa"])

            w_out_v = w_out_d.rearrange("(kt p) c -> p kt c", p=128)
            P.alias([("woutbf", 0), ("woutbf", 1)], ["hT"])
            P.alias([("xres", 0), ("xres", 1), ("ores", 0), ("ores", 1)], UT)
            def load_wout(q):
                wb = q % 2
                for k4 in range(4):
                    sl = state["w"] % 2
                    state["w"] += 1
                    P.dma("sp", wst4[sl], w_out_v[:, k4 * 4:k4 * 4 + 4, q * 512:(q + 1) * 512], writes=[("wst", sl)])
                    P.op("act", lambda e, sl=sl, wb=wb, k4=k4: e.activation(
                        out=woutbf[:, wb, k4 * 4:k4 * 4 + 4, :], in_=wst4[sl], func=AF.Copy),
                        reads=[("wst", sl)], writes=[("woutbf", wb)])

            load_wout(0)
            for q in range(4):
                wb = q % 2
                if q + 1 < 4:
                    load_wout(q + 1)
                for blk in range(8):
                    b = next_banks(1)[0]
                    for kt in range(KT):
                        P.op("pe", lambda e, b=b, kt=kt, blk=blk, wb=wb: e.matmul(
                            ps[b][:], lhsT=Pa[:, kt, blk * 128:(blk + 1) * 128], rhs=woutbf[:, wb, kt, :],
                            start=(kt == 0), stop=(kt == KT - 1)), reads=["Pa", ("woutbf", wb)], writes=[PS[b]])
                    xb = (q * 8 + blk) % 2
                    P.dma("sp", xres[:, xb, :], x_d[128 + blk * 128:128 + (blk + 1) * 128, q * 512:(q + 1) * 512],
                          writes=[("xres", xb)])
                    P.op("dve", lambda e, b=b, xb=xb: e.tensor_tensor(out=ores[:, xb, :], in0=ps[b][:], in1=xres[:, xb, :],
                                                                      op=ALU.add),
                         reads=[PS[b], ("xres", xb)], writes=[("ores", xb)])
                    ev = P.dma("pool", out_d[blk * 128:(blk + 1) * 128, q * 512:(q + 1) * 512], ores[:, xb, :],
                               reads=[("ores", xb)], writes=[("out", q, blk)])
                    out_evs.append(ev)
        body()
        P.wait_all("sp", out_evs + list(dbg_out.values()))
        P.emit(st)
    return nc, wseq_rec


_CACHE = {}


def kernel(**inputs):
    inp = {k: np.asarray(v) for k, v in inputs.items()}
    sh = _prep_shared(inp)
    in_maps = []
    for c in range(NCORES):
        m = dict(sh)
        m.update(_prep_core(inp, c))
        in_maps.append(m)
    if "nc" not in _CACHE:
        _CACHE["nc"] = build()
    nc = _CACHE["nc"]
    res = run_bass_kernel_spmd(nc, in_maps, core_ids=list(range(NCORES)))
    out = np.zeros((2, 4096, D), np.float32)
    for c in range(NCORES):
        b, j = c // 4, c % 4
        out[b, j * NT:(j + 1) * NT] = res.results[c]["out"]
    return out
```
